# Optimizing a Trainium2 kernel written in Bass

```python
import jax, jax.numpy as jnp
from jax import lax
import numpy as np

D_MODEL = 2048
BATCH = 2
SEQ = 4096
DEPTH = 4
DEC_BATCH = 8
DEC_SEQ = 4
PAST_LEN = 16384
PAGE_SIZE = 128

F32 = jnp.float32
L_EVEN = (DEPTH + 1) // 2
L_ODD = DEPTH // 2
ALPHA = (2 * DEPTH) ** 0.25
BETA = (8 * DEPTH) ** -0.25
LN_EPS = 1e-5
A_HEADS = 12
A_HEAD_DIM = 64
A_WIDTH = A_HEADS * A_HEAD_DIM
A_LORA_W = 64
A_LORA_A = 64
A_SHIFT_W = 3 * A_WIDTH + A_LORA_W + A_LORA_A
A_GN_EPS = 64e-5
B_CONFIGS = ((128, 1), (512, 4), (2048, 16))
B_GROUPS = 3
B_HEADS_PER_GROUP = 4
B_HEAD_DIM = 64
B_WIDTH = B_GROUPS * B_HEADS_PER_GROUP * B_HEAD_DIM
B_OUT = B_HEADS_PER_GROUP * B_HEAD_DIM
C_HEADS = 6
C_HEAD_DIM = 256
C_WIDTH = C_HEADS * C_HEAD_DIM
C_CHUNK = 128
C_ROT_BASE = 10000.0
C_NORM_EPS = 1e-6
M_TOKENS = 256
M_HEADS = 4
M_HEAD_DIM = 128
M_WIDTH = M_HEADS * M_HEAD_DIM
EVEN_IN = A_SHIFT_W + A_WIDTH + 3 * B_WIDTH + B_OUT + 2 * M_WIDTH
EVEN_OUT = A_WIDTH + B_OUT + M_WIDTH
ODD_IN = 4 * C_WIDTH + 2 * M_WIDTH
ODD_OUT = C_WIDTH + M_WIDTH

kernel_name = 'rwkv7_dilated_retention_hybrid_step'


def _split(h, sizes):
    return jnp.split(h, np.cumsum(sizes)[:-1].tolist(), axis=-1)


def layer_norm(x, g, b):
    xf = x.astype(F32)
    mu = jnp.mean(xf, -1, keepdims=True)
    var = jnp.mean(jnp.square(xf - mu), -1, keepdims=True)
    return (xf - mu) * lax.rsqrt(var + LN_EPS) * g + b


def _softmax_with_lse(s, mask):
    s = jnp.where(mask, s, -jnp.inf)
    m = jnp.max(s, -1, keepdims=True)
    e = jnp.exp(s - m)
    l = jnp.sum(e, -1, keepdims=True)
    return e / l, (m + jnp.log(l))[..., 0]


def rwkv7_mix(hs, s0, w0, w_up, a0, a_up, k_k, k_a, r_k, lnx_g, lnx_b):
    bn, t, _ = hs.shape
    r, k, v, hw, ha = _split(hs, (A_WIDTH, A_WIDTH, A_WIDTH, A_LORA_W, A_LORA_A))
    w_log = -jax.nn.softplus(-(w0 + jnp.tanh(hw) @ w_up).astype(F32)) - 0.5
    decay = jnp.exp(-jnp.exp(w_log))
    a = jax.nn.sigmoid(a0 + ha @ a_up)
    heads = lambda z: z.reshape(bn, t, A_HEADS, A_HEAD_DIM).astype(F32)
    kk = heads(k * k_k)
    kk = kk * lax.rsqrt(jnp.maximum(jnp.sum(kk * kk, -1, keepdims=True), 1e-24))
    k = k * (1.0 + (a - 1.0) * k_a)
    r_h, k_h, v_h, a_h, d_h = heads(r), heads(k), heads(v), heads(a), heads(decay)

    def step(S, inp):
        r_t, d_t, k_t, v_t, kk_t, a_t = inp
        sa = jnp.einsum('bhij,bhj->bhi', S, -kk_t)
        S = S * d_t[:, :, None, :] + sa[..., None] * (kk_t * a_t)[:, :, None, :] + v_t[..., None] * k_t[:, :, None, :]
        return S, jnp.einsum('bhij,bhj->bhi', S, r_t)

    xs = tuple(jnp.swapaxes(z, 0, 1) for z in (r_h, d_h, k_h, v_h, kk, a_h))
    s_new, y = lax.scan(step, s0.astype(F32), xs)
    y = jnp.swapaxes(y, 0, 1)
    mu = jnp.mean(y, -1, keepdims=True)
    var = jnp.mean(jnp.square(y - mu), -1, keepdims=True)
    y = ((y - mu) * lax.rsqrt(var + A_GN_EPS)).reshape(bn, t, A_WIDTH) * lnx_g + lnx_b
    bonus = jnp.sum(r_h * k_h * r_k, -1, keepdims=True) * v_h
    return y + bonus.reshape(bn, t, A_WIDTH), s_new


def dilated_window_prompt(q, k, v, window, dilation):
    bn, s_len, nh, dh = q.shape
    blk = window // dilation
    unit = blk * dilation
    s_pad = -(-s_len // unit) * unit
    n_blk = s_pad // unit

    def fold(z):
        z = jnp.pad(z, ((0, 0), (0, s_pad - s_len), (0, 0), (0, 0)))
        return z.reshape(bn, n_blk, blk, dilation, nh, dh)

    qb, kb, vb = fold(q), fold(k), fold(v)
    prev = lambda z: jnp.concatenate([jnp.zeros_like(z[:, :1]), z[:, :-1]], axis=1)
    kw = jnp.concatenate([prev(kb), kb], axis=2)
    vw = jnp.concatenate([prev(vb), vb], axis=2)
    s = jnp.einsum('bnirhd,bnjrhd->bnrhij', qb, kw).astype(F32) * dh ** -0.5
    i = jnp.arange(blk)[:, None]
    j = jnp.arange(2 * blk)[None, :]
    band = (j >= i) & (j <= i + blk)
    has_prev = (jnp.arange(n_blk) > 0)[:, None, None] | (j >= blk)[None]
    mask = (band[None] & has_prev)[None, :, None, None]
    p, lse = _softmax_with_lse(s, mask)
    o = jnp.einsum('bnrhij,bnjrhd->bnirhd', p.astype(vw.dtype), vw)
    o = o.reshape(bn, s_pad, nh, dh)[:, :s_len]
    lse = jnp.transpose(lse, (0, 1, 4, 2, 3)).reshape(bn, s_pad, nh)[:, :s_len]
    return o, lse


def dilated_window_decode(q, k, v, buf, window, dilation):
    L = buf.shape[1]
    t = q.shape[1]
    dh = q.shape[-1]
    keys = jnp.concatenate([buf[:, :, 0], k.astype(buf.dtype)], axis=1)
    vals = jnp.concatenate([buf[:, :, 1], v.astype(buf.dtype)], axis=1)
    n_keys = window // dilation + 1
    idx = L + jnp.arange(t)[:, None] - dilation * jnp.arange(n_keys)[None, :]
    valid = idx >= 0
    idx = jnp.maximum(idx, 0)
    kg, vg = keys[:, idx], vals[:, idx]
    s = jnp.einsum('nthd,ntjhd->nthj', q, kg).astype(F32) * dh ** -0.5
    p, lse = _softmax_with_lse(s, valid[None, :, None, :])
    o = jnp.einsum('nthj,ntjhd->nthd', p.astype(vg.dtype), vg)
    return o, lse


def combine_by_denominator(outs, lses):
    wts = jax.nn.softmax(jnp.stack(lses, 0), axis=0)
    return jnp.sum(wts[..., None] * jnp.stack(outs, 0).astype(F32), axis=0)


def memory_attention(q, mkv):
    s = jnp.einsum('bthd,bmhd->bhtm', q, mkv[:, :, 0]).astype(F32) * M_HEAD_DIM ** -0.5
    p = jax.nn.softmax(s, axis=-1)
    return jnp.einsum('bhtm,bmhd->bthd', p.astype(mkv.dtype), mkv[:, :, 1])


def retention_rotate(z, pos):
    angle = 1.0 / (C_ROT_BASE ** jnp.linspace(0.0, 1.0, C_HEAD_DIM // 2, dtype=F32))
    ph = pos[:, None] * jnp.repeat(angle, 2)[None]
    sin, cos = jnp.sin(ph)[None, :, None], jnp.cos(ph)[None, :, None]
    rot = jnp.stack([-z[..., 1::2], z[..., 0::2]], axis=-1).reshape(z.shape)
    return z.astype(F32) * cos + rot.astype(F32) * sin


def retention_chunked(q, k, v, r0):
    bn, t, nh, dk = q.shape
    chunk = C_CHUNK if t % C_CHUNK == 0 else t
    n = t // chunk
    lg = jnp.log(1.0 - 2.0 ** (-5.0 - jnp.arange(nh, dtype=F32)))
    idx = jnp.arange(chunk, dtype=F32)
    diff = idx[:, None] - idx[None, :]
    dmat = jnp.where(diff >= 0, jnp.exp(lg[:, None, None] * jnp.maximum(diff, 0.0)), 0.0)
    xi = jnp.exp(lg[None, :] * (idx[:, None] + 1.0))
    zeta = jnp.exp(lg[None, :] * (chunk - 1.0 - idx[:, None]))
    g_chunk = jnp.exp(lg * chunk)
    to_chunks = lambda z: jnp.moveaxis(z.astype(F32).reshape(bn, n, chunk, nh, z.shape[-1]), 1, 0)

    def step(R, inp):
        qc, kc, vc = inp
        sc = jnp.einsum('bihd,bjhd->bhij', qc, kc) * dmat[None]
        o = jnp.einsum('bhij,bjhe->bihe', sc, vc) + jnp.einsum('bihd,bhde->bihe', qc, R) * xi[None, :, :, None]
        R = R * g_chunk[None, :, None, None] + jnp.einsum('bjhd,bjhe->bhde', kc * zeta[None, :, :, None], vc)
        return R, o

    r_new, o = lax.scan(step, r0.astype(F32), (to_chunks(q), to_chunks(k), to_chunks(v)))
    return jnp.moveaxis(o, 0, 1).reshape(bn, t, nh, v.shape[-1]), r_new


def even_layer(x, mkv, dwa_bufs, s0, shift0, w_in, w_out, ln_g, ln_b, mu, w0, w_up, a0, a_up, k_k, k_a, r_k, lnx_g, lnx_b):
    bn, t, _ = x.shape
    h = x @ w_in
    h_sh, gate_a, q_b, k_b, v_b, gate_b, q_m, gate_m = _split(
        h, (A_SHIFT_W, A_WIDTH, B_WIDTH, B_WIDTH, B_WIDTH, B_OUT, M_WIDTH, M_WIDTH))
    prev = jnp.concatenate([shift0[:, None].astype(h_sh.dtype), h_sh[:, :-1]], axis=1)
    y_a, s_new = rwkv7_mix(h_sh + (prev - h_sh) * mu, s0, w0, w_up, a0, a_up, k_k, k_a, r_k, lnx_g, lnx_b)
    grp = lambda z: z.reshape(bn, t, B_GROUPS, B_HEADS_PER_GROUP, B_HEAD_DIM)
    q_b, k_b, v_b = grp(q_b), grp(k_b), grp(v_b)
    outs, lses, rows = [], [], []
    for g, (win, dil) in enumerate(B_CONFIGS):
        qg, kg, vg = q_b[:, :, g], k_b[:, :, g], v_b[:, :, g]
        if dwa_bufs is None:
            o, lse = dilated_window_prompt(qg, kg, vg, win, dil)
            keep = min(win, t)
            rows.append(jnp.stack([kg[:, t - keep:], vg[:, t - keep:]], axis=2))
        else:
            o, lse = dilated_window_decode(qg, kg, vg, dwa_bufs[g], win, dil)
            rows.append(jnp.stack([kg, vg], axis=2))
        outs.append(o)
        lses.append(lse)
    y_b = combine_by_denominator(outs, lses).reshape(bn, t, B_OUT)
    y_m = memory_attention(q_m.reshape(bn, t, M_HEADS, M_HEAD_DIM), mkv).reshape(bn, t, M_WIDTH)
    u = jnp.concatenate([y_a * jax.nn.silu(gate_a), y_b * jax.nn.silu(gate_b), y_m * jax.nn.silu(gate_m)], axis=-1).astype(x.dtype)
    x = layer_norm(ALPHA * x + u @ w_out, ln_g, ln_b)
    return x, s_new, h_sh[:, -1], rows


def odd_layer(x, mkv, r0, pos0, w_in, w_out, ln_g, ln_b):
    bn, t, _ = x.shape
    h = x @ w_in
    q, k, v, g, q_m, g_m = _split(h, (C_WIDTH, C_WIDTH, C_WIDTH, C_WIDTH, M_WIDTH, M_WIDTH))
    heads = lambda z: z.reshape(bn, t, C_HEADS, C_HEAD_DIM)
    pos = jnp.arange(t, dtype=F32) + pos0
    q = retention_rotate(heads(q), pos)
    k = retention_rotate(heads(k), pos) * C_HEAD_DIM ** -0.5
    y, r_new = retention_chunked(q, k, heads(v), r0)
    y = y * lax.rsqrt(jnp.mean(y * y, -1, keepdims=True) + C_NORM_EPS)
    y_m = memory_attention(q_m.reshape(bn, t, M_HEADS, M_HEAD_DIM), mkv).reshape(bn, t, M_WIDTH)
    u = jnp.concatenate([y.reshape(bn, t, C_WIDTH) * jax.nn.silu(g), y_m * jax.nn.silu(g_m)], axis=-1).astype(x.dtype)
    x = layer_norm(ALPHA * x + u @ w_out, ln_g, ln_b)
    return x, r_new


def setup_inputs(seed: int = 0) -> dict:
    key = jax.random.key(seed)
    ks = iter(jax.random.split(key, 40))

    def nrm(shape, scale=1.0):
        return jax.random.normal(next(ks), shape, F32) * scale

    d = D_MODEL
    rows = [min(w, PAST_LEN) for w, _ in B_CONFIGS]
    return {
        'x_prompt': nrm((BATCH, SEQ, d)),
        'x_sample': nrm((DEC_BATCH, DEC_SEQ, d)),
        'state_rwkv': nrm((L_EVEN, DEC_BATCH, A_HEADS, A_HEAD_DIM, A_HEAD_DIM), 0.5),
        'state_rwkv_shift': nrm((L_EVEN, DEC_BATCH, A_SHIFT_W)),
        'cache_dwa_g0': nrm((L_EVEN, DEC_BATCH, rows[0], 2, B_HEADS_PER_GROUP, B_HEAD_DIM)),
        'cache_dwa_g1': nrm((L_EVEN, DEC_BATCH, rows[1], 2, B_HEADS_PER_GROUP, B_HEAD_DIM)),
        'cache_dwa_g2': nrm((L_EVEN, DEC_BATCH, rows[2], 2, B_HEADS_PER_GROUP, B_HEAD_DIM)),
        'state_ret': nrm((L_ODD, DEC_BATCH, C_HEADS, C_HEAD_DIM, C_HEAD_DIM), 0.5),
        'cache_mem_kv': nrm((DEPTH, DEC_BATCH, M_TOKENS, 2, M_HEADS, M_HEAD_DIM)),
        'mem_prompt': nrm((BATCH, M_TOKENS, d)),
        'w_in_even': nrm((L_EVEN, d, EVEN_IN), d ** -0.5),
        'w_out_even': nrm((L_EVEN, EVEN_OUT, d), BETA * EVEN_OUT ** -0.5),
        'w_in_odd': nrm((L_ODD, d, ODD_IN), d ** -0.5),
        'w_out_odd': nrm((L_ODD, ODD_OUT, d), BETA * ODD_OUT ** -0.5),
        'w_mem_kv': nrm((DEPTH, d, 2 * M_WIDTH), d ** -0.5),
        'ln_g': 1.0 + nrm((DEPTH, d), 0.02),
        'ln_b': nrm((DEPTH, d), 0.02),
        'rwkv_mu': jax.random.uniform(next(ks), (L_EVEN, A_SHIFT_W), F32),
        'rwkv_w0': jnp.linspace(-6.0, -1.0, A_WIDTH, dtype=F32)[None] + nrm((L_EVEN, A_WIDTH), 0.1),
        'rwkv_w_up': nrm((L_EVEN, A_LORA_W, A_WIDTH), 0.1),
        'rwkv_a0': nrm((L_EVEN, A_WIDTH), 0.1),
        'rwkv_a_up': nrm((L_EVEN, A_LORA_A, A_WIDTH), A_LORA_A ** -0.5),
        'rwkv_k_k': 0.85 + nrm((L_EVEN, A_WIDTH), 0.02),
        'rwkv_k_a': 1.0 + nrm((L_EVEN, A_WIDTH), 0.02),
        'rwkv_r_k': nrm((L_EVEN, A_HEADS, A_HEAD_DIM), 0.1),
        'rwkv_lnx_g': 1.0 + nrm((L_EVEN, A_WIDTH), 0.02),
        'rwkv_lnx_b': nrm((L_EVEN, A_WIDTH), 0.02),
    }


def reference(x_prompt, x_sample, state_rwkv, state_rwkv_shift, cache_dwa_g0, cache_dwa_g1, cache_dwa_g2, state_ret, cache_mem_kv, mem_prompt, w_in_even, w_out_even, w_in_odd, w_out_odd, w_mem_kv, ln_g, ln_b, rwkv_mu, rwkv_w0, rwkv_w_up, rwkv_a0, rwkv_a_up, rwkv_k_k, rwkv_k_a, rwkv_r_k, rwkv_lnx_g, rwkv_lnx_b):
    xp, xs = x_prompt, x_sample
    bp = xp.shape[0]
    dwa_cache = (cache_dwa_g0, cache_dwa_g1, cache_dwa_g2)
    rwkv_p, rwkv_s, shift_p, shift_s, ret_p, ret_s, mem_p = [], [], [], [], [], [], []
    dwa_p = [[] for _ in B_CONFIGS]
    dwa_s = [[] for _ in B_CONFIGS]
    for l in range(DEPTH):
        mkv_p = (mem_prompt @ w_mem_kv[l]).reshape(bp, M_TOKENS, 2, M_HEADS, M_HEAD_DIM)
        mem_p.append(mkv_p)
        mkv_s = cache_mem_kv[l]
        if l % 2 == 0:
            e = l // 2
            prm = (w_in_even[e], w_out_even[e], ln_g[l], ln_b[l], rwkv_mu[e], rwkv_w0[e], rwkv_w_up[e], rwkv_a0[e],
                   rwkv_a_up[e], rwkv_k_k[e], rwkv_k_a[e], rwkv_r_k[e], rwkv_lnx_g[e], rwkv_lnx_b[e])
            s0 = jnp.zeros((bp, A_HEADS, A_HEAD_DIM, A_HEAD_DIM), F32)
            sh0 = jnp.zeros((bp, A_SHIFT_W), xp.dtype)
            xp, st, sh, rows = even_layer(xp, mkv_p, None, s0, sh0, *prm)
            rwkv_p.append(st)
            shift_p.append(sh)
            for g in range(B_GROUPS):
                dwa_p[g].append(rows[g])
            bufs = tuple(c[e] for c in dwa_cache)
            xs, st, sh, rows = even_layer(xs, mkv_s, bufs, state_rwkv[e], state_rwkv_shift[e], *prm)
            rwkv_s.append(st)
            shift_s.append(sh)
            for g in range(B_GROUPS):
                dwa_s[g].append(rows[g])
        else:
            o = l // 2
            prm = (w_in_odd[o], w_out_odd[o], ln_g[l], ln_b[l])
            r0 = jnp.zeros((bp, C_HEADS, C_HEAD_DIM, C_HEAD_DIM), F32)
            xp, st = odd_layer(xp, mkv_p, r0, 0, *prm)
            ret_p.append(st)
            xs, st = odd_layer(xs, mkv_s, state_ret[o], PAST_LEN, *prm)
            ret_s.append(st)
    y_prompt, y_sample = xp, xs
    new_rwkv_prompt, new_rwkv_sample = jnp.stack(rwkv_p), jnp.stack(rwkv_s)
    new_shift_prompt, new_shift_sample = jnp.stack(shift_p), jnp.stack(shift_s)
    new_dwa_g0_prompt, new_dwa_g0_sample = jnp.stack(dwa_p[0]), jnp.stack(dwa_s[0])
    new_dwa_g1_prompt, new_dwa_g1_sample = jnp.stack(dwa_p[1]), jnp.stack(dwa_s[1])
    new_dwa_g2_prompt, new_dwa_g2_sample = jnp.stack(dwa_p[2]), jnp.stack(dwa_s[2])
    new_ret_prompt, new_ret_sample = jnp.stack(ret_p), jnp.stack(ret_s)
    new_mem_kv_prompt = jnp.stack(mem_p)
    return (y_prompt, y_sample, new_rwkv_prompt, new_rwkv_sample, new_shift_prompt, new_shift_sample, new_dwa_g0_prompt, new_dwa_g0_sample, new_dwa_g1_prompt, new_dwa_g1_sample, new_dwa_g2_prompt, new_dwa_g2_sample, new_ret_prompt, new_ret_sample, new_mem_kv_prompt)
```

```python
import contextlib
import os
import numpy as np
import concourse.bass as bass
import concourse.mybir as mybir
from concourse.bass_utils import run_bass_kernel_spmd

F32 = mybir.dt.float32
BF16 = mybir.dt.bfloat16
AF = mybir.ActivationFunctionType
ALU = mybir.AluOpType
AX = mybir.AxisListType

D = 2048
KC = D // 128
TS = 4
ALPHA_FULL_DEPTH = 4
LN_EPS = 1e-5
A_H, A_N = 12, 64
A_W = 768
A_SHIFT = 3 * A_W + 128
B_W = 768
B_OUT = 256
M_W = 512
C_H, C_D = 6, 256
C_W = 1536
EVEN_IN = A_SHIFT + A_W + 3 * B_W + B_OUT + 2 * M_W
EVEN_OUT = A_W + B_OUT + M_W
ODD_IN = 4 * C_W + 2 * M_W
ODD_OUT = C_W + M_W
B_CONFIGS = ((128, 1), (512, 4), (2048, 16))


class FW:
    def __init__(self, nc, n_dma_slots=14):
        self.nc = nc
        self.engs = {"pe": nc.tensor, "act": nc.scalar, "dve": nc.vector, "pool": nc.gpsimd, "sp": nc.sync}
        self.sem, self.cnt, self._ctx = {}, {}, []
        for e in ("pe", "act", "dve", "pool"):
            cm = nc.semaphore("s_" + e)
            self.sem[e] = cm.__enter__()
            self._ctx.append(cm)
            self.cnt[e] = 0
        self.slots = {}
        for q in ("sp", "act", "pool"):
            lst = []
            for i in range(n_dma_slots):
                cm = nc.semaphore(f"d_{q}{i}")
                s = cm.__enter__()
                self._ctx.append(cm)
                lst.append([s, 0])
            self.slots[q] = [lst, 0]
        self.known = {e: {} for e in self.engs}
        self.lastw, self.readers = {}, {}
        self.dw, self.dr = {}, {}
        self.n_inst = 0
        self.n_wait = 0

    @staticmethod
    def _isd(k):
        return isinstance(k, str) and k.startswith("D:") or (isinstance(k, tuple) and isinstance(k[0], str) and k[0].startswith("D:"))

    def _deps(self, reads, writes):
        deps = []
        for r in reads:
            if self._isd(r):
                deps.extend(self.dw.get(r, ()))
            elif r in self.lastw:
                deps.append(self.lastw[r])
        for w in writes:
            if self._isd(w):
                deps.extend(self.dr.get(w, ()))
            else:
                if w in self.lastw:
                    deps.append(self.lastw[w])
                deps.extend(self.readers.get(w, ()))
        return deps

    def _wait(self, e, deps, skip_sem=None):
        need = {}
        for (s, v) in deps:
            if skip_sem is not None and s is skip_sem:
                continue
            k = id(s)
            if k not in need or need[k][1] < v:
                need[k] = (s, v)
        kn = self.known[e]
        for k, (s, v) in need.items():
            if kn.get(k, 0) >= v:
                continue
            self.engs[e].wait_ge(s, v)
            self.n_wait += 1
            kn[k] = v

    def _record(self, tok, reads, writes):
        for r in reads:
            if self._isd(r):
                self.dr.setdefault(r, []).append(tok)
            else:
                self.readers.setdefault(r, []).append(tok)
        for w in writes:
            if self._isd(w):
                if self.dr.get(w):
                    self.dr[w] = []
                    self.dw[w] = []
                self.dw.setdefault(w, []).append(tok)
            else:
                self.lastw[w] = tok
                self.readers[w] = []

    def op(self, e, fn, reads=(), writes=()):
        deps = self._deps(reads, writes)
        self._wait(e, deps, skip_sem=self.sem["pe"] if e == "pe" else None)
        ins = fn(self.engs[e])
        self.cnt[e] += 1
        ins.then_inc(self.sem[e], 1)
        self.n_inst += 1
        tok = (self.sem[e], self.cnt[e])
        self._record(tok, reads, writes)
        return tok

    def dma(self, q, out, in_, reads=(), writes=()):
        deps = self._deps(reads, writes)
        lst, idx = self.slots[q]
        slot = lst[idx % len(lst)]
        self.slots[q][1] = idx + 1
        if slot[1] > 0:
            deps.append((slot[0], slot[1]))
        self._wait(q, deps)
        slot[1] += 16
        self.engs[q].dma_start(out=out, in_=in_).then_inc(slot[0], 16)
        self.n_inst += 1
        tok = (slot[0], slot[1])
        self._record(tok, reads, writes)
        return tok

    def all_toks(self):
        toks = []
        for e in ("pe", "act", "dve", "pool"):
            if self.cnt[e]:
                toks.append((self.sem[e], self.cnt[e]))
        for q in self.slots:
            for s, v in self.slots[q][0]:
                if v:
                    toks.append((s, v))
        return toks

    def barrier(self):
        if not hasattr(self, "bar"):
            cm = self.nc.semaphore("s_bar")
            self.bar = cm.__enter__()
            self._ctx.append(cm)
            self.barv = 0
        toks = self.all_toks()
        self._wait("sp", toks)
        self.barv += 1
        self.engs["sp"].sem_inc(self.bar, 1)
        for e in ("pe", "act", "dve", "pool"):
            self.engs[e].wait_ge(self.bar, self.barv)
            for (s_, v_) in toks:
                self.known[e][id(s_)] = max(self.known[e].get(id(s_), 0), v_)
        self.lastw, self.readers = {}, {}
        self.dw, self.dr = {}, {}

    def finish(self):
        toks = []
        for e in ("pe", "act", "dve", "pool"):
            if self.cnt[e]:
                toks.append((self.sem[e], self.cnt[e]))
        for q in self.slots:
            for s, v in self.slots[q][0]:
                if v:
                    toks.append((s, v))
        self._wait("sp", toks)

    def close(self):
        for cm in reversed(self._ctx):
            cm.__exit__(None, None, None)


class Rot:
    def __init__(self, tiles):
        self.tiles = tiles
        self.i = 0

    def next(self):
        t = self.tiles[self.i % len(self.tiles)]
        self.i += 1
        return t


class Builder:
    def __init__(self, SEQ, DEPTH, debug=False, kinds=None):
        self.SEQ, self.DEPTH, self.debug = SEQ, DEPTH, debug
        self.kinds = kinds or ["e" if l % 2 == 0 else "o" for l in range(DEPTH)]
        self.NT = SEQ + TS
        self.LE, self.LO = (DEPTH + 1) // 2, DEPTH // 2
        self.ALPHA = (2 * ALPHA_FULL_DEPTH) ** 0.25
        self.nc = nc = bass.Bass("TRN2", target_bir_lowering=False)
        self.fw = FW(nc)
        self.tiles = [(r, 128) for r in range(0, SEQ, 128)] + [(SEQ, TS)]
        self.keep = [min(w, SEQ) for w, _ in B_CONFIGS]
        self.io = {}
        self._cnt = 0

    @contextlib.contextmanager
    def phase(self):
        with contextlib.ExitStack() as es:
            yield es
            self.fw.barrier()

    def din(self, name, shape, dt=F32):
        t = self.nc.dram_tensor(name, list(shape), dt, kind="ExternalInput").ap()
        self.io[name] = t
        return t

    def dout(self, name, shape, dt=F32):
        t = self.nc.dram_tensor(name, list(shape), dt, kind="ExternalOutput").ap()
        self.io[name] = t
        return t

    def dscr(self, name, shape, dt=F32):
        kind = "ExternalOutput" if self.debug else "Internal"
        t = self.nc.dram_tensor(name, list(shape), dt, kind=kind).ap()
        self.io[name] = t
        return t

    def sb(self, es, name, shape, dt=F32):
        self._cnt += 1
        return es.enter_context(self.nc.sbuf_tensor(f"{name}_{self._cnt}", list(shape), dt))

    def ps(self, es, name, shape, dt=F32):
        self._cnt += 1
        t = es.enter_context(self.nc.psum_tensor(f"{name}_{self._cnt}", list(shape), dt))
        return t

    def sbn(self, es, name, shape, dt, n):
        return Rot([self.sb(es, f"{name}{i}", shape, dt) for i in range(n)])

    def psn(self, es, name, shape, dt, n):
        return Rot([self.ps(es, f"{name}{i}", shape, dt) for i in range(n)])

    @staticmethod
    def key(t):
        return t.name if hasattr(t, "name") else str(t)

    def declare(self):
        SEQ, NT, LE, LO, DEPTH = self.SEQ, self.NT, self.LE, self.LO, self.DEPTH
        d = self.din
        self.x_in = d("x_in", [NT, D])
        self.memp = d("memp", [256, D])
        self.st_rwkv = d("st_rwkv", [LE, A_H, 64, 64])
        self.st_shift = d("st_shift", [LE, A_SHIFT])
        self.c_g = [d(f"c_g{g}", [LE, B_CONFIGS[g][0], 512]) for g in range(3)]
        self.st_ret = d("st_ret", [max(LO, 1), C_H, 256, 256])
        self.c_mem = d("c_mem", [DEPTH, 256, 1024])
        self.w_in_even = d("w_in_even", [LE, D, EVEN_IN])
        self.w_out_even = d("w_out_even", [LE, EVEN_OUT, D])
        self.w_in_odd = d("w_in_odd", [max(LO, 1), D, ODD_IN])
        self.w_out_odd = d("w_out_odd", [max(LO, 1), ODD_OUT, D])
        self.w_mem_kv = d("w_mem_kv", [DEPTH, D, 1024])
        self.ln_g = d("ln_g", [DEPTH, D])
        self.ln_b = d("ln_b", [DEPTH, D])
        self.rw = {}
        for nm, shp in (("mu", [LE, A_SHIFT]), ("w0", [LE, A_W]), ("w_up", [LE, 64, A_W]), ("a0", [LE, A_W]),
                        ("a_up", [LE, 64, A_W]), ("k_k", [LE, A_W]), ("k_a", [LE, A_W]), ("r_k", [LE, A_W]),
                        ("lnx_g", [LE, A_W]), ("lnx_b", [LE, A_W])):
            self.rw[nm] = d("rwkv_" + nm, shp)
        self.c_ident = d("c_ident", [128, 128])
        self.c_rot = d("c_rot", [NT, 2, 256])
        self.c_retsc = d("c_retsc", [128, 2, C_H])
        self.c_masks = d("c_masks", [128, 4, 128])
        o = self.dout
        self.y = o("y", [NT, D])
        self.o_rwkv_p = o("o_rwkv_p", [LE, A_H, 64, 64])
        self.o_rwkv_s = o("o_rwkv_s", [LE, A_H, 64, 64])
        self.o_shift_p = o("o_shift_p", [LE, A_SHIFT])
        self.o_shift_s = o("o_shift_s", [LE, A_SHIFT])
        self.o_g_p = [o(f"o_g{g}_p", [LE, self.keep[g], 512]) for g in range(3)]
        self.o_g_s = [o(f"o_g{g}_s", [LE, TS, 512]) for g in range(3)]
        self.o_ret_p = o("o_ret_p", [max(LO, 1), C_H, 256, 256])
        self.o_ret_s = o("o_ret_s", [max(LO, 1), C_H, 256, 256])
        self.o_mem_p = o("o_mem_p", [DEPTH, 256, 1024])
        s = self.dscr
        self.xT_d = s("xT_d", [D, NT], BF16)
        self.memT_d = s("memT_d", [D, 256], BF16)
        self.h_d = s("h_d", [NT, ODD_IN])
        self.u_d = s("u_d", [NT, D], BF16)
        self.x_d = [s("x_d0", [NT, D]), s("x_d1", [NT, D])]
        self.dwa_d = s("dwa_d", [3, SEQ + TS, 260])

    def load_consts(self, es):
        fw = self.fw
        self.idf = self.sb(es, "idf", [128, 128], F32)
        self.idb = self.sb(es, "idb", [128, 128], BF16)
        fw.dma("sp", self.idf[:], self.c_ident, writes=["idf"])
        fw.op("dve", lambda e: e.tensor_copy(out=self.idb[:], in_=self.idf[:]), reads=["idf"], writes=["idb"])
        self.maskf = self.sb(es, "maskf", [128, 4, 128], F32)
        self.maskb = self.sb(es, "maskb", [128, 4, 128], BF16)
        fw.dma("sp", self.maskf[:], self.c_masks, writes=["maskf"])
        fw.op("dve", lambda e: e.tensor_copy(out=self.maskb[:], in_=self.maskf[:]), reads=["maskf"], writes=["maskb"])

    def transposes(self, dst, src, nblk, n, ptrot, dkey, skey, evac=("act", "dve")):
        fw = self.fw
        gi = 0
        for g0 in range(0, nblk, 4):
            gn = min(4, nblk - g0)
            pt = ptrot.next()
            pk = self.key(pt)
            for j in range(gn):
                k = g0 + j
                fw.op("pe", lambda e, k=k, j=j, pt=pt: e.transpose(out=pt[:, j, :n], in_=src[:n, k * 128:(k + 1) * 128], identity=self.idb[:n, :n]),
                      reads=[skey, "idb"], writes=[pk])
            en = evac[gi % len(evac)]
            gi += 1
            if en == "act":
                fw.op("act", lambda e, pt=pt, g0=g0, gn=gn: e.copy(out=dst[:, g0:g0 + gn, :n], in_=pt[:, :gn, :n]), reads=[pk], writes=[dkey])
            else:
                fw.op(en, lambda e, pt=pt, g0=g0, gn=gn: e.tensor_copy(out=dst[:, g0:g0 + gn, :n], in_=pt[:, :gn, :n]), reads=[pk], writes=[dkey])

    def phase_xT_from(self, src_dram, rows_tiles, dstT, dkeyname):
        fw = self.fw
        with self.phase() as es:
            xf = self.sbn(es, "p0xf", [128, D], F32, 2)
            xb = self.sbn(es, "p0xb", [128, D], BF16, 2)
            xT = self.sbn(es, "p0xT", [128, KC, 128], BF16, 2)
            pt = self.psn(es, "p0pt", [128, 4, 128], BF16, 2)
            for (r0, n) in rows_tiles:
                a, b_, c = xf.next(), xb.next(), xT.next()
                fw.dma("sp", a[:n, :], src_dram[r0:r0 + n, :], writes=[self.key(a)])
                fw.op("pool", lambda e, a=a, b_=b_: e.tensor_copy(out=b_[:n, :], in_=a[:n, :]), reads=[self.key(a)], writes=[self.key(b_)])
                self.transposes(c, b_, KC, n, pt, self.key(c), self.key(b_))
                fw.dma("pool", dstT.rearrange("(k p) t -> p k t", p=128)[:, :, r0:r0 + n], c[:, :, :n], reads=[self.key(c)], writes=[dkeyname])

    def proj(self, W, ncol, xT_src, xkey, blocks, dst, dkey):
        fw = self.fw
        with self.phase() as es:
            wst = self.sbn(es, "pjws", [128, KC // 2, 512], F32, 4)
            wbf = self.sbn(es, "pjwb", [128, KC, 512], BF16, 4)
            xTb = self.sbn(es, "pjx", [128, KC, 516], BF16, 2)
            ho = self.sbn(es, "pjho", [128, 512], F32, 4)
            pp = self.psn(es, "pjps", [128, 512], F32, 4)
            Wv = W.rearrange("(k p) n -> p k n", p=128)
            xv = xT_src.rearrange("(k p) t -> p k t", p=128)
            ei = 0
            h = KC // 2
            cgs = [(cg, min(512, ncol - cg)) for cg in range(0, ncol, 512)]
            def load_pair(pi):
                wbs = []
                for (cg, cw) in cgs[pi:pi + 2]:
                    wb = wbf.next()
                    for hf in range(2):
                        ws = wst.next()
                        fw.dma("sp" if hf == 0 else "act", ws[:, :, :cw], Wv[:, hf * h:(hf + 1) * h, cg:cg + cw], writes=[self.key(ws)])
                        for q in range(2):
                            en = "pool" if q % 2 else "dve"
                            k0 = hf * h + q * 4
                            fw.op(en, lambda e, q=q, ws=ws, wb=wb, k0=k0, cw=cw: e.tensor_copy(out=wb[:, k0:k0 + 4, :cw], in_=ws[:, q * 4:(q + 1) * 4, :cw]),
                                  reads=[self.key(ws)], writes=[(self.key(wb), k0 // 4)])
                    wbs.append((wb, cg, cw))
                return wbs

            nxt = load_pair(0)
            for pi in range(0, len(cgs), 2):
                wbs = nxt
                nxt = None
                for (c0, nb, tl) in blocks:
                    xt = xTb.next()
                    fw.dma("sp", xt[:, :, :nb], xv[:, :, c0:c0 + nb], reads=[xkey], writes=[self.key(xt)])
                    for (off, n, r0) in tl:
                        ps = [pp.next() for _ in wbs]
                        for k in range(KC):
                            for p, (wb, cg, cw) in zip(ps, wbs):
                                fw.op("pe", lambda e, k=k, p=p, xt=xt, wb=wb, off=off, n=n, cw=cw: e.matmul(p[:n, :cw], lhsT=xt[:, k, off:off + n], rhs=wb[:, k, :cw], start=(k == 0), stop=(k == KC - 1)),
                                      reads=[self.key(xt), (self.key(wb), k // 4)], writes=[self.key(p)])
                        for p, (wb, cg, cw) in zip(ps, wbs):
                            o = ho.next()
                            if ei % 2 == 0:
                                fw.op("act", lambda e, o=o, p=p, n=n, cw=cw: e.copy(out=o[:n, :cw], in_=p[:n, :cw]), reads=[self.key(p)], writes=[self.key(o)])
                            else:
                                fw.op("dve", lambda e, o=o, p=p, n=n, cw=cw: e.tensor_copy(out=o[:n, :cw], in_=p[:n, :cw]), reads=[self.key(p)], writes=[self.key(o)])
                            ei += 1
                            fw.dma("pool", dst[r0:r0 + n, cg:cg + cw], o[:n, :cw], reads=[self.key(o)], writes=[dkey])
                    if nxt is None and pi + 2 < len(cgs) and (c0, nb, tl) == blocks[min(1, len(blocks) - 1)]:
                        nxt = load_pair(pi + 2)

    def tok_blocks(self):
        SEQ = self.SEQ
        blocks = []
        for c0 in range(0, SEQ, 512):
            nb = min(512, SEQ - c0)
            tl = [(o, 128, c0 + o) for o in range(0, nb, 128)]
            blocks.append([c0, nb, tl])
        blocks[-1][1] += TS
        blocks[-1][2].append((blocks[-1][1] - TS, TS, SEQ))
        return [tuple(b) for b in blocks]

    def mem_prep(self, es, src, skey, name):
        fw = self.fw
        kT = self.sb(es, name + "kT", [128, 4, 256], BF16)
        va = self.sb(es, name + "va", [128, 2, 4, 129], BF16)
        with self.phase() as es2:
            kv = self.sb(es2, "mpkv", [128, 2, 1024], F32)
            kb = self.sb(es2, "mpkb", [128, 2, 512], BF16)
            pt = self.psn(es2, "mppt", [128, 4, 128], BF16, 2)
            fw.dma("sp", kv[:], src.rearrange("(c p) n -> p c n", p=128), reads=[skey], writes=["mpkv"])
            fw.op("dve", lambda e: e.tensor_copy(out=kb[:], in_=kv[:, :, 0:512]), reads=["mpkv"], writes=["mpkb"])
            fw.op("pool", lambda e: e.memset(va[:], 1.0), writes=[name + "va"])
            fw.op("pool", lambda e: e.tensor_copy(out=va[:, :, :, 0:128], in_=kv[:, :, 512:1024].rearrange("p c (h d) -> p c h d", h=4)),
                  reads=["mpkv"], writes=[name + "va"])
            for c in range(2):
                p = pt.next()
                for h in range(4):
                    fw.op("pe", lambda e, h=h, c=c, p=p: e.transpose(out=p[:, h, :], in_=kb[:, c, h * 128:(h + 1) * 128], identity=self.idb[:]),
                          reads=["mpkb", "idb"], writes=[self.key(p)])
                fw.op("act", lambda e, c=c, p=p: e.copy(out=kT[:, :, c * 128:(c + 1) * 128], in_=p[:]), reads=[self.key(p)], writes=[name + "kT"])
        return kT, va, name + "kT", name + "va"

    def mem_attn(self, tiles, kT, va, kTk, vak, qcol, ucol):
        fw = self.fw
        sc = 1.0 / np.sqrt(128.0)
        with self.phase() as es:
            qg = self.sbn(es, "maqg", [128, 1024], F32, 2)
            qb = self.sbn(es, "maqb", [128, 512], BF16, 2)
            qT = self.sbn(es, "maqT", [128, 4, 128], BF16, 2)
            pT = self.sbn(es, "mapT", [128, 8, 128], BF16, 2)
            rl = self.sbn(es, "marl", [128, 4], F32, 2)
            sg = self.sbn(es, "masg", [128, 512], F32, 2)
            om = self.sbn(es, "maom", [128, 512], F32, 2)
            ub = self.sbn(es, "maub", [128, 512], BF16, 2)
            pt = self.psn(es, "mapt", [128, 4, 128], BF16, 1)
            pss = self.psn(es, "mapss", [128, 8, 128], F32, 2)
            pso = self.psn(es, "mapso", [128, 4, 256], F32, 1)
            pend = [None]
            for (r0, n) in tiles:
                q, b_, t_, P, r_, s_, o_, u_ = qg.next(), qb.next(), qT.next(), pT.next(), rl.next(), sg.next(), om.next(), ub.next()
                k = self.key
                fw.dma("sp", q[:n, :], self.h_d[r0:r0 + n, qcol:qcol + 1024], reads=["D:h"], writes=[k(q)])
                fw.op("pool", lambda e, q=q, b_=b_: e.tensor_copy(out=b_[:n, :], in_=q[:n, 0:512]), reads=[k(q)], writes=[k(b_)])
                self.transposes(t_, b_, 4, n, pt, k(t_), k(b_), evac=("dve",))
                S = pss.next()
                for h in range(4):
                    for mc in range(2):
                        fw.op("pe", lambda e, h=h, mc=mc, S=S, t_=t_: e.matmul(S[:, h * 2 + mc, :n], lhsT=kT[:, h, mc * 128:(mc + 1) * 128], rhs=t_[:, h, :n], start=True, stop=True),
                              reads=[kTk, k(t_)], writes=[k(S)])
                fw.op("act", lambda e, S=S, P=P: e.activation(out=P[:, :, :n], in_=S[:, :, :n], func=AF.Exp, scale=float(sc)), reads=[k(S)], writes=[k(P)])
                fw.op("act", lambda e, q=q, s_=s_: e.activation(out=s_[:n, :], in_=q[:n, 512:1024], func=AF.Silu), reads=[k(q)], writes=[k(s_)])

                def part2(r0=r0, n=n, P=P, r_=r_, s_=s_, o_=o_, u_=u_):
                    O = pso.next()
                    for h in range(4):
                        for mc in range(2):
                            fw.op("pe", lambda e, h=h, mc=mc, O=O: e.matmul(O[:n, h, 0:129], lhsT=P[:, h * 2 + mc, :n], rhs=va[:, mc, h, :], start=(mc == 0), stop=(mc == 1)),
                                  reads=[vak, k(P)], writes=[k(O)])
                    fw.op("dve", lambda e, O=O: e.reciprocal(out=r_[:n, :], in_=O[:n, :, 128]), reads=[k(O)], writes=[k(r_)])
                    fw.op("dve", lambda e, O=O: e.tensor_tensor(out=o_[:n, :].rearrange("p (h d) -> p h d", h=4), in0=O[:n, :, 0:128],
                                                                in1=r_[:n, :].unsqueeze(2).to_broadcast([n, 4, 128]), op=ALU.mult),
                          reads=[k(O), k(r_)], writes=[k(o_)])
                    fw.op("pool", lambda e: e.tensor_tensor(out=u_[:n, :], in0=o_[:n, :], in1=s_[:n, :], op=ALU.mult), reads=[k(o_), k(s_)], writes=[k(u_)])
                    fw.dma("pool", self.u_d[r0:r0 + n, ucol:ucol + 512], u_[:n, :], reads=[k(u_)], writes=["D:u"])
                if pend[0] is not None:
                    pend[0]()
                pend[0] = part2
            if pend[0] is not None:
                pend[0]()

    def out_ln(self, l, Wout, UW, x_src, xskey, last):
        fw = self.fw
        nk = UW // 128
        SD, FM, AD = self.nc.vector.BN_STATS_DIM, self.nc.vector.BN_STATS_FMAX, self.nc.vector.BN_AGGR_DIM
        nch = D // FM
        with self.phase() as es:
            wo = self.sb(es, "olwo", [128, nk, D], BF16)
            wst = self.sbn(es, "olws", [128, D], F32, 2)
            gt = self.sb(es, "olg", [128, D], F32)
            bt = self.sb(es, "olb", [128, D], F32)
            fw.dma("sp", gt[:], self.ln_g[l].partition_broadcast(128), writes=["olg"])
            fw.dma("sp", bt[:], self.ln_b[l].partition_broadcast(128), writes=["olb"])
            for kk in range(nk):
                w = wst.next()
                fw.dma("sp" if kk % 2 else "act", w[:], Wout[kk * 128:(kk + 1) * 128, :], writes=[self.key(w)])
                fw.op("pool" if kk % 2 else "dve", lambda e, w=w, kk=kk: e.tensor_copy(out=wo[:, kk, :], in_=w[:]), reads=[self.key(w)], writes=[("olwo", kk)])
            wkeys = [("olwo", kk) for kk in range(nk)]
            ut = self.sbn(es, "olu", [128, UW], BF16, 2)
            uT = self.sbn(es, "oluT", [128, nk, 128], BF16, 2)
            xt = self.sbn(es, "olx", [128, D], F32, 2)
            zt = self.sbn(es, "olz", [128, D], F32, 3)
            st = self.sbn(es, "olst", [128, nch, SD], F32, 2)
            mv = self.sbn(es, "olmv", [128, AD], F32, 2)
            rs = self.sbn(es, "olrs", [128, 1], F32, 2)
            xb = self.sbn(es, "olxb", [128, D], BF16, 2)
            xT = self.sbn(es, "olxT", [128, KC, 128], BF16, 2)
            pt = self.psn(es, "olpt", [128, 4, 128], BF16, 2)
            po = self.psn(es, "olpo", [128, 4, 512], F32, 1)
            k = self.key
            pend = None

            def tail(r0, n, z):
                b_, T_ = xb.next(), xT.next()
                fw.op("act", lambda e, b_=b_, z=z: e.copy(out=b_[:n, :], in_=z[:n, :]), reads=[k(z)], writes=[k(b_)])
                self.transposes(T_, b_, KC, n, pt, k(T_), k(b_))
                fw.dma("pool", self.xT_d.rearrange("(k p) t -> p k t", p=128)[:, :, r0:r0 + n], T_[:, :, :n], reads=[k(T_)], writes=["D:xT"])

            for (r0, n) in self.tiles:
                u, uT_, x_, z, s_, m_, r_ = ut.next(), uT.next(), xt.next(), zt.next(), st.next(), mv.next(), rs.next()
                fw.dma("sp", u[:n, :], self.u_d[r0:r0 + n, 0:UW], reads=["D:u"], writes=[k(u)])
                fw.dma("sp", x_[:n, :], x_src[r0:r0 + n, :], reads=[xskey], writes=[k(x_)])
                self.transposes(uT_, u, nk, n, pt, k(uT_), k(u))
                P = po.next()
                for kk in range(nk):
                    for nb in range(4):
                        fw.op("pe", lambda e, nb=nb, kk=kk, P=P, uT_=uT_: e.matmul(P[:n, nb, :], lhsT=uT_[:, kk, :n], rhs=wo[:, kk, nb * 512:(nb + 1) * 512], start=(kk == 0), stop=(kk == nk - 1)),
                              reads=[k(uT_), ("olwo", kk)], writes=[k(P)])
                if pend is not None:
                    tail(*pend)
                    pend = None
                fw.op("dve", lambda e, z=z, x_=x_, P=P: e.scalar_tensor_tensor(out=z[:n, :], in0=x_[:n, :], scalar=float(self.ALPHA), in1=P[:n].rearrange("p a b -> p (a b)"), op0=ALU.mult, op1=ALU.add),
                      reads=[k(x_), k(P)], writes=[k(z)])
                for c in range(nch):
                    fw.op("dve", lambda e, c=c, s_=s_, z=z: e.bn_stats(out=s_[:n, c, :], in_=z[:n, c * FM:(c + 1) * FM]), reads=[k(z)], writes=[k(s_)])
                fw.op("dve", lambda e, s_=s_, m_=m_: e.bn_aggr(out=m_[:n, :], in_=s_[:n]), reads=[k(s_)], writes=[k(m_)])
                fw.op("dve", lambda e, r_=r_, m_=m_: e.tensor_scalar_add(out=r_[:n, :], in0=m_[:n, 1:2], scalar1=LN_EPS), reads=[k(m_)], writes=[k(r_)])
                fw.op("act", lambda e, r_=r_: e.sqrt(out=r_[:n, :], in_=r_[:n, :]), reads=[k(r_)], writes=[k(r_)])
                fw.op("dve", lambda e, r_=r_: e.reciprocal(out=r_[:n, :], in_=r_[:n, :]), reads=[k(r_)], writes=[k(r_)])
                fw.op("dve", lambda e, z=z, m_=m_, r_=r_: e.tensor_scalar(out=z[:n, :], in0=z[:n, :], scalar1=m_[:n, 0:1], scalar2=r_[:n, 0:1], op0=ALU.subtract, op1=ALU.mult),
                      reads=[k(z), k(m_), k(r_)], writes=[k(z)])
                fw.op("pool", lambda e, z=z: e.tensor_tensor(out=z[:n, :], in0=z[:n, :], in1=gt[:n, :], op=ALU.mult), reads=[k(z), "olg"], writes=[k(z)])
                fw.op("pool", lambda e, z=z: e.tensor_tensor(out=z[:n, :], in0=z[:n, :], in1=bt[:n, :], op=ALU.add), reads=[k(z), "olb"], writes=[k(z)])
                if last:
                    fw.dma("pool", self.y[r0:r0 + n, :], z[:n, :], reads=[k(z)], writes=["D:y"])
                else:
                    fw.dma("pool", self.x_d[l % 2][r0:r0 + n, :], z[:n, :], reads=[k(z)], writes=[f"D:x{l % 2}"])
                    pend = (r0, n, z)
            if pend is not None:
                tail(*pend)

    def build(self, stop_after=None):
        fw = self.fw
        self.declare()
        with self.phase() as es:
            self.load_consts(es)
            self.phase_xT_from(self.x_in, self.tiles, self.xT_d, "D:xT")
            self.phase_xT_from(self.memp, [(0, 128), (128, 128)], self.memT_d, "D:memT")
            blocks = self.tok_blocks()
            mblocks = [(0, 256, [(0, 128, 0), (128, 128, 128)])]
            for l in range(self.DEPTH):
                even = self.kinds[l] == "e"
                i2 = sum(1 for q in self.kinds[:l] if q == self.kinds[l])
                W = self.w_in_even[i2] if even else self.w_in_odd[i2]
                ncol = EVEN_IN if even else ODD_IN
                self.proj(W, ncol, self.xT_d, "D:xT", blocks, self.h_d, "D:h")
                self.proj(self.w_mem_kv[l], 1024, self.memT_d, "D:memT", mblocks, self.o_mem_p[l], "D:omem")
                qcol = ncol - 1024
                ucol = (EVEN_OUT if even else ODD_OUT) - 512
                with self.phase() as es2:
                    kT, va, kTk, vak = self.mem_prep(es2, self.o_mem_p[l], "D:omem", "mp")
                    self.mem_attn(self.tiles[:-1], kT, va, kTk, vak, qcol, ucol)
                with self.phase() as es2:
                    kT, va, kTk, vak = self.mem_prep(es2, self.c_mem[l], "D:cmem", "ms")
                    self.mem_attn(self.tiles[-1:], kT, va, kTk, vak, qcol, ucol)
                if even:
                    self.even_mixers(i2)
                else:
                    self.odd_mixers(i2)
                x_src, xk = (self.x_in, "D:xin") if l == 0 else (self.x_d[(l - 1) % 2], f"D:x{(l - 1) % 2}")
                self.out_ln(l, self.w_out_even[i2] if even else self.w_out_odd[i2], EVEN_OUT if even else ODD_OUT, x_src, xk, l == self.DEPTH - 1)
            fw.finish()
        fw.close()
        return self.nc

    def even_mixers(self, e):
        self.dwa(e)
        self.rwkv(e)

    def dwa(self, e):
        fw, k = self.fw, self.key
        SEQ = self.SEQ
        QC, KCOL, VC, GB = 3200, 3968, 4736, 5504
        hp_ = self.h_d[0:SEQ, :]
        for g in range(3):
            kp = self.keep[g]
            for j, col in enumerate((KCOL + g * 256, VC + g * 256)):
                fw.dma("act", self.o_g_p[g][e][:, j * 256:(j + 1) * 256], self.h_d[SEQ - kp:SEQ, col:col + 256], reads=["D:h"], writes=["D:ogp"])
                fw.dma("act", self.o_g_s[g][e][:, j * 256:(j + 1) * 256], self.h_d[SEQ:SEQ + TS, col:col + 256], reads=["D:h"], writes=["D:ogs"])
        fw.dma("act", self.o_shift_p[e], self.h_d[SEQ - 1, 0:A_SHIFT], reads=["D:h"], writes=["D:osh"])
        fw.dma("act", self.o_shift_s[e], self.h_d[SEQ + TS - 1, 0:A_SHIFT], reads=["D:h"], writes=["D:osh"])
        units = []
        for g, (win, d) in enumerate(B_CONFIGS):
            qc, kc, vc = QC + g * 256, KCOL + g * 256, VC + g * 256
            Lc = SEQ // d
            n = min(128, Lc)
            hv = hp_.rearrange("(m d) c -> d m c", d=d)
            ov = self.dwa_d[g][0:SEQ, :].rearrange("(m d) c -> d m c", d=d)
            for r in range(d):
                for m0 in range(0, Lc, n):
                    cur = hv[r, m0:m0 + n]
                    prev = hv[r, m0 - n:m0] if m0 > 0 else None
                    units.append((cur[:, qc:qc + 256], cur[:, kc:kc + 256], cur[:, vc:vc + 256], n,
                                  None if prev is None else prev[:, kc:kc + 256], None if prev is None else prev[:, vc:vc + 256], n, ov[r, m0:m0 + n], "D:h"))
            cg = self.c_g[g][e]
            if d == 1:
                cur = self.h_d[SEQ:SEQ + TS]
                units.append((cur[:, qc:qc + 256], cur[:, kc:kc + 256], cur[:, vc:vc + 256], TS, cg[:, 0:256], cg[:, 256:512], 128, self.dwa_d[g][SEQ:SEQ + TS, :], "D:cg"))
            else:
                cv = cg.rearrange("(m d) c -> d m c", d=d)
                for i in range(TS):
                    cur = self.h_d[SEQ + i:SEQ + i + 1]
                    units.append((cur[:, qc:qc + 256], cur[:, kc:kc + 256], cur[:, vc:vc + 256], 1, cv[i][:, 0:256], cv[i][:, 256:512], 128, self.dwa_d[g][SEQ + i:SEQ + i + 1, :], "D:cg"))
        import os
        bis = os.environ.get("DWA_BIS", "")
        if bis == "none":
            units = []
        elif bis == "prompt":
            units = [u for u in units if u[8] == "D:h"]
        elif bis == "g0":
            units = units[:4]
        elif bis == "samp":
            units = [u for u in units if u[8] != "D:h"]
        with self.phase() as es:
            Xc = self.sbn(es, "dwXc", [128, 3, 256], F32, 2)
            Xp = self.sbn(es, "dwXp", [128, 2, 256], F32, 2)
            qkb = self.sbn(es, "dwqkb", [128, 3, 256], BF16, 2)
            Va = self.sbn(es, "dwVa", [128, 2, 4, 65], BF16, 2)
            T = self.sbn(es, "dwT", [64, 3, 4, 128], BF16, 2)
            P = self.sbn(es, "dwP", [128, 4, 2, 128], BF16, 2)
            Os = self.sbn(es, "dwOs", [128, 260], F32, 2)
            pt = self.psn(es, "dwpt", [128, 4, 128], BF16, 2)
            pS = self.psn(es, "dwpS", [128, 4, 2, 128], F32, 2)
            pO = self.psn(es, "dwpO", [128, 4, 128], F32, 2)
            pend2 = [None]
            for (qs, ks, vs, n, pks, pvs, npv, dst, pkey) in units:
                xc, xp, qb, va, t, p, os_ = Xc.next(), Xp.next(), qkb.next(), Va.next(), T.next(), P.next(), Os.next()
                hasp = pks is not None
                fw.dma("sp", xc[:n, 0, :], qs, reads=["D:h"], writes=[(k(xc), 0)])
                fw.dma("sp", xc[:n, 1, :], ks, reads=["D:h"], writes=[(k(xc), 1)])
                fw.dma("sp", xc[:n, 2, :], vs, reads=["D:h"], writes=[(k(xc), 2)])
                fw.op("pool", lambda e_, va=va: e_.memset(va[:], 1.0), writes=[k(va)])
                fw.op("dve", lambda e_, xc=xc, qb=qb: e_.tensor_copy(out=qb[:n, 0:2, :], in_=xc[:n, 0:2, :]), reads=[(k(xc), 0), (k(xc), 1)], writes=[(k(qb), 0)])
                fw.op("pool", lambda e_, xc=xc, va=va: e_.tensor_copy(out=va[:n, 0, :, 0:64], in_=xc[:n, 2, :].rearrange("p (h c) -> p h c", h=4)), reads=[(k(xc), 2)], writes=[k(va)])
                if hasp:
                    fw.dma("act", xp[:npv, 0, :], pks, reads=[pkey], writes=[(k(xp), 0)])
                    fw.dma("act", xp[:npv, 1, :], pvs, reads=[pkey], writes=[(k(xp), 1)])
                    fw.op("dve", lambda e_, xp=xp, qb=qb: e_.tensor_copy(out=qb[:npv, 2, :], in_=xp[:npv, 0, :]), reads=[(k(xp), 0)], writes=[(k(qb), 1)])
                    fw.op("pool", lambda e_, xp=xp, va=va: e_.tensor_copy(out=va[:npv, 1, :, 0:64], in_=xp[:npv, 1, :].rearrange("p (h c) -> p h c", h=4)), reads=[(k(xp), 1)], writes=[k(va)])
                stage = int(os.environ.get("DWA_STAGE", "9"))
                if stage < 2:
                    continue
                for w_, (nn, rk) in enumerate(((n, 0), (n, 0), (npv, 1))):
                    if w_ == 2 and not hasp:
                        continue
                    pp_ = pt.next()
                    for b_ in range(4):
                        fw.op("pe", lambda e_, w_=w_, b_=b_, pp_=pp_, qb=qb, nn=nn: e_.transpose(out=pp_[0:64, b_, :nn], in_=qb[:nn, w_, b_ * 64:(b_ + 1) * 64], identity=self.idb[:nn, :nn]),
                              reads=[(k(qb), rk), "idb"], writes=[k(pp_)])
                    fw.op("act" if w_ % 2 else "dve", (lambda e_, w_=w_, pp_=pp_, t=t, nn=nn: e_.copy(out=t[:, w_, :, :nn], in_=pp_[0:64, :, :nn])) if w_ % 2 else
                          (lambda e_, w_=w_, pp_=pp_, t=t, nn=nn: e_.tensor_copy(out=t[:, w_, :, :nn], in_=pp_[0:64, :, :nn])), reads=[k(pp_)], writes=[(k(t), w_)])
                if stage < 3:
                    continue
                S = pS.next()
                for h in range(4):
                    fw.op("pe", lambda e_, h=h, S=S, t=t: e_.matmul(S[:n, h, 0, :n], lhsT=t[:, 1, h, :n], rhs=t[:, 0, h, :n], start=True, stop=True),
                          reads=[(k(t), 0), (k(t), 1)], writes=[k(S)])
                    if hasp:
                        fw.op("pe", lambda e_, h=h, S=S, t=t: e_.matmul(S[:npv, h, 1, :n], lhsT=t[:, 2, h, :npv], rhs=t[:, 0, h, :n], start=True, stop=True),
                              reads=[(k(t), 0), (k(t), 2)], writes=[k(S)])
                if stage < 4:
                    continue
                fw.op("act", lambda e_, S=S, p=p: e_.activation(out=p[:n, :, 0, :n], in_=S[:n, :, 0, :n], func=AF.Exp, scale=0.125), reads=[k(S)], writes=[(k(p), 0)])
                fw.op("pool", lambda e_, p=p: e_.tensor_tensor(out=p[:n, :, 0, :n], in0=p[:n, :, 0, :n], in1=self.maskb[:n, 0, :n].unsqueeze(1).to_broadcast([n, 4, n]), op=ALU.mult),
                      reads=[(k(p), 0), "maskb"], writes=[(k(p), 0)])
                if hasp:
                    fw.op("act", lambda e_, S=S, p=p: e_.activation(out=p[:npv, :, 1, :n], in_=S[:npv, :, 1, :n], func=AF.Exp, scale=0.125), reads=[k(S)], writes=[(k(p), 1)])
                    fw.op("dve", lambda e_, p=p: e_.tensor_tensor(out=p[:npv, :, 1, :n], in0=p[:npv, :, 1, :n], in1=self.maskb[:npv, 1, :n].unsqueeze(1).to_broadcast([npv, 4, n]), op=ALU.mult),
                          reads=[(k(p), 1), "maskb"], writes=[(k(p), 1)])
                def part2(n=n, npv=npv, hasp=hasp, p=p, va=va, os_=os_, dst=dst):
                    O = pO.next()
                    for h in range(4):
                        fw.op("pe", lambda e_, h=h, O=O: e_.matmul(O[:n, h, 0:65], lhsT=p[:n, h, 0, :n], rhs=va[:n, 0, h, :], start=True, stop=not hasp),
                              reads=[(k(p), 0), k(va)], writes=[k(O)])
                        if hasp:
                            fw.op("pe", lambda e_, h=h, O=O: e_.matmul(O[:n, h, 0:65], lhsT=p[:npv, h, 1, :n], rhs=va[:npv, 1, h, :], start=False, stop=True),
                                  reads=[(k(p), 1), k(va)], writes=[k(O)])
                    fw.op("act", lambda e_, O=O: e_.copy(out=os_[:n, :].rearrange("p (h c) -> p h c", h=4), in_=O[:n, :, 0:65]), reads=[k(O)], writes=[k(os_)])
                    fw.dma("pool", dst, os_[:n, :], reads=[k(os_)], writes=["D:dwa"])
                if pend2[0] is not None:
                    pend2[0]()
                pend2[0] = part2
            if pend2[0] is not None:
                pend2[0]()
        with self.phase() as es:
            A = self.sbn(es, "dcA", [128, 3, 260], F32, 2)
            G = self.sbn(es, "dcG", [128, 256], F32, 2)
            rl = self.sbn(es, "dcrl", [128, 4], F32, 2)
            Y = self.sbn(es, "dcY", [128, 256], F32, 2)
            U = self.sbn(es, "dcU", [128, 256], BF16, 2)
            for (r0, n) in self.tiles:
                a, g_, r_, y_, u_ = A.next(), G.next(), rl.next(), Y.next(), U.next()
                fw.dma("sp", a[:n], self.dwa_d[:, r0:r0 + n, :].rearrange("g p c -> p g c"), reads=["D:dwa"], writes=[k(a)])
                fw.dma("sp", g_[:n, :], self.h_d[r0:r0 + n, GB:GB + 256], reads=["D:h"], writes=[k(g_)])
                fw.op("dve", lambda e_, a=a: e_.tensor_tensor(out=a[:n, 0, :], in0=a[:n, 0, :], in1=a[:n, 1, :], op=ALU.add), reads=[k(a)], writes=[k(a)])
                fw.op("dve", lambda e_, a=a: e_.tensor_tensor(out=a[:n, 0, :], in0=a[:n, 0, :], in1=a[:n, 2, :], op=ALU.add), reads=[k(a)], writes=[k(a)])
                a4 = a[:n, 0, :].rearrange("p (h c) -> p h c", h=4)
                fw.op("dve", lambda e_, a4=a4, r_=r_: e_.reciprocal(out=r_[:n, :], in_=a4[:, :, 64]), reads=[k(a)], writes=[k(r_)])
                fw.op("act", lambda e_, g_=g_: e_.activation(out=g_[:n, :], in_=g_[:n, :], func=AF.Silu), reads=[k(g_)], writes=[k(g_)])
                fw.op("dve", lambda e_, a4=a4, r_=r_, y_=y_: e_.tensor_tensor(out=y_[:n, :].rearrange("p (h c) -> p h c", h=4), in0=a4[:, :, 0:64], in1=r_[:n, :].unsqueeze(2).to_broadcast([n, 4, 64]), op=ALU.mult),
                      reads=[k(a), k(r_)], writes=[k(y_)])
                fw.op("pool", lambda e_, y_=y_, g_=g_, u_=u_: e_.tensor_tensor(out=u_[:n, :], in0=y_[:n, :], in1=g_[:n, :], op=ALU.mult), reads=[k(y_), k(g_)], writes=[k(u_)])
                fw.dma("pool", self.u_d[r0:r0 + n, A_W:A_W + 256], u_[:n, :], reads=[k(u_)], writes=["D:u"])

    def rwkv(self, e):
        fw, k = self.fw, self.key
        SEQ = self.SEQ
        NEG = -float(np.exp(-0.5))
        with self.phase() as es:
            def bc(name, src, w):
                t = self.sb(es, name, [128, w], F32)
                fw.dma("sp", t[:], src.partition_broadcast(128), writes=[name])
                return t
            mu = bc("rwmu", self.rw["mu"][e], A_SHIFT)
            w0 = bc("rww0", self.rw["w0"][e], A_W)
            a0 = bc("rwa0", self.rw["a0"][e], A_W)
            k_k = bc("rwkk", self.rw["k_k"][e], A_W)
            k_a = bc("rwka", self.rw["k_a"][e], A_W)
            r_k = bc("rwrk", self.rw["r_k"][e], A_W)
            lg = bc("rwlg", self.rw["lnx_g"][e], A_W)
            lb = bc("rwlb", self.rw["lnx_b"][e], A_W)
            wup = self.sb(es, "rwwup", [64, 2, A_W], F32)
            fw.dma("sp", wup[:, 0, :], self.rw["w_up"][e], writes=["rwwup"])
            fw.dma("sp", wup[:, 1, :], self.rw["a_up"][e], writes=["rwwup"])
            ones = self.sb(es, "rwones", [128, 1], F32)
            fw.op("pool", lambda e_: e_.memset(ones[:], 1.0), writes=["rwones"])
            H = self.sbn(es, "rwH", [128, A_SHIFT], F32, 1)
            hs = self.sb(es, "rwhs", [128, A_SHIFT], F32)
            lT = self.sb(es, "rwlT", [64, 2, 128], F32)
            sw = self.sb(es, "rwsw", [128, A_W], F32)
            av = self.sb(es, "rwa", [128, A_W], F32)
            kk = self.sb(es, "rwkkv", [128, A_W], F32)
            tmp = self.sb(es, "rwtmp", [128, A_W], F32)
            tmp2 = self.sb(es, "rwtmp2", [128, A_W], F32)
            s12p = self.sb(es, "rws12p", [128, 2, 12], F32)
            kmod = self.sb(es, "rwkmod", [128, A_W], F32)
            cs = self.sb(es, "rwcs", [128, A_W], F32)
            E3 = self.sb(es, "rwE3", [128, 3, A_W], BF16)
            raw = self.sb(es, "rwraw", [128, 12, 128], BF16)
            Pm = {nm: self.sb(es, "rwM" + nm, [128, 12, 128], BF16) for nm in ("P", "PT", "Pn", "PTn")}
            setsA, setsB = [], []
            for si in range(3):
                d = {}
                d["X4"] = self.sb(es, f"rwX4{si}", [128, 4, A_W], BF16)
                d["vb"] = self.sb(es, f"rwvb{si}", [128, A_W], BF16)
                d["gC"] = self.sb(es, f"rwgC{si}", [64, 12], F32)
                d["bon"] = self.sb(es, f"rwbon{si}", [128, A_W], F32)
                d["sg"] = self.sb(es, f"rwsg{si}", [128, A_W], F32)
                setsA.append(d)
            for si in range(2):
                d = {}
                d["XT"] = self.sb(es, f"rwXT{si}", [64, 4, 12, 128], BF16)
                for nm in ("AkT", "BbT", "BkT", "WT"):
                    d[nm] = self.sb(es, f"rwM{nm}{si}", [128, 12, 128], BF16)
                setsB.append(d)
            RH = self.sb(es, "rwRH", [128, 12, 64], BF16)
            Un = self.sb(es, "rwUn", [128, 12, 64], BF16)
            Z = self.sb(es, "rwZ", [64, 12, 64], F32)
            Zb = self.sb(es, "rwZb", [64, 12, 64], BF16)
            Y = self.sb(es, "rwY", [128, A_W], F32)
            Y2 = self.sb(es, "rwY2", [128, A_W], F32)
            Ssb = Y2[0:64, :].rearrange("p (h c) -> p h c", h=12)
            s12 = self.sb(es, "rws12", [128, 3, 12], F32)
            U = self.sbn(es, "rwU", [128, A_W], BF16, 2)
            print("rwkv sbuf remaining", self.nc.sbuf_bytes_remaining)
            pb = self.psn(es, "rwpb", [128, 512], F32, 8)

            def tt(en, out, in0, in1, op, reads, writes):
                fw.op(en, lambda e_: e_.tensor_tensor(out=out, in0=in0, in1=in1, op=op), reads=reads, writes=writes)

            def acopy(dst, src, reads, writes, scale=None):
                if scale is None:
                    fw.op("act", lambda e_: e_.copy(out=dst, in_=src), reads=reads, writes=writes)
                else:
                    fw.op("act", lambda e_: e_.mul(out=dst, in_=src, mul=float(scale)), reads=reads, writes=writes)

            def dcopy(dst, src, reads, writes):
                fw.op("dve", lambda e_: e_.tensor_copy(out=dst, in_=src), reads=reads, writes=writes)

            v3 = lambda t_: t_.rearrange("p (h c) -> p h c", h=12)
            allk = lambda nm: [nm, (nm, 0), (nm, 1), (nm, 2)]

            def bank6(n_, bf=False):
                B = pb.next()
                if bf:
                    return B, B[:n_, :].bitcast(BF16)[:, 0:512].rearrange("p (h c) -> p h c", h=4)
                return B, B[:n_, 0:512].rearrange("p (h c) -> p h c", h=4)

            def stageA(ci, r0, n, seg, S):
                X4, gC, vb, bon, sg = S["X4"], S["gC"], S["vb"], S["bon"], S["sg"]
                kX4 = k(X4)
                h_ = H.next()
                fw.dma("sp", h_[:n, :], self.h_d[r0:r0 + n, 0:A_SHIFT], reads=["D:h"], writes=[k(h_)])
                fw.dma("sp", sg[:n, :], self.h_d[r0:r0 + n, A_SHIFT:3200], reads=["D:h"], writes=[k(sg)])
                if ci == 0:
                    if seg == 0:
                        fw.op("pool", lambda e_: e_.memset(hs[0:1, :], 0.0), writes=["rwhs"])
                    else:
                        fw.dma("act", hs[0:1, :], self.st_shift[e:e + 1, :], writes=["rwhs"])
                    if n > 1:
                        fw.dma("act", hs[1:n, :], self.h_d[r0:r0 + n - 1, 0:A_SHIFT], reads=["D:h"], writes=[("rwhs", 1)])
                else:
                    fw.dma("act", hs[:n, :], self.h_d[r0 - 1:r0 + n - 1, 0:A_SHIFT], reads=["D:h"], writes=["rwhs", ("rwhs", 1)])
                hpk = ["rwhs", ("rwhs", 1)]
                tt("dve", hs[:n, :], hs[:n, :], h_[:n, 0:A_SHIFT], ALU.subtract, hpk + [k(h_)], hpk)
                tt("pool", hs[:n, :], hs[:n, :], mu[:n, :], ALU.mult, hpk + ["rwmu"], hpk)
                tt("dve", hs[:n, :], hs[:n, :], h_[:n, 0:A_SHIFT], ALU.add, hpk + [k(h_)], hpk)
                r_, k_, v_ = hs[:n, 0:768], hs[:n, 768:1536], hs[:n, 1536:2304]
                fw.op("act", lambda e_: e_.activation(out=sg[:n, :], in_=sg[:n, :], func=AF.Silu), reads=[k(sg)], writes=[k(sg)])
                fw.op("act", lambda e_: e_.activation(out=hs[:n, 2304:2368], in_=hs[:n, 2304:2368], func=AF.Tanh), reads=["rwhs"], writes=["rwhs"])
                acopy(vb[:n, :], v_, ["rwhs"], [k(vb)])
                yield
                B = pb.next()
                for j in range(2):
                    fw.op("pe", lambda e_, j=j, B=B: e_.transpose(out=B[:64, j * 128:j * 128 + n], in_=hs[:n, 2304 + j * 64:2368 + j * 64], identity=self.idf[:n, :n]), reads=["rwhs", "idf"], writes=[k(B)])
                dcopy(lT[:, :, :n], B[:64, 0:256].rearrange("p (a c) -> p a c", a=2)[:, :, :n], [k(B)], ["rwlT"])
                yield
                lb_ = {}
                for j in range(2):
                    for hf in range(2):
                        B = pb.next()
                        lb_[(j, hf)] = B
                        fw.op("pe", lambda e_, j=j, hf=hf, B=B: e_.matmul(B[:n, 0:384], lhsT=lT[:, j, :n], rhs=wup[:, j, hf * 384:(hf + 1) * 384], start=True, stop=True), reads=["rwlT", "rwwup"], writes=[k(B)])
                for j, (dst, off, dk_, ok_) in enumerate(((sw, w0, "rwsw", "rww0"), (av, a0, "rwa", "rwa0"))):
                    for hf in range(2):
                        B = lb_[(j, hf)]
                        tt("dve", dst[:n, hf * 384:(hf + 1) * 384], B[:n, 0:384], off[:n, hf * 384:(hf + 1) * 384], ALU.add, [k(B), ok_], [dk_])
                yield
                fw.op("act", lambda e_: e_.activation(out=sw[:n, :], in_=sw[:n, :], func=AF.Sigmoid), reads=["rwsw"], writes=["rwsw"])
                fw.op("act", lambda e_: e_.activation(out=av[:n, :], in_=av[:n, :], func=AF.Sigmoid), reads=["rwa"], writes=["rwa"])
                yield
                tt("dve", kk[:n, :], k_, k_k[:n, :], ALU.mult, ["rwhs", "rwkk"], ["rwkkv"])
                tt("pool", tmp[:n, :], kk[:n, :], kk[:n, :], ALU.mult, ["rwkkv"], ["rwtmp"])
                fw.op("dve", lambda e_: e_.tensor_reduce(out=s12p[:n, 0, :], in_=v3(tmp[:n, :]), axis=AX.X, op=ALU.add), reads=["rwtmp"], writes=["rws12p"])
                fw.op("dve", lambda e_: e_.tensor_scalar_max(out=s12p[:n, 0, :], in0=s12p[:n, 0, :], scalar1=1e-24), reads=["rws12p"], writes=["rws12p"])
                yield
                fw.op("act", lambda e_: e_.sqrt(out=s12p[:n, 0, :], in_=s12p[:n, 0, :]), reads=["rws12p"], writes=["rws12p"])
                fw.op("dve", lambda e_: e_.reciprocal(out=s12p[:n, 0, :], in_=s12p[:n, 0, :]), reads=["rws12p"], writes=["rws12p"])
                tt("dve", v3(kk[:n, :]), v3(kk[:n, :]), s12p[:n, 0, :].unsqueeze(2).to_broadcast([n, 12, 64]), ALU.mult, ["rwkkv", "rws12p"], ["rwkkv"])
                fw.op("dve", lambda e_: e_.scalar_tensor_tensor(out=tmp[:n, :], in0=av[:n, :], scalar=-1.0, in1=k_a[:n, :], op0=ALU.add, op1=ALU.mult), reads=["rwa", "rwka"], writes=["rwtmp"])
                fw.op("dve", lambda e_: e_.scalar_tensor_tensor(out=kmod[:n, :], in0=tmp[:n, :], scalar=1.0, in1=k_, op0=ALU.add, op1=ALU.mult), reads=["rwtmp", "rwhs"], writes=["rwkmod"])
                tt("pool", tmp2[:n, :], r_, kmod[:n, :], ALU.mult, ["rwhs", "rwkmod"], ["rwtmp2"])
                tt("pool", tmp2[:n, :], tmp2[:n, :], r_k[:n, :], ALU.mult, ["rwtmp2", "rwrk"], ["rwtmp2"])
                fw.op("dve", lambda e_: e_.tensor_reduce(out=s12p[:n, 1, :], in_=v3(tmp2[:n, :]), axis=AX.X, op=ALU.add), reads=["rwtmp2"], writes=["rws12p"])
                tt("pool", v3(bon[:n, :]), v3(v_), s12p[:n, 1, :].unsqueeze(2).to_broadcast([n, 12, 64]), ALU.mult, ["rwhs", "rws12p"], [k(bon)])
                yield
                for hf in range(2):
                    B = pb.next()
                    fw.op("pe", lambda e_, hf=hf, B=B: e_.matmul(B[:n, 0:384], lhsT=self.maskf[:n, 0, :n], rhs=sw[:n, hf * 384:(hf + 1) * 384], start=True, stop=True), reads=["maskf", "rwsw"], writes=[k(B)])
                    if hf:
                        acopy(cs[:n, hf * 384:(hf + 1) * 384], B[:n, 0:384], [k(B)], [("rwcs", hf)])
                    else:
                        dcopy(cs[:n, hf * 384:(hf + 1) * 384], B[:n, 0:384], [k(B)], [("rwcs", hf)])
                csk = [("rwcs", 0), ("rwcs", 1)]
                yield
                fw.op("act", lambda e_: e_.activation(out=E3[:n, 0, :], in_=cs[:n, :], func=AF.Exp, scale=NEG), reads=csk, writes=[("rwE3", 0)])
                fw.op("act", lambda e_: e_.activation(out=E3[:n, 1, :], in_=cs[:n, :], func=AF.Exp, scale=-NEG), reads=csk, writes=[("rwE3", 1)])
                tt("dve", tmp[:n, :], cs[:n, :], sw[:n, :], ALU.subtract, csk + ["rwsw"], ["rwtmp"])
                fw.op("act", lambda e_: e_.activation(out=E3[:n, 2, :], in_=tmp[:n, :], func=AF.Exp, scale=NEG), reads=["rwtmp"], writes=[("rwE3", 2)])
                B = pb.next()
                for h in range(12):
                    fw.op("pe", lambda e_, h=h, B=B: e_.matmul(B[:64, h:h + 1], lhsT=sw[:n, h * 64:(h + 1) * 64], rhs=ones[:n, 0:1], start=True, stop=True), reads=["rwsw", "rwones"], writes=[k(B)])
                fw.op("act", lambda e_, B=B: e_.activation(out=gC[:, :], in_=B[:64, 0:12], func=AF.Exp, scale=NEG), reads=[k(B)], writes=[k(gC)])
                yield
                tt("dve", X4[:n, 0, :], kk[:n, :], E3[:n, 2, :], ALU.mult, ["rwkkv", ("rwE3", 2)], [(kX4, 0)])
                tt("pool", X4[:n, 1, :], r_, E3[:n, 0, :], ALU.mult, ["rwhs", ("rwE3", 0)], [(kX4, 1)])
                tt("dve", tmp[:n, :], kk[:n, :], av[:n, :], ALU.mult, ["rwkkv", "rwa"], ["rwtmp"])
                tt("pool", X4[:n, 2, :], tmp[:n, :], E3[:n, 1, :], ALU.mult, ["rwtmp", ("rwE3", 1)], [(kX4, 2)])
                tt("dve", X4[:n, 3, :], kmod[:n, :], E3[:n, 1, :], ALU.mult, ["rwkmod", ("rwE3", 1)], [(kX4, 3)])
                yield

            def stageB(n, nlev, SA, S):
                X4, XT = SA["X4"], S["XT"]
                kX4, kXT = k(X4), k(XT)
                for wi_, w_ in enumerate((2, 0, 3, 1)):
                    for hg in range(3):
                        B, Bv = bank6(64, bf=True)
                        for hh in range(4):
                            h = hg * 4 + hh
                            fw.op("pe", lambda e_, w_=w_, h=h, hh=hh, Bv=Bv: e_.transpose(out=Bv[:, hh, :n], in_=X4[:n, w_, h * 64:(h + 1) * 64], identity=self.idb[:n, :n]), reads=[(kX4, w_), "idb"], writes=[k(B)])
                        if (w_ + hg) % 2:
                            acopy(XT[:, w_, hg * 4:(hg + 1) * 4, :n], Bv[:, :, :n], [k(B)], [(kXT, w_)])
                        else:
                            dcopy(XT[:, w_, hg * 4:(hg + 1) * 4, :n], Bv[:, :, :n], [k(B)], [(kXT, w_)])
                    if wi_ % 2:
                        yield

                def prod(wl, wr, midx, sign, dst, dkey):
                    for hg in range(3):
                        B, Bv = bank6(n)
                        for hh in range(4):
                            h = hg * 4 + hh
                            fw.op("pe", lambda e_, h=h, hh=hh, Bv=Bv: e_.matmul(Bv[:, hh, :n], lhsT=XT[:, wl, h, :n], rhs=XT[:, wr, h, :n], start=True, stop=True), reads=[(kXT, wl), (kXT, wr)], writes=[k(B)])
                        hr = slice(hg * 4, (hg + 1) * 4)
                        acopy(raw[:n, hr, :n], Bv[:, :, :n], [k(B)], [("rwraw", hg)], scale=(sign if sign != 1.0 else None))
                        tt("pool", dst[:n, hr, :n], raw[:n, hr, :n], self.maskb[:n, midx, :n].unsqueeze(1).to_broadcast([n, 4, n]), ALU.mult, [("rwraw", hg), "maskb"], [dkey, (dkey, hg)])
                prod(2, 0, 2, -1.0, Pm["PT"], "rwMPT")
                prod(0, 2, 3, -1.0, Pm["P"], "rwMP")
                yield
                prod(3, 0, 2, 1.0, S["AkT"], k(S["AkT"]))
                prod(2, 1, 0, 1.0, S["BbT"], k(S["BbT"]))
                yield
                prod(3, 1, 0, 1.0, S["BkT"], k(S["BkT"]))
                WT, kWT = S["WT"], k(S["WT"])
                tt("dve", WT[:n, :, :n], Pm["PT"][:n, :, :n], self.idf[:n, :n].unsqueeze(1).to_broadcast([n, 12, n]), ALU.add, ["rwMPT", "idf"], allk(kWT))
                yield
                P, PT, Pn, PTn = "P", "PT", "Pn", "PTn"
                for lev in range(1, nlev):
                    for hg in range(3):
                        hr = slice(hg * 4, (hg + 1) * 4)
                        B, Bv = bank6(n)
                        for hh in range(4):
                            h = hg * 4 + hh
                            fw.op("pe", lambda e_, h=h, hh=hh, Bv=Bv, P=P, PT=PT: e_.matmul(Bv[:, hh, :n], lhsT=Pm[PT][:n, h, :n], rhs=Pm[P][:n, h, :n], start=True, stop=True), reads=allk("rwM" + P) + allk("rwM" + PT), writes=[k(B)])
                        acopy(Pm[Pn][:n, hr, :n], Bv[:, :, :n], [k(B)], [("rwM" + Pn, hg)])
                    yield
                    if lev < nlev - 1:
                        for hg in range(3):
                            hr = slice(hg * 4, (hg + 1) * 4)
                            B, Bv = bank6(n)
                            for hh in range(4):
                                h = hg * 4 + hh
                                fw.op("pe", lambda e_, h=h, hh=hh, Bv=Bv, P=P, PT=PT: e_.matmul(Bv[:, hh, :n], lhsT=Pm[P][:n, h, :n], rhs=Pm[PT][:n, h, :n], start=True, stop=True), reads=allk("rwM" + P) + allk("rwM" + PT), writes=[k(B)])
                            acopy(Pm[PTn][:n, hr, :n], Bv[:, :, :n], [k(B)], [("rwM" + PTn, hg)])
                        yield
                    for hg in range(3):
                        hr = slice(hg * 4, (hg + 1) * 4)
                        B, Bv = bank6(n)
                        for hh in range(4):
                            h = hg * 4 + hh
                            fw.op("pe", lambda e_, h=h, hh=hh, Bv=Bv, Pn=Pn: e_.matmul(Bv[:, hh, :n], lhsT=Pm[Pn][:n, h, :n], rhs=WT[:n, h, :n], start=True, stop=True), reads=[("rwM" + Pn, hg), (kWT, hg)], writes=[k(B)])
                        tt("dve", WT[:n, hr, :n], WT[:n, hr, :n], Bv[:, :, :n], ALU.add, [(kWT, hg), k(B)], [(kWT, hg)])
                    yield
                    P, Pn = Pn, P
                    PT, PTn = PTn, PT

            def solve(r0, n, SA, S):
                X4, XT, gC, vb, bon, sg = SA["X4"], S["XT"], SA["gC"], SA["vb"], SA["bon"], SA["sg"]
                kX4, kXT = k(X4), k(XT)
                AkT, BbT, BkT, WT = S["AkT"], S["BbT"], S["BkT"], S["WT"]
                for hg in range(3):
                    hr = slice(hg * 4, (hg + 1) * 4)
                    B, Bv = bank6(n)
                    for hh in range(4):
                        h = hg * 4 + hh
                        fw.op("pe", lambda e_, h=h, hh=hh, Bv=Bv: e_.matmul(Bv[:, hh, 0:64], lhsT=XT[:, 0, h, :n], rhs=Zb[:, h, :], start=True, stop=False), reads=[(kXT, 0), "rwZb"], writes=[k(B)])
                        fw.op("pe", lambda e_, h=h, hh=hh, Bv=Bv: e_.matmul(Bv[:, hh, 0:64], lhsT=AkT[:n, h, :n], rhs=vb[:n, h * 64:(h + 1) * 64], start=False, stop=True), reads=[k(AkT), k(vb)], writes=[k(B)])
                    if hg % 2:
                        acopy(RH[:n, hr, :], Bv[:, :, 0:64], [k(B)], [("rwRH", hg)])
                    else:
                        dcopy(RH[:n, hr, :], Bv[:, :, 0:64], [k(B)], [("rwRH", hg)])
                    yield
                for hg in range(3):
                    hr = slice(hg * 4, (hg + 1) * 4)
                    B, Bv = bank6(n)
                    for hh in range(4):
                        h = hg * 4 + hh
                        fw.op("pe", lambda e_, h=h, hh=hh, Bv=Bv: e_.matmul(Bv[:, hh, 0:64], lhsT=WT[:n, h, :n], rhs=RH[:n, h, :], start=True, stop=True), reads=[k(WT), (k(WT), hg), ("rwRH", hg)], writes=[k(B)])
                    acopy(Un[:n, hr, :], Bv[:, :, 0:64], [k(B)], [("rwUn", hg)], scale=-1.0)
                    yield
                for hg in range(3):
                    hr = slice(hg * 4, (hg + 1) * 4)
                    B = pb.next()
                    Bv = B[:64, 0:512].rearrange("p (h c) -> p h c", h=4)[:, :, 0:64]
                    B2, Bv2 = bank6(n)
                    for hh in range(4):
                        h = hg * 4 + hh
                        fw.op("pe", lambda e_, h=h, hh=hh, Bv2=Bv2: e_.matmul(Bv2[:, hh, 0:64], lhsT=XT[:, 1, h, :n], rhs=Zb[:, h, :], start=True, stop=False), reads=[(kXT, 1), "rwZb"], writes=[k(B2)])
                        fw.op("pe", lambda e_, h=h, hh=hh, Bv2=Bv2: e_.matmul(Bv2[:, hh, 0:64], lhsT=BbT[:n, h, :n], rhs=Un[:n, h, :], start=False, stop=False), reads=[k(BbT), ("rwUn", hg)], writes=[k(B2)])
                        fw.op("pe", lambda e_, h=h, hh=hh, Bv2=Bv2: e_.matmul(Bv2[:, hh, 0:64], lhsT=BkT[:n, h, :n], rhs=vb[:n, h * 64:(h + 1) * 64], start=False, stop=True), reads=[k(BkT), k(vb)], writes=[k(B2)])
                    for hh in range(4):
                        h = hg * 4 + hh
                        fw.op("pe", lambda e_, h=h, hh=hh, Bv=Bv: e_.matmul(Bv[:, hh, 0:64], lhsT=X4[:n, 2, h * 64:(h + 1) * 64], rhs=Un[:n, h, :], start=True, stop=False), reads=[(kX4, 2), ("rwUn", hg)], writes=[k(B)])
                        fw.op("pe", lambda e_, h=h, hh=hh, Bv=Bv: e_.matmul(Bv[:, hh, 0:64], lhsT=X4[:n, 3, h * 64:(h + 1) * 64], rhs=vb[:n, h * 64:(h + 1) * 64], start=False, stop=True), reads=[(kX4, 3), k(vb)], writes=[k(B)])
                    tt("dve", Z[:, hr, :], Z[:, hr, :], Bv, ALU.add, ["rwZ", k(B)], ["rwZ"])
                    tt("pool", Z[:, hr, :], Z[:, hr, :], gC[:, hr].unsqueeze(2).to_broadcast([64, 4, 64]), ALU.mult, ["rwZ", k(gC)], ["rwZ"])
                    acopy(Zb[:, hr, :], Z[:, hr, :], ["rwZ"], ["rwZb"])
                    dcopy(Y[:n, hg * 256:(hg + 1) * 256].rearrange("p (h c) -> p h c", h=4), Bv2[:, :, 0:64], [k(B2)], [("rwY", hg)])
                    yield
                yk = [("rwY", 0), ("rwY", 1), ("rwY", 2)]
                b12 = lambda col: s12[:n, col, :].unsqueeze(2).to_broadcast([n, 12, 64])
                fw.op("dve", lambda e_: e_.tensor_reduce(out=s12[:n, 0, :], in_=v3(Y[:n, :]), axis=AX.X, op=ALU.add), reads=yk, writes=["rws12"])
                tt("pool", Y2[:n, :], Y[:n, :], Y[:n, :], ALU.mult, yk, ["rwY2"])
                fw.op("dve", lambda e_: e_.tensor_reduce(out=s12[:n, 1, :], in_=v3(Y2[:n, :]), axis=AX.X, op=ALU.add), reads=["rwY2"], writes=["rws12"])
                fw.op("dve", lambda e_: e_.tensor_scalar_mul(out=s12[:n, 0, :], in0=s12[:n, 0, :], scalar1=1.0 / 64), reads=["rws12"], writes=["rws12"])
                tt("dve", s12[:n, 2, :], s12[:n, 0, :], s12[:n, 0, :], ALU.mult, ["rws12"], ["rws12"])
                fw.op("dve", lambda e_: e_.scalar_tensor_tensor(out=s12[:n, 1, :], in0=s12[:n, 1, :], scalar=1.0 / 64, in1=s12[:n, 2, :], op0=ALU.mult, op1=ALU.subtract), reads=["rws12"], writes=["rws12"])
                fw.op("dve", lambda e_: e_.tensor_scalar_add(out=s12[:n, 1, :], in0=s12[:n, 1, :], scalar1=64e-5), reads=["rws12"], writes=["rws12"])
                fw.op("act", lambda e_: e_.sqrt(out=s12[:n, 1, :], in_=s12[:n, 1, :]), reads=["rws12"], writes=["rws12"])
                fw.op("dve", lambda e_: e_.reciprocal(out=s12[:n, 1, :], in_=s12[:n, 1, :]), reads=["rws12"], writes=["rws12"])
                yield
                tt("dve", v3(Y[:n, :]), v3(Y[:n, :]), b12(0), ALU.subtract, yk + ["rws12"], yk)
                tt("dve", v3(Y[:n, :]), v3(Y[:n, :]), b12(1), ALU.mult, yk + ["rws12"], yk)
                tt("pool", Y[:n, :], Y[:n, :], lg[:n, :], ALU.mult, yk + ["rwlg"], yk)
                tt("pool", Y[:n, :], Y[:n, :], lb[:n, :], ALU.add, yk + ["rwlb"], yk)
                tt("dve", Y[:n, :], Y[:n, :], bon[:n, :], ALU.add, yk + [k(bon)], yk)
                u_ = U.next()
                tt("pool", u_[:n, :], Y[:n, :], sg[:n, :], ALU.mult, yk + [k(sg)], [k(u_)])
                fw.dma("pool", self.u_d[r0:r0 + n, 0:A_W], u_[:n, :], reads=[k(u_)], writes=["D:u"])
                yield

            def drive(gens, ratios):
                gens = list(gens)
                alive = [g is not None for g in gens]
                while any(alive):
                    for gi, g in enumerate(gens):
                        for _ in range(ratios[gi]):
                            if alive[gi]:
                                try:
                                    next(g)
                                except StopIteration:
                                    alive[gi] = False

            for seg in range(2):
                if seg == 0:
                    C = min(128, SEQ)
                    chunks = [(r, C) for r in range(0, SEQ, C)]
                    fw.op("pool", lambda e_: e_.memset(Z[:], 0.0), writes=["rwZ"])
                    Sout = self.o_rwkv_p[e]
                else:
                    C = TS
                    chunks = [(SEQ, TS)]
                    Sout = self.o_rwkv_s[e]
                    fw.dma("sp", Ssb[:], self.st_rwkv[e].rearrange("h i j -> i h j"), writes=["rwS"])
                    for hg in range(3):
                        B = pb.next()
                        Bv = B[:64, 0:512].rearrange("p (h c) -> p h c", h=4)[:, :, 0:64]
                        for hh in range(4):
                            fw.op("pe", lambda e_, hh=hh, hg=hg, Bv=Bv: e_.transpose(out=Bv[:, hh, :], in_=Ssb[:, hg * 4 + hh, :], identity=self.idf[:64, :64]), reads=["rwS", "idf"], writes=[k(B)])
                        dcopy(Z[:, hg * 4:(hg + 1) * 4, :], Bv, [k(B)], ["rwZ"])
                fw.op("dve", lambda e_: e_.tensor_copy(out=Zb[:], in_=Z[:]), reads=["rwZ"], writes=["rwZb"])
                nlev = int(np.log2(C))
                nch = len(chunks)
                gA = lambda ci: stageA(ci, chunks[ci][0], chunks[ci][1], seg, setsA[ci % 3]) if ci < nch else None
                gB = lambda ci: stageB(chunks[ci][1], nlev, setsA[ci % 3], setsB[ci % 2]) if ci < nch else None
                gS = lambda ci: solve(chunks[ci][0], chunks[ci][1], setsA[ci % 3], setsB[ci % 2]) if 0 <= ci < nch else None
                drive([gA(0)], [1])
                drive([gA(1), gB(0)], [2, 4])
                for ci in range(nch):
                    drive([gA(ci + 2), gB(ci + 1), gS(ci)], [2, 4, 2])
                for hg in range(3):
                    B = pb.next()
                    Bv = B[:64, 0:512].rearrange("p (h c) -> p h c", h=4)[:, :, 0:64]
                    for hh in range(4):
                        fw.op("pe", lambda e_, hh=hh, hg=hg, Bv=Bv: e_.transpose(out=Bv[:, hh, :], in_=Z[:, hg * 4 + hh, :], identity=self.idf[:64, :64]), reads=["rwZ", "idf"], writes=[k(B)])
                    dcopy(Ssb[:, hg * 4:(hg + 1) * 4, :], Bv, [k(B)], ["rwS"])
                fw.dma("pool", Sout.rearrange("h i j -> i h j"), Ssb[:], reads=["rwS"], writes=["D:orw"])

    def odd_mixers(self, o):
        fw, k = self.fw, self.key
        SEQ = self.SEQ
        lg = [float(np.log(1.0 - 2.0 ** (-5.0 - h))) for h in range(C_H)]
        with self.phase() as es:
            sc = self.sb(es, "rtsc", [128, 2, C_H], F32)
            fw.dma("sp", sc[:], self.c_retsc, writes=["rtsc"])
            R = self.sb(es, "rtR", [128, C_H, 2, 256], F32)
            Rb = self.sb(es, "rtRb", [128, C_H, 2, 256], BF16)
            qkvg = self.sbn(es, "rtin", [128, 4, C_W], F32, 2)
            rot = self.sbn(es, "rtrot", [128, 2, 256], F32, 2)
            t1 = self.sb(es, "rtt1", [128, 12, 256], F32)
            t2 = self.sb(es, "rtt2", [128, 12, 256], F32)
            qkb = self.sbn(es, "rtqkb", [128, 12, 256], BF16, 2)
            vb = self.sbn(es, "rtvb", [128, C_W], BF16, 2)
            qT = self.sbn(es, "rtqT", [128, 12, 128], BF16, 2)
            kT = self.sbn(es, "rtkT", [128, 12, 128], BF16, 2)
            Sb = self.sbn(es, "rtSb", [128, C_H, 128], BF16, 2)
            sq = self.sb(es, "rtsq", [128, C_W], F32)
            ss = self.sb(es, "rtss", [128, C_H], F32)
            sgr = self.sbn(es, "rtsg", [128, C_W], F32, 2)
            yt = self.sb(es, "rtyt", [128, C_W], F32)
            ub = self.sbn(es, "rtub", [128, C_W], BF16, 2)
            pt = self.psn(es, "rtpt", [128, 4, 128], BF16, 1)
            pS = self.psn(es, "rtpS", [128, C_H, 128], F32, 1)
            pO = self.psn(es, "rtpO", [128, C_H, 256], F32, 1)
            pR = self.psn(es, "rtpR", [128, 2, 256], F32, 2)
            for seg in range(2):
                if seg == 0:
                    fw.op("pool", lambda e: e.memset(R[:], 0.0), writes=["rtR"])
                    chunks = self.tiles[:-1]
                    Rout = self.o_ret_p[o]
                else:
                    fw.dma("sp", R[:], self.st_ret[o].rearrange("h (c p) e -> p h c e", p=128), writes=["rtR"])
                    chunks = self.tiles[-1:]
                    Rout = self.o_ret_s[o]
                fw.op("dve", lambda e: e.tensor_copy(out=Rb[:], in_=R[:]), reads=["rtR"], writes=["rtRb"])
                def prefix(r0, n, T):
                    X, rt, QK, V, QT, KT, S_, U, sg = T
                    fw.dma("sp", X[:n, 0:2, :], self.h_d[r0:r0 + n, 0:2 * C_W].rearrange("p (a c) -> p a c", a=2), reads=["D:h"], writes=[(k(X), 0)])
                    fw.dma("act", X[:n, 2:4, :], self.h_d[r0:r0 + n, 2 * C_W:4 * C_W].rearrange("p (a c) -> p a c", a=2), reads=["D:h"], writes=[(k(X), 1)])
                    fw.dma("sp", rt[:n], self.c_rot[r0:r0 + n], writes=[k(rt)])
                    x12 = X[:n, 0:2, :].rearrange("p a (h c) -> p (a h) c", h=C_H)
                    x12p = X[:n, 0:2, :].rearrange("p a (h c two) -> p (a h) c two", h=C_H, two=2)
                    t2p = t2[:n].rearrange("p a (c two) -> p a c two", two=2)
                    rtp = rt[:n].rearrange("p a (c two) -> p a c two", two=2)
                    fw.op("pool", lambda e, x12=x12, rt=rt: e.tensor_tensor(out=t1[:n], in0=x12, in1=rt[:n, 0, :].unsqueeze(1).to_broadcast([n, 12, 256]), op=ALU.mult),
                          reads=[(k(X), 0), k(rt)], writes=["rtt1"])
                    fw.op("dve", lambda e, x12p=x12p, t2p=t2p, rtp=rtp: e.tensor_tensor(out=t2p[:, :, :, 0], in0=x12p[:, :, :, 1], in1=rtp[:, 1, :, 0].unsqueeze(1).to_broadcast([n, 12, 128]), op=ALU.mult),
                          reads=[(k(X), 0), k(rt)], writes=[("rtt2", 0)])
                    fw.op("dve", lambda e, x12p=x12p, t2p=t2p, rtp=rtp: e.tensor_tensor(out=t2p[:, :, :, 1], in0=x12p[:, :, :, 0], in1=rtp[:, 1, :, 1].unsqueeze(1).to_broadcast([n, 12, 128]), op=ALU.mult),
                          reads=[(k(X), 0), k(rt)], writes=[("rtt2", 1)])
                    yield
                    fw.op("dve", lambda e: e.tensor_tensor(out=t1[:n], in0=t1[:n], in1=t2[:n], op=ALU.add), reads=["rtt1", ("rtt2", 0), ("rtt2", 1)], writes=["rtt1"])
                    fw.op("dve", lambda e, QK=QK: e.tensor_tensor(out=QK[:n], in0=t1[:n], in1=sc[:n].rearrange("p a h -> p (a h)").unsqueeze(2).to_broadcast([n, 12, 256]), op=ALU.mult),
                          reads=["rtt1", "rtsc"], writes=[k(QK)])
                    fw.op("act", lambda e, V=V, X=X: e.copy(out=V[:n, :], in_=X[:n, 2, :]), reads=[(k(X), 1)], writes=[k(V)])
                    qflat = QK[:, 0:6, :].rearrange("p h c -> p (h c)")
                    kflat = QK[:, 6:12, :].rearrange("p h c -> p (h c)")
                    yield
                    self.transposes(QT, qflat, 12, n, pt, k(QT), k(QK))
                    yield
                    self.transposes(KT, kflat, 12, n, pt, k(KT), k(QK))
                    yield
                    S = pS.next()
                    for h in range(C_H):
                        for c in range(2):
                            fw.op("pe", lambda e, h=h, c=c, S=S, KT=KT, QT=QT: e.matmul(S[:n, h, :n], lhsT=KT[:, h * 2 + c, :n], rhs=QT[:, h * 2 + c, :n], start=(c == 0), stop=(c == 1)),
                                  reads=[k(KT), k(QT)], writes=[k(S)])
                    fw.op("dve", lambda e, S=S, S_=S_: e.tensor_tensor(out=S_[:n, :, :n], in0=S[:n, :, :n], in1=self.maskf[:n, 0, :n].unsqueeze(1).to_broadcast([n, C_H, n]), op=ALU.mult),
                          reads=[k(S), "maskf"], writes=[k(S_)])
                    fw.op("act", lambda e, X=X, sg=sg: e.activation(out=sg[:n, :], in_=X[:n, 3, :], func=AF.Silu), reads=[(k(X), 1)], writes=[k(sg)])
                    yield

                def solve(r0, n, T):
                    X, rt, QK, V, QT, KT, S_, U, sg = T
                    O = pO.next()
                    for h in range(C_H):
                        fw.op("pe", lambda e, h=h, O=O, S_=S_, V=V: e.matmul(O[:n, h, :], lhsT=S_[:n, h, :n], rhs=V[:n, h * 256:(h + 1) * 256], start=True, stop=False),
                              reads=[k(S_), k(V)], writes=[k(O)])
                        for c in range(2):
                            fw.op("pe", lambda e, h=h, c=c, O=O, QT=QT: e.matmul(O[:n, h, :], lhsT=QT[:, h * 2 + c, :n], rhs=Rb[:, h, c, :], start=False, stop=(c == 1)),
                                  reads=[k(QT), "rtRb"], writes=[k(O)])
                    yield
                    for h in range(C_H):
                        dR = pR.next()
                        for c in range(2):
                            fw.op("pe", lambda e, h=h, c=c, dR=dR, QK=QK, V=V: e.matmul(dR[:, c, :], lhsT=QK[:n, 6 + h, c * 128:(c + 1) * 128], rhs=V[:n, h * 256:(h + 1) * 256], start=True, stop=True),
                                  reads=[k(QK), k(V)], writes=[k(dR)])
                        fw.op("dve", lambda e, h=h, dR=dR: e.tensor_tensor(out=R[:, h], in0=R[:, h], in1=dR[:], op=ALU.add), reads=[k(dR), "rtR"], writes=["rtR"])
                        gch = float(np.exp(lg[h] * n))
                        fw.op("act", lambda e, h=h, gch=gch: e.mul(out=Rb[:, h], in_=R[:, h], mul=gch), reads=["rtR"], writes=["rtRb"])
                        fw.op("act", lambda e, h=h, gch=gch: e.mul(out=R[:, h], in_=R[:, h], mul=gch), reads=["rtR"], writes=["rtR"])
                        if h % 2:
                            yield
                    yield
                    Of = O[:n].rearrange("p h c -> p (h c)")
                    fw.op("act", lambda e, Of=Of: e.activation(out=sq[:n, :], in_=Of, func=AF.Square), reads=[k(O)], writes=["rtsq"])
                    fw.op("dve", lambda e: e.tensor_reduce(out=ss[:n, :], in_=sq[:n, :].rearrange("p (h c) -> p h c", h=C_H), axis=AX.X, op=ALU.add), reads=["rtsq"], writes=["rtss"])
                    fw.op("dve", lambda e: e.tensor_scalar(out=ss[:n, :], in0=ss[:n, :], scalar1=1.0 / 256.0, scalar2=1e-6, op0=ALU.mult, op1=ALU.add), reads=["rtss"], writes=["rtss"])
                    fw.op("act", lambda e: e.sqrt(out=ss[:n, :], in_=ss[:n, :]), reads=["rtss"], writes=["rtss"])
                    fw.op("dve", lambda e: e.reciprocal(out=ss[:n, :], in_=ss[:n, :]), reads=["rtss"], writes=["rtss"])
                    fw.op("dve", lambda e, O=O: e.tensor_tensor(out=yt[:n, :].rearrange("p (h c) -> p h c", h=C_H), in0=O[:n], in1=ss[:n, :].unsqueeze(2).to_broadcast([n, C_H, 256]), op=ALU.mult),
                          reads=[k(O), "rtss"], writes=["rtyt"])
                    fw.op("pool", lambda e, U=U, sg=sg: e.tensor_tensor(out=U[:n, :], in0=yt[:n, :], in1=sg[:n, :], op=ALU.mult), reads=["rtyt", k(sg)], writes=[k(U)])
                    fw.dma("pool", self.u_d[r0:r0 + n, 0:C_W], U[:n, :], reads=[k(U)], writes=["D:u"])

                    yield

                def drive2(g1, g2):
                    al = [g1 is not None, g2 is not None]
                    gs = [g1, g2]
                    while any(al):
                        for gi in range(2):
                            if al[gi]:
                                try:
                                    next(gs[gi])
                                except StopIteration:
                                    al[gi] = False
                Ts = [(qkvg.next(), rot.next(), qkb.next(), vb.next(), qT.next(), kT.next(), Sb.next(), ub.next(), sgr.next()) for _ in chunks]
                drive2(prefix(chunks[0][0], chunks[0][1], Ts[0]), None)
                for ci in range(len(chunks)):
                    gp = prefix(chunks[ci + 1][0], chunks[ci + 1][1], Ts[ci + 1]) if ci + 1 < len(chunks) else None
                    drive2(gp, solve(chunks[ci][0], chunks[ci][1], Ts[ci]))
                fw.dma("pool", Rout.rearrange("h (c p) e -> p h c e", p=128), R[:], reads=["rtR"], writes=["D:oret"])


PAST_LEN = 16384


def _consts(SEQ):
    NT = SEQ + TS
    c = {}
    c["c_ident"] = np.eye(128, dtype=np.float32)
    pos = np.concatenate([np.arange(SEQ, dtype=np.float32), np.arange(TS, dtype=np.float32) + np.float32(PAST_LEN)])
    angle = (1.0 / (np.float32(10000.0) ** np.linspace(0.0, 1.0, 128, dtype=np.float32))).astype(np.float32)
    ph = (pos[:, None] * np.repeat(angle, 2)[None]).astype(np.float32)
    sgn = np.tile(np.array([-1.0, 1.0], np.float32), 128)
    c["c_rot"] = np.stack([np.cos(ph), np.sin(ph) * sgn[None]], axis=1).astype(np.float32)
    lg = np.log(1.0 - 2.0 ** (-5.0 - np.arange(C_H, dtype=np.float32))).astype(np.float32)
    idx = np.arange(128, dtype=np.float32)
    xi = np.exp(lg[None, :] * (idx[:, None] + 1.0))
    kz = np.exp(-lg[None, :] * (idx[:, None] + 1.0)) * (256.0 ** -0.5)
    c["c_retsc"] = np.stack([xi, kz], axis=1).astype(np.float32)
    j = np.arange(128)[:, None]
    i = np.arange(128)[None, :]
    c["c_masks"] = np.stack([(j <= i), (j >= i), (j < i), (j > i)], axis=1).astype(np.float32)
    return c


def make_in_maps(inp, SEQ, DEPTH, n_cores):
    LE, LO = (DEPTH + 1) // 2, DEPTH // 2
    cst = _consts(SEQ)
    f = lambda a: np.ascontiguousarray(np.asarray(a, dtype=np.float32))
    maps = []
    for c in range(n_cores):
        b = c % inp["x_prompt"].shape[0]
        m = {}
        m["x_in"] = f(np.concatenate([inp["x_prompt"][b, :SEQ], inp["x_sample"][c]], axis=0))
        m["memp"] = f(inp["mem_prompt"][b])
        m["st_rwkv"] = f(inp["state_rwkv"][:LE, c])
        m["st_shift"] = f(inp["state_rwkv_shift"][:LE, c])
        for g in range(3):
            a = inp[f"cache_dwa_g{g}"][:LE, c]
            m[f"c_g{g}"] = f(a.reshape(a.shape[0], a.shape[1], 512))
        m["st_ret"] = f(inp["state_ret"][:max(LO, 1), c])
        m["c_mem"] = f(inp["cache_mem_kv"][:DEPTH, c].reshape(DEPTH, 256, 1024))
        m["w_in_even"] = f(inp["w_in_even"][:LE])
        m["w_out_even"] = f(inp["w_out_even"][:LE])
        m["w_in_odd"] = f(inp["w_in_odd"][:max(LO, 1)])
        m["w_out_odd"] = f(inp["w_out_odd"][:max(LO, 1)])
        m["w_mem_kv"] = f(inp["w_mem_kv"][:DEPTH])
        m["ln_g"] = f(inp["ln_g"][:DEPTH])
        m["ln_b"] = f(inp["ln_b"][:DEPTH])
        for nm in ("mu", "w0", "w_up", "a0", "a_up", "k_k", "k_a", "lnx_g", "lnx_b"):
            m["rwkv_" + nm] = f(inp["rwkv_" + nm][:LE])
        m["rwkv_r_k"] = f(np.asarray(inp["rwkv_r_k"])[:LE].reshape(LE, A_W))
        m.update(cst)
        maps.append(m)
    return maps


_NC_CACHE = {}


def run(inp, SEQ, DEPTH, n_cores, debug=False, kinds=None):
    key = (SEQ, DEPTH, debug, str(kinds))
    if key not in _NC_CACHE:
        _NC_CACHE[key] = Builder(SEQ, DEPTH, debug=debug, kinds=kinds).build()
    nc = _NC_CACHE[key]
    maps = make_in_maps(inp, SEQ, DEPTH, n_cores)
    res = run_bass_kernel_spmd(nc, maps, core_ids=list(range(n_cores)))
    return res.results


def kernel(**inp):
    inp = {k: np.asarray(v) for k, v in inp.items()}
    SEQ, DEPTH = inp["x_prompt"].shape[1], 4
    BP = inp["x_prompt"].shape[0]
    NS = inp["x_sample"].shape[0]
    r = run(inp, SEQ, DEPTH, NS)
    LE, LO = 2, 2
    P = range(BP)
    S = range(NS)
    st = lambda name, cores: np.stack([r[c][name] for c in cores], axis=1)
    y_p = np.stack([r[b]["y"][:SEQ] for b in P], 0)
    y_s = np.stack([r[c]["y"][SEQ:] for c in S], 0)
    outs = [y_p, y_s, st("o_rwkv_p", P), st("o_rwkv_s", S), st("o_shift_p", P), st("o_shift_s", S)]
    for g in range(3):
        a = st(f"o_g{g}_p", P)
        outs.append(a.reshape(LE, BP, a.shape[2], 2, 4, 64))
        a = st(f"o_g{g}_s", S)
        outs.append(a.reshape(LE, NS, TS, 2, 4, 64))
    outs.append(st("o_ret_p", P))
    outs.append(st("o_ret_s", S))
    outs.append(st("o_mem_p", P).reshape(DEPTH, BP, 256, 2, 4, 128))
    return tuple(np.ascontiguousarray(o.astype(np.float32)) for o in outs)
```

```python
import contextlib
import os
import numpy as np
import concourse.bass as bass
import concourse.mybir as mybir
from concourse.bass_utils import run_bass_kernel_spmd

F32 = mybir.dt.float32
BF16 = mybir.dt.bfloat16
AF = mybir.ActivationFunctionType
ALU = mybir.AluOpType
AX = mybir.AxisListType

D = 2048
KC = D // 128
TS = 4
ALPHA_FULL_DEPTH = 4
LN_EPS = 1e-5
A_H, A_N = 12, 64
A_W = 768
A_SHIFT = 3 * A_W + 128
B_W = 768
B_OUT = 256
M_W = 512
C_H, C_D = 6, 256
C_W = 1536
EVEN_IN = A_SHIFT + A_W + 3 * B_W + B_OUT + 2 * M_W
EVEN_OUT = A_W + B_OUT + M_W
ODD_IN = 4 * C_W + 2 * M_W
ODD_OUT = C_W + M_W
B_CONFIGS = ((128, 1), (512, 4), (2048, 16))


class FW:
    def __init__(self, nc, n_dma_slots=14):
        self.nc = nc
        self.engs = {"pe": nc.tensor, "act": nc.scalar, "dve": nc.vector, "pool": nc.gpsimd, "sp": nc.sync}
        self.sem, self.cnt, self._ctx = {}, {}, []
        for e in ("pe", "act", "dve", "pool"):
            cm = nc.semaphore("s_" + e)
            self.sem[e] = cm.__enter__()
            self._ctx.append(cm)
            self.cnt[e] = 0
        self.slots = {}
        for q in ("sp", "act", "pool"):
            lst = []
            for i in range(n_dma_slots):
                cm = nc.semaphore(f"d_{q}{i}")
                s = cm.__enter__()
                self._ctx.append(cm)
                lst.append([s, 0])
            self.slots[q] = [lst, 0]
        self.known = {e: {} for e in self.engs}
        self.lastw, self.readers = {}, {}
        self.dw, self.dr = {}, {}
        self.n_inst = 0
        self.n_wait = 0

    @staticmethod
    def _isd(k):
        return isinstance(k, str) and k.startswith("D:") or (isinstance(k, tuple) and isinstance(k[0], str) and k[0].startswith("D:"))

    def _deps(self, reads, writes):
        deps = []
        for r in reads:
            if self._isd(r):
                deps.extend(self.dw.get(r, ()))
            elif r in self.lastw:
                deps.append(self.lastw[r])
        for w in writes:
            if self._isd(w):
                deps.extend(self.dr.get(w, ()))
            else:
                if w in self.lastw:
                    deps.append(self.lastw[w])
                deps.extend(self.readers.get(w, ()))
        return deps

    def _wait(self, e, deps, skip_sem=None):
        need = {}
        for (s, v) in deps:
            if skip_sem is not None and s is skip_sem:
                continue
            k = id(s)
            if k not in need or need[k][1] < v:
                need[k] = (s, v)
        kn = self.known[e]
        for k, (s, v) in need.items():
            if kn.get(k, 0) >= v:
                continue
            self.engs[e].wait_ge(s, v)
            self.n_wait += 1
            kn[k] = v

    def _record(self, tok, reads, writes):
        for r in reads:
            if self._isd(r):
                self.dr.setdefault(r, []).append(tok)
            else:
                self.readers.setdefault(r, []).append(tok)
        for w in writes:
            if self._isd(w):
                if self.dr.get(w):
                    self.dr[w] = []
                    self.dw[w] = []
                self.dw.setdefault(w, []).append(tok)
            else:
                self.lastw[w] = tok
                self.readers[w] = []

    def op(self, e, fn, reads=(), writes=()):
        deps = self._deps(reads, writes)
        self._wait(e, deps, skip_sem=self.sem["pe"] if e == "pe" else None)
        ins = fn(self.engs[e])
        self.cnt[e] += 1
        ins.then_inc(self.sem[e], 1)
        self.n_inst += 1
        tok = (self.sem[e], self.cnt[e])
        self._record(tok, reads, writes)
        return tok

    def dma(self, q, out, in_, reads=(), writes=()):
        deps = self._deps(reads, writes)
        lst, idx = self.slots[q]
        slot = lst[idx % len(lst)]
        self.slots[q][1] = idx + 1
        if slot[1] > 0:
            deps.append((slot[0], slot[1]))
        self._wait(q, deps)
        slot[1] += 16
        self.engs[q].dma_start(out=out, in_=in_).then_inc(slot[0], 16)
        self.n_inst += 1
        tok = (slot[0], slot[1])
        self._record(tok, reads, writes)
        return tok

    def all_toks(self):
        toks = []
        for e in ("pe", "act", "dve", "pool"):
            if self.cnt[e]:
                toks.append((self.sem[e], self.cnt[e]))
        for q in self.slots:
            for s, v in self.slots[q][0]:
                if v:
                    toks.append((s, v))
        return toks

    def barrier(self):
        if not hasattr(self, "bar"):
            cm = self.nc.semaphore("s_bar")
            self.bar = cm.__enter__()
            self._ctx.append(cm)
            self.barv = 0
        toks = self.all_toks()
        self._wait("sp", toks)
        self.barv += 1
        self.engs["sp"].sem_inc(self.bar, 1)
        for e in ("pe", "act", "dve", "pool"):
            self.engs[e].wait_ge(self.bar, self.barv)
            for (s_, v_) in toks:
                self.known[e][id(s_)] = max(self.known[e].get(id(s_), 0), v_)
        self.lastw, self.readers = {}, {}
        self.dw, self.dr = {}, {}

    def finish(self):
        toks = []
        for e in ("pe", "act", "dve", "pool"):
            if self.cnt[e]:
                toks.append((self.sem[e], self.cnt[e]))
        for q in self.slots:
            for s, v in self.slots[q][0]:
                if v:
                    toks.append((s, v))
        self._wait("sp", toks)

    def close(self):
        for cm in reversed(self._ctx):
            cm.__exit__(None, None, None)


class Rot:
    def __init__(self, tiles):
        self.tiles = tiles
        self.i = 0

    def next(self):
        t = self.tiles[self.i % len(self.tiles)]
        self.i += 1
        return t


class Builder:
    def __init__(self, SEQ, DEPTH, debug=False, kinds=None):
        self.SEQ, self.DEPTH, self.debug = SEQ, DEPTH, debug
        self.kinds = kinds or ["e" if l % 2 == 0 else "o" for l in range(DEPTH)]
        self.NT = SEQ + TS
        self.LE, self.LO = (DEPTH + 1) // 2, DEPTH // 2
        self.ALPHA = (2 * ALPHA_FULL_DEPTH) ** 0.25
        self.nc = nc = bass.Bass("TRN2", target_bir_lowering=False)
        self.fw = FW(nc)
        self.tiles = [(r, 128) for r in range(0, SEQ, 128)] + [(SEQ, TS)]
        self.keep = [min(w, SEQ) for w, _ in B_CONFIGS]
        self.io = {}
        self._cnt = 0

    @contextlib.contextmanager
    def phase(self):
        with contextlib.ExitStack() as es:
            yield es
            self.fw.barrier()

    def din(self, name, shape, dt=F32):
        t = self.nc.dram_tensor(name, list(shape), dt, kind="ExternalInput").ap()
        self.io[name] = t
        return t

    def dout(self, name, shape, dt=F32):
        t = self.nc.dram_tensor(name, list(shape), dt, kind="ExternalOutput").ap()
        self.io[name] = t
        return t

    def dscr(self, name, shape, dt=F32):
        kind = "ExternalOutput" if self.debug else "Internal"
        t = self.nc.dram_tensor(name, list(shape), dt, kind=kind).ap()
        self.io[name] = t
        return t

    def sb(self, es, name, shape, dt=F32):
        self._cnt += 1
        return es.enter_context(self.nc.sbuf_tensor(f"{name}_{self._cnt}", list(shape), dt))

    def ps(self, es, name, shape, dt=F32):
        self._cnt += 1
        t = es.enter_context(self.nc.psum_tensor(f"{name}_{self._cnt}", list(shape), dt))
        return t

    def sbn(self, es, name, shape, dt, n):
        return Rot([self.sb(es, f"{name}{i}", shape, dt) for i in range(n)])

    def psn(self, es, name, shape, dt, n):
        return Rot([self.ps(es, f"{name}{i}", shape, dt) for i in range(n)])

    @staticmethod
    def key(t):
        return t.name if hasattr(t, "name") else str(t)

    def declare(self):
        SEQ, NT, LE, LO, DEPTH = self.SEQ, self.NT, self.LE, self.LO, self.DEPTH
        d = self.din
        self.x_in = d("x_in", [NT, D])
        self.memp = d("memp", [256, D])
        self.st_rwkv = d("st_rwkv", [LE, A_H, 64, 64])
        self.st_shift = d("st_shift", [LE, A_SHIFT])
        self.c_g = [d(f"c_g{g}", [LE, B_CONFIGS[g][0], 512]) for g in range(3)]
        self.st_ret = d("st_ret", [max(LO, 1), C_H, 256, 256])
        self.c_mem = d("c_mem", [DEPTH, 256, 1024])
        self.w_in_even = d("w_in_even", [LE, D, EVEN_IN])
        self.w_out_even = d("w_out_even", [LE, EVEN_OUT, D])
        self.w_in_odd = d("w_in_odd", [max(LO, 1), D, ODD_IN])
        self.w_out_odd = d("w_out_odd", [max(LO, 1), ODD_OUT, D])
        self.w_mem_kv = d("w_mem_kv", [DEPTH, D, 1024])
        self.ln_g = d("ln_g", [DEPTH, D])
        self.ln_b = d("ln_b", [DEPTH, D])
        self.rw = {}
        for nm, shp in (("mu", [LE, A_SHIFT]), ("w0", [LE, A_W]), ("w_up", [LE, 64, A_W]), ("a0", [LE, A_W]),
                        ("a_up", [LE, 64, A_W]), ("k_k", [LE, A_W]), ("k_a", [LE, A_W]), ("r_k", [LE, A_W]),
                        ("lnx_g", [LE, A_W]), ("lnx_b", [LE, A_W])):
            self.rw[nm] = d("rwkv_" + nm, shp)
        self.c_ident = d("c_ident", [128, 128])
        self.c_rot = d("c_rot", [NT, 2, 256])
        self.c_retsc = d("c_retsc", [128, 2, C_H])
        self.c_masks = d("c_masks", [128, 4, 128])
        o = self.dout
        self.y = o("y", [NT, D])
        self.o_rwkv_p = o("o_rwkv_p", [LE, A_H, 64, 64])
        self.o_rwkv_s = o("o_rwkv_s", [LE, A_H, 64, 64])
        self.o_shift_p = o("o_shift_p", [LE, A_SHIFT])
        self.o_shift_s = o("o_shift_s", [LE, A_SHIFT])
        self.o_g_p = [o(f"o_g{g}_p", [LE, self.keep[g], 512]) for g in range(3)]
        self.o_g_s = [o(f"o_g{g}_s", [LE, TS, 512]) for g in range(3)]
        self.o_ret_p = o("o_ret_p", [max(LO, 1), C_H, 256, 256])
        self.o_ret_s = o("o_ret_s", [max(LO, 1), C_H, 256, 256])
        self.o_mem_p = o("o_mem_p", [DEPTH, 256, 1024])
        s = self.dscr
        self.xT_d = s("xT_d", [D, NT], BF16)
        self.memT_d = s("memT_d", [D, 256], BF16)
        self.h_d = s("h_d", [NT, ODD_IN])
        self.u_d = s("u_d", [NT, D], BF16)
        self.x_d = [s("x_d0", [NT, D]), s("x_d1", [NT, D])]
        self.dwa_d = s("dwa_d", [3, SEQ + TS, 260])

    def load_consts(self, es):
        fw = self.fw
        self.idf = self.sb(es, "idf", [128, 128], F32)
        self.idb = self.sb(es, "idb", [128, 128], BF16)
        fw.dma("sp", self.idf[:], self.c_ident, writes=["idf"])
        fw.op("dve", lambda e: e.tensor_copy(out=self.idb[:], in_=self.idf[:]), reads=["idf"], writes=["idb"])
        self.maskf = self.sb(es, "maskf", [128, 4, 128], F32)
        self.maskb = self.sb(es, "maskb", [128, 4, 128], BF16)
        fw.dma("sp", self.maskf[:], self.c_masks, writes=["maskf"])
        fw.op("dve", lambda e: e.tensor_copy(out=self.maskb[:], in_=self.maskf[:]), reads=["maskf"], writes=["maskb"])

    def transposes(self, dst, src, nblk, n, ptrot, dkey, skey, evac=("act", "dve")):
        fw = self.fw
        gi = 0
        for g0 in range(0, nblk, 4):
            gn = min(4, nblk - g0)
            pt = ptrot.next()
            pk = self.key(pt)
            for j in range(gn):
                k = g0 + j
                fw.op("pe", lambda e, k=k, j=j, pt=pt: e.transpose(out=pt[:, j, :n], in_=src[:n, k * 128:(k + 1) * 128], identity=self.idb[:n, :n]),
                      reads=[skey, "idb"], writes=[pk])
            en = evac[gi % len(evac)]
            gi += 1
            if en == "act":
                fw.op("act", lambda e, pt=pt, g0=g0, gn=gn: e.copy(out=dst[:, g0:g0 + gn, :n], in_=pt[:, :gn, :n]), reads=[pk], writes=[dkey])
            else:
                fw.op(en, lambda e, pt=pt, g0=g0, gn=gn: e.tensor_copy(out=dst[:, g0:g0 + gn, :n], in_=pt[:, :gn, :n]), reads=[pk], writes=[dkey])

    def phase_xT_from(self, src_dram, rows_tiles, dstT, dkeyname):
        fw = self.fw
        with self.phase() as es:
            xf = self.sbn(es, "p0xf", [128, D], F32, 2)
            xb = self.sbn(es, "p0xb", [128, D], BF16, 2)
            xT = self.sbn(es, "p0xT", [128, KC, 128], BF16, 2)
            pt = self.psn(es, "p0pt", [128, 4, 128], BF16, 2)
            for (r0, n) in rows_tiles:
                a, b_, c = xf.next(), xb.next(), xT.next()
                fw.dma("sp", a[:n, :], src_dram[r0:r0 + n, :], writes=[self.key(a)])
                fw.op("pool", lambda e, a=a, b_=b_: e.tensor_copy(out=b_[:n, :], in_=a[:n, :]), reads=[self.key(a)], writes=[self.key(b_)])
                self.transposes(c, b_, KC, n, pt, self.key(c), self.key(b_))
                fw.dma("pool", dstT.rearrange("(k p) t -> p k t", p=128)[:, :, r0:r0 + n], c[:, :, :n], reads=[self.key(c)], writes=[dkeyname])

    def proj(self, W, ncol, xT_src, xkey, blocks, dst, dkey):
        fw = self.fw
        with self.phase() as es:
            wst = self.sbn(es, "pjws", [128, KC // 2, 512], F32, 4)
            wbf = self.sbn(es, "pjwb", [128, KC, 512], BF16, 4)
            xTb = self.sbn(es, "pjx", [128, KC, 516], BF16, 2)
            ho = self.sbn(es, "pjho", [128, 512], F32, 4)
            pp = self.psn(es, "pjps", [128, 512], F32, 4)
            Wv = W.rearrange("(k p) n -> p k n", p=128)
            xv = xT_src.rearrange("(k p) t -> p k t", p=128)
            ei = 0
            h = KC // 2
            cgs = [(cg, min(512, ncol - cg)) for cg in range(0, ncol, 512)]
            def load_pair(pi):
                wbs = []
                for (cg, cw) in cgs[pi:pi + 2]:
                    wb = wbf.next()
                    for hf in range(2):
                        ws = wst.next()
                        fw.dma("sp" if hf == 0 else "act", ws[:, :, :cw], Wv[:, hf * h:(hf + 1) * h, cg:cg + cw], writes=[self.key(ws)])
                        for q in range(2):
                            en = "pool" if q % 2 else "dve"
                            k0 = hf * h + q * 4
                            fw.op(en, lambda e, q=q, ws=ws, wb=wb, k0=k0, cw=cw: e.tensor_copy(out=wb[:, k0:k0 + 4, :cw], in_=ws[:, q * 4:(q + 1) * 4, :cw]),
                                  reads=[self.key(ws)], writes=[(self.key(wb), k0 // 4)])
                    wbs.append((wb, cg, cw))
                return wbs

            nxt = load_pair(0)
            for pi in range(0, len(cgs), 2):
                wbs = nxt
                nxt = None
                for (c0, nb, tl) in blocks:
                    xt = xTb.next()
                    fw.dma("sp", xt[:, :, :nb], xv[:, :, c0:c0 + nb], reads=[xkey], writes=[self.key(xt)])
                    for (off, n, r0) in tl:
                        ps = [pp.next() for _ in wbs]
                        for k in range(KC):
                            for p, (wb, cg, cw) in zip(ps, wbs):
                                fw.op("pe", lambda e, k=k, p=p, xt=xt, wb=wb, off=off, n=n, cw=cw: e.matmul(p[:n, :cw], lhsT=xt[:, k, off:off + n], rhs=wb[:, k, :cw], start=(k == 0), stop=(k == KC - 1)),
                                      reads=[self.key(xt), (self.key(wb), k // 4)], writes=[self.key(p)])
                        for p, (wb, cg, cw) in zip(ps, wbs):
                            o = ho.next()
                            if ei % 2 == 0:
                                fw.op("act", lambda e, o=o, p=p, n=n, cw=cw: e.copy(out=o[:n, :cw], in_=p[:n, :cw]), reads=[self.key(p)], writes=[self.key(o)])
                            else:
                                fw.op("dve", lambda e, o=o, p=p, n=n, cw=cw: e.tensor_copy(out=o[:n, :cw], in_=p[:n, :cw]), reads=[self.key(p)], writes=[self.key(o)])
                            ei += 1
                            fw.dma("pool", dst[r0:r0 + n, cg:cg + cw], o[:n, :cw], reads=[self.key(o)], writes=[dkey])
                    if nxt is None and pi + 2 < len(cgs) and (c0, nb, tl) == blocks[min(1, len(blocks) - 1)]:
                        nxt = load_pair(pi + 2)

    def tok_blocks(self):
        SEQ = self.SEQ
        blocks = []
        for c0 in range(0, SEQ, 512):
            nb = min(512, SEQ - c0)
            tl = [(o, 128, c0 + o) for o in range(0, nb, 128)]
            blocks.append([c0, nb, tl])
        blocks[-1][1] += TS
        blocks[-1][2].append((blocks[-1][1] - TS, TS, SEQ))
        return [tuple(b) for b in blocks]

    def mem_prep(self, es, src, skey, name):
        fw = self.fw
        kT = self.sb(es, name + "kT", [128, 4, 256], BF16)
        va = self.sb(es, name + "va", [128, 2, 4, 129], BF16)
        with self.phase() as es2:
            kv = self.sb(es2, "mpkv", [128, 2, 1024], F32)
            kb = self.sb(es2, "mpkb", [128, 2, 512], BF16)
            pt = self.psn(es2, "mppt", [128, 4, 128], BF16, 2)
            fw.dma("sp", kv[:], src.rearrange("(c p) n -> p c n", p=128), reads=[skey], writes=["mpkv"])
            fw.op("dve", lambda e: e.tensor_copy(out=kb[:], in_=kv[:, :, 0:512]), reads=["mpkv"], writes=["mpkb"])
            fw.op("pool", lambda e: e.memset(va[:], 1.0), writes=[name + "va"])
            fw.op("pool", lambda e: e.tensor_copy(out=va[:, :, :, 0:128], in_=kv[:, :, 512:1024].rearrange("p c (h d) -> p c h d", h=4)),
                  reads=["mpkv"], writes=[name + "va"])
            for c in range(2):
                p = pt.next()
                for h in range(4):
                    fw.op("pe", lambda e, h=h, c=c, p=p: e.transpose(out=p[:, h, :], in_=kb[:, c, h * 128:(h + 1) * 128], identity=self.idb[:]),
                          reads=["mpkb", "idb"], writes=[self.key(p)])
                fw.op("act", lambda e, c=c, p=p: e.copy(out=kT[:, :, c * 128:(c + 1) * 128], in_=p[:]), reads=[self.key(p)], writes=[name + "kT"])
        return kT, va, name + "kT", name + "va"

    def mem_attn(self, tiles, kT, va, kTk, vak, qcol, ucol):
        fw = self.fw
        sc = 1.0 / np.sqrt(128.0)
        with self.phase() as es:
            qg = self.sbn(es, "maqg", [128, 1024], F32, 2)
            qb = self.sbn(es, "maqb", [128, 512], BF16, 2)
            qT = self.sbn(es, "maqT", [128, 4, 128], BF16, 2)
            pT = self.sbn(es, "mapT", [128, 8, 128], BF16, 2)
            rl = self.sbn(es, "marl", [128, 4], F32, 2)
            sg = self.sbn(es, "masg", [128, 512], F32, 2)
            om = self.sbn(es, "maom", [128, 512], F32, 2)
            ub = self.sbn(es, "maub", [128, 512], BF16, 2)
            pt = self.psn(es, "mapt", [128, 4, 128], BF16, 1)
            pss = self.psn(es, "mapss", [128, 8, 128], F32, 2)
            pso = self.psn(es, "mapso", [128, 4, 256], F32, 1)
            pend = [None]
            for (r0, n) in tiles:
                q, b_, t_, P, r_, s_, o_, u_ = qg.next(), qb.next(), qT.next(), pT.next(), rl.next(), sg.next(), om.next(), ub.next()
                k = self.key
                fw.dma("sp", q[:n, :], self.h_d[r0:r0 + n, qcol:qcol + 1024], reads=["D:h"], writes=[k(q)])
                fw.op("pool", lambda e, q=q, b_=b_: e.tensor_copy(out=b_[:n, :], in_=q[:n, 0:512]), reads=[k(q)], writes=[k(b_)])
                self.transposes(t_, b_, 4, n, pt, k(t_), k(b_), evac=("dve",))
                S = pss.next()
                for h in range(4):
                    for mc in range(2):
                        fw.op("pe", lambda e, h=h, mc=mc, S=S, t_=t_: e.matmul(S[:, h * 2 + mc, :n], lhsT=kT[:, h, mc * 128:(mc + 1) * 128], rhs=t_[:, h, :n], start=True, stop=True),
                              reads=[kTk, k(t_)], writes=[k(S)])
                fw.op("act", lambda e, S=S, P=P: e.activation(out=P[:, :, :n], in_=S[:, :, :n], func=AF.Exp, scale=float(sc)), reads=[k(S)], writes=[k(P)])
                fw.op("act", lambda e, q=q, s_=s_: e.activation(out=s_[:n, :], in_=q[:n, 512:1024], func=AF.Silu), reads=[k(q)], writes=[k(s_)])

                def part2(r0=r0, n=n, P=P, r_=r_, s_=s_, o_=o_, u_=u_):
                    O = pso.next()
                    for h in range(4):
                        for mc in range(2):
                            fw.op("pe", lambda e, h=h, mc=mc, O=O: e.matmul(O[:n, h, 0:129], lhsT=P[:, h * 2 + mc, :n], rhs=va[:, mc, h, :], start=(mc == 0), stop=(mc == 1)),
                                  reads=[vak, k(P)], writes=[k(O)])
                    fw.op("dve", lambda e, O=O: e.reciprocal(out=r_[:n, :], in_=O[:n, :, 128]), reads=[k(O)], writes=[k(r_)])
                    fw.op("dve", lambda e, O=O: e.tensor_tensor(out=o_[:n, :].rearrange("p (h d) -> p h d", h=4), in0=O[:n, :, 0:128],
                                                                in1=r_[:n, :].unsqueeze(2).to_broadcast([n, 4, 128]), op=ALU.mult),
                          reads=[k(O), k(r_)], writes=[k(o_)])
                    fw.op("pool", lambda e: e.tensor_tensor(out=u_[:n, :], in0=o_[:n, :], in1=s_[:n, :], op=ALU.mult), reads=[k(o_), k(s_)], writes=[k(u_)])
                    fw.dma("pool", self.u_d[r0:r0 + n, ucol:ucol + 512], u_[:n, :], reads=[k(u_)], writes=["D:u"])
                if pend[0] is not None:
                    pend[0]()
                pend[0] = part2
            if pend[0] is not None:
                pend[0]()

    def out_ln(self, l, Wout, UW, x_src, xskey, last):
        fw = self.fw
        nk = UW // 128
        SD, FM, AD = self.nc.vector.BN_STATS_DIM, self.nc.vector.BN_STATS_FMAX, self.nc.vector.BN_AGGR_DIM
        nch = D // FM
        with self.phase() as es:
            wo = self.sb(es, "olwo", [128, nk, D], BF16)
            wst = self.sbn(es, "olws", [128, D], F32, 2)
            gt = self.sb(es, "olg", [128, D], F32)
            bt = self.sb(es, "olb", [128, D], F32)
            fw.dma("sp", gt[:], self.ln_g[l].partition_broadcast(128), writes=["olg"])
            fw.dma("sp", bt[:], self.ln_b[l].partition_broadcast(128), writes=["olb"])
            for kk in range(nk):
                w = wst.next()
                fw.dma("sp" if kk % 2 else "act", w[:], Wout[kk * 128:(kk + 1) * 128, :], writes=[self.key(w)])
                fw.op("pool" if kk % 2 else "dve", lambda e, w=w, kk=kk: e.tensor_copy(out=wo[:, kk, :], in_=w[:]), reads=[self.key(w)], writes=[("olwo", kk)])
            wkeys = [("olwo", kk) for kk in range(nk)]
            ut = self.sbn(es, "olu", [128, UW], BF16, 2)
            uT = self.sbn(es, "oluT", [128, nk, 128], BF16, 2)
            xt = self.sbn(es, "olx", [128, D], F32, 2)
            zt = self.sbn(es, "olz", [128, D], F32, 3)
            st = self.sbn(es, "olst", [128, nch, SD], F32, 2)
            mv = self.sbn(es, "olmv", [128, AD], F32, 2)
            rs = self.sbn(es, "olrs", [128, 1], F32, 2)
            xb = self.sbn(es, "olxb", [128, D], BF16, 2)
            xT = self.sbn(es, "olxT", [128, KC, 128], BF16, 2)
            pt = self.psn(es, "olpt", [128, 4, 128], BF16, 2)
            po = self.psn(es, "olpo", [128, 4, 512], F32, 1)
            k = self.key
            pend = None

            def tail(r0, n, z):
                b_, T_ = xb.next(), xT.next()
                fw.op("act", lambda e, b_=b_, z=z: e.copy(out=b_[:n, :], in_=z[:n, :]), reads=[k(z)], writes=[k(b_)])
                self.transposes(T_, b_, KC, n, pt, k(T_), k(b_))
                fw.dma("pool", self.xT_d.rearrange("(k p) t -> p k t", p=128)[:, :, r0:r0 + n], T_[:, :, :n], reads=[k(T_)], writes=["D:xT"])

            for (r0, n) in self.tiles:
                u, uT_, x_, z, s_, m_, r_ = ut.next(), uT.next(), xt.next(), zt.next(), st.next(), mv.next(), rs.next()
                fw.dma("sp", u[:n, :], self.u_d[r0:r0 + n, 0:UW], reads=["D:u"], writes=[k(u)])
                fw.dma("sp", x_[:n, :], x_src[r0:r0 + n, :], reads=[xskey], writes=[k(x_)])
                self.transposes(uT_, u, nk, n, pt, k(uT_), k(u))
                P = po.next()
                for kk in range(nk):
                    for nb in range(4):
                        fw.op("pe", lambda e, nb=nb, kk=kk, P=P, uT_=uT_: e.matmul(P[:n, nb, :], lhsT=uT_[:, kk, :n], rhs=wo[:, kk, nb * 512:(nb + 1) * 512], start=(kk == 0), stop=(kk == nk - 1)),
                              reads=[k(uT_), ("olwo", kk)], writes=[k(P)])
                if pend is not None:
                    tail(*pend)
                    pend = None
                fw.op("dve", lambda e, z=z, x_=x_, P=P: e.scalar_tensor_tensor(out=z[:n, :], in0=x_[:n, :], scalar=float(self.ALPHA), in1=P[:n].rearrange("p a b -> p (a b)"), op0=ALU.mult, op1=ALU.add),
                      reads=[k(x_), k(P)], writes=[k(z)])
                for c in range(nch):
                    fw.op("dve", lambda e, c=c, s_=s_, z=z: e.bn_stats(out=s_[:n, c, :], in_=z[:n, c * FM:(c + 1) * FM]), reads=[k(z)], writes=[k(s_)])
                fw.op("dve", lambda e, s_=s_, m_=m_: e.bn_aggr(out=m_[:n, :], in_=s_[:n]), reads=[k(s_)], writes=[k(m_)])
                fw.op("dve", lambda e, r_=r_, m_=m_: e.tensor_scalar_add(out=r_[:n, :], in0=m_[:n, 1:2], scalar1=LN_EPS), reads=[k(m_)], writes=[k(r_)])
                fw.op("act", lambda e, r_=r_: e.sqrt(out=r_[:n, :], in_=r_[:n, :]), reads=[k(r_)], writes=[k(r_)])
                fw.op("dve", lambda e, r_=r_: e.reciprocal(out=r_[:n, :], in_=r_[:n, :]), reads=[k(r_)], writes=[k(r_)])
                fw.op("dve", lambda e, z=z, m_=m_, r_=r_: e.tensor_scalar(out=z[:n, :], in0=z[:n, :], scalar1=m_[:n, 0:1], scalar2=r_[:n, 0:1], op0=ALU.subtract, op1=ALU.mult),
                      reads=[k(z), k(m_), k(r_)], writes=[k(z)])
                fw.op("pool", lambda e, z=z: e.tensor_tensor(out=z[:n, :], in0=z[:n, :], in1=gt[:n, :], op=ALU.mult), reads=[k(z), "olg"], writes=[k(z)])
                fw.op("pool", lambda e, z=z: e.tensor_tensor(out=z[:n, :], in0=z[:n, :], in1=bt[:n, :], op=ALU.add), reads=[k(z), "olb"], writes=[k(z)])
                if last:
                    fw.dma("pool", self.y[r0:r0 + n, :], z[:n, :], reads=[k(z)], writes=["D:y"])
                else:
                    fw.dma("pool", self.x_d[l % 2][r0:r0 + n, :], z[:n, :], reads=[k(z)], writes=[f"D:x{l % 2}"])
                    pend = (r0, n, z)
            if pend is not None:
                tail(*pend)

    def build(self, stop_after=None):
        fw = self.fw
        self.declare()
        with self.phase() as es:
            self.load_consts(es)
            self.phase_xT_from(self.x_in, self.tiles, self.xT_d, "D:xT")
            self.phase_xT_from(self.memp, [(0, 128), (128, 128)], self.memT_d, "D:memT")
            blocks = self.tok_blocks()
            mblocks = [(0, 256, [(0, 128, 0), (128, 128, 128)])]
            for l in range(self.DEPTH):
                even = self.kinds[l] == "e"
                i2 = sum(1 for q in self.kinds[:l] if q == self.kinds[l])
                W = self.w_in_even[i2] if even else self.w_in_odd[i2]
                ncol = EVEN_IN if even else ODD_IN
                self.proj(W, ncol, self.xT_d, "D:xT", blocks, self.h_d, "D:h")
                self.proj(self.w_mem_kv[l], 1024, self.memT_d, "D:memT", mblocks, self.o_mem_p[l], "D:omem")
                qcol = ncol - 1024
                ucol = (EVEN_OUT if even else ODD_OUT) - 512
                with self.phase() as es2:
                    kT, va, kTk, vak = self.mem_prep(es2, self.o_mem_p[l], "D:omem", "mp")
                    self.mem_attn(self.tiles[:-1], kT, va, kTk, vak, qcol, ucol)
                with self.phase() as es2:
                    kT, va, kTk, vak = self.mem_prep(es2, self.c_mem[l], "D:cmem", "ms")
                    self.mem_attn(self.tiles[-1:], kT, va, kTk, vak, qcol, ucol)
                if even:
                    self.even_mixers(i2)
                else:
                    self.odd_mixers(i2)
                x_src, xk = (self.x_in, "D:xin") if l == 0 else (self.x_d[(l - 1) % 2], f"D:x{(l - 1) % 2}")
                self.out_ln(l, self.w_out_even[i2] if even else self.w_out_odd[i2], EVEN_OUT if even else ODD_OUT, x_src, xk, l == self.DEPTH - 1)
            fw.finish()
        fw.close()
        return self.nc

    def even_mixers(self, e):
        self.dwa(e)
        self.rwkv(e)

    def dwa(self, e):
        fw, k = self.fw, self.key
        SEQ = self.SEQ
        QC, KCOL, VC, GB = 3200, 3968, 4736, 5504
        hp_ = self.h_d[0:SEQ, :]
        for g in range(3):
            kp = self.keep[g]
            for j, col in enumerate((KCOL + g * 256, VC + g * 256)):
                fw.dma("act", self.o_g_p[g][e][:, j * 256:(j + 1) * 256], self.h_d[SEQ - kp:SEQ, col:col + 256], reads=["D:h"], writes=["D:ogp"])
                fw.dma("act", self.o_g_s[g][e][:, j * 256:(j + 1) * 256], self.h_d[SEQ:SEQ + TS, col:col + 256], reads=["D:h"], writes=["D:ogs"])
        fw.dma("act", self.o_shift_p[e], self.h_d[SEQ - 1, 0:A_SHIFT], reads=["D:h"], writes=["D:osh"])
        fw.dma("act", self.o_shift_s[e], self.h_d[SEQ + TS - 1, 0:A_SHIFT], reads=["D:h"], writes=["D:osh"])
        units = []
        for g, (win, d) in enumerate(B_CONFIGS):
            qc, kc, vc = QC + g * 256, KCOL + g * 256, VC + g * 256
            Lc = SEQ // d
            n = min(128, Lc)
            hv = hp_.rearrange("(m d) c -> d m c", d=d)
            ov = self.dwa_d[g][0:SEQ, :].rearrange("(m d) c -> d m c", d=d)
            for r in range(d):
                for m0 in range(0, Lc, n):
                    cur = hv[r, m0:m0 + n]
                    prev = hv[r, m0 - n:m0] if m0 > 0 else None
                    units.append((cur[:, qc:qc + 256], cur[:, kc:kc + 256], cur[:, vc:vc + 256], n,
                                  None if prev is None else prev[:, kc:kc + 256], None if prev is None else prev[:, vc:vc + 256], n, ov[r, m0:m0 + n], "D:h"))
            cg = self.c_g[g][e]
            if d == 1:
                cur = self.h_d[SEQ:SEQ + TS]
                units.append((cur[:, qc:qc + 256], cur[:, kc:kc + 256], cur[:, vc:vc + 256], TS, cg[:, 0:256], cg[:, 256:512], 128, self.dwa_d[g][SEQ:SEQ + TS, :], "D:cg"))
            else:
                cv = cg.rearrange("(m d) c -> d m c", d=d)
                for i in range(TS):
                    cur = self.h_d[SEQ + i:SEQ + i + 1]
                    units.append((cur[:, qc:qc + 256], cur[:, kc:kc + 256], cur[:, vc:vc + 256], 1, cv[i][:, 0:256], cv[i][:, 256:512], 128, self.dwa_d[g][SEQ + i:SEQ + i + 1, :], "D:cg"))
        import os
        bis = os.environ.get("DWA_BIS", "")
        if bis == "none":
            units = []
        elif bis == "prompt":
            units = [u for u in units if u[8] == "D:h"]
        elif bis == "g0":
            units = units[:4]
        elif bis == "samp":
            units = [u for u in units if u[8] != "D:h"]
        with self.phase() as es:
            Xc = self.sbn(es, "dwXc", [128, 3, 256], F32, 2)
            Xp = self.sbn(es, "dwXp", [128, 2, 256], F32, 2)
            qkb = self.sbn(es, "dwqkb", [128, 3, 256], BF16, 2)
            Va = self.sbn(es, "dwVa", [128, 2, 4, 65], BF16, 2)
            T = self.sbn(es, "dwT", [64, 3, 4, 128], BF16, 2)
            P = self.sbn(es, "dwP", [128, 4, 2, 128], BF16, 2)
            Os = self.sbn(es, "dwOs", [128, 260], F32, 2)
            pt = self.psn(es, "dwpt", [128, 4, 128], BF16, 2)
            pS = self.psn(es, "dwpS", [128, 4, 2, 128], F32, 2)
            pO = self.psn(es, "dwpO", [128, 4, 128], F32, 2)
            pend2 = [None]
            for (qs, ks, vs, n, pks, pvs, npv, dst, pkey) in units:
                xc, xp, qb, va, t, p, os_ = Xc.next(), Xp.next(), qkb.next(), Va.next(), T.next(), P.next(), Os.next()
                hasp = pks is not None
                fw.dma("sp", xc[:n, 0, :], qs, reads=["D:h"], writes=[(k(xc), 0)])
                fw.dma("sp", xc[:n, 1, :], ks, reads=["D:h"], writes=[(k(xc), 1)])
                fw.dma("sp", xc[:n, 2, :], vs, reads=["D:h"], writes=[(k(xc), 2)])
                fw.op("pool", lambda e_, va=va: e_.memset(va[:], 1.0), writes=[k(va)])
                fw.op("dve", lambda e_, xc=xc, qb=qb: e_.tensor_copy(out=qb[:n, 0:2, :], in_=xc[:n, 0:2, :]), reads=[(k(xc), 0), (k(xc), 1)], writes=[(k(qb), 0)])
                fw.op("pool", lambda e_, xc=xc, va=va: e_.tensor_copy(out=va[:n, 0, :, 0:64], in_=xc[:n, 2, :].rearrange("p (h c) -> p h c", h=4)), reads=[(k(xc), 2)], writes=[k(va)])
                if hasp:
                    fw.dma("sp", xp[:npv, 0, :], pks, reads=[pkey], writes=[(k(xp), 0)])
                    fw.dma("sp", xp[:npv, 1, :], pvs, reads=[pkey], writes=[(k(xp), 1)])
                    fw.op("dve", lambda e_, xp=xp, qb=qb: e_.tensor_copy(out=qb[:npv, 2, :], in_=xp[:npv, 0, :]), reads=[(k(xp), 0)], writes=[(k(qb), 1)])
                    fw.op("pool", lambda e_, xp=xp, va=va: e_.tensor_copy(out=va[:npv, 1, :, 0:64], in_=xp[:npv, 1, :].rearrange("p (h c) -> p h c", h=4)), reads=[(k(xp), 1)], writes=[k(va)])
                stage = int(os.environ.get("DWA_STAGE", "9"))
                if stage < 2:
                    continue
                for w_, (nn, rk) in enumerate(((n, 0), (n, 0), (npv, 1))):
                    if w_ == 2 and not hasp:
                        continue
                    pp_ = pt.next()
                    for b_ in range(4):
                        fw.op("pe", lambda e_, w_=w_, b_=b_, pp_=pp_, qb=qb, nn=nn: e_.transpose(out=pp_[0:64, b_, :nn], in_=qb[:nn, w_, b_ * 64:(b_ + 1) * 64], identity=self.idb[:nn, :nn]),
                              reads=[(k(qb), rk), "idb"], writes=[k(pp_)])
                    fw.op("act" if w_ % 2 else "dve", (lambda e_, w_=w_, pp_=pp_, t=t, nn=nn: e_.copy(out=t[:, w_, :, :nn], in_=pp_[0:64, :, :nn])) if w_ % 2 else
                          (lambda e_, w_=w_, pp_=pp_, t=t, nn=nn: e_.tensor_copy(out=t[:, w_, :, :nn], in_=pp_[0:64, :, :nn])), reads=[k(pp_)], writes=[(k(t), w_)])
                if stage < 3:
                    continue
                S = pS.next()
                for h in range(4):
                    fw.op("pe", lambda e_, h=h, S=S, t=t: e_.matmul(S[:n, h, 0, :n], lhsT=t[:, 1, h, :n], rhs=t[:, 0, h, :n], start=True, stop=True),
                          reads=[(k(t), 0), (k(t), 1)], writes=[k(S)])
                    if hasp:
                        fw.op("pe", lambda e_, h=h, S=S, t=t: e_.matmul(S[:npv, h, 1, :n], lhsT=t[:, 2, h, :npv], rhs=t[:, 0, h, :n], start=True, stop=True),
                              reads=[(k(t), 0), (k(t), 2)], writes=[k(S)])
                if stage < 4:
                    continue
                fw.op("act", lambda e_, S=S, p=p: e_.activation(out=p[:n, :, 0, :n], in_=S[:n, :, 0, :n], func=AF.Exp, scale=0.125), reads=[k(S)], writes=[(k(p), 0)])
                fw.op("pool", lambda e_, p=p: e_.tensor_tensor(out=p[:n, :, 0, :n], in0=p[:n, :, 0, :n], in1=self.maskb[:n, 0, :n].unsqueeze(1).to_broadcast([n, 4, n]), op=ALU.mult),
                      reads=[(k(p), 0), "maskb"], writes=[(k(p), 0)])
                if hasp:
                    fw.op("act", lambda e_, S=S, p=p: e_.activation(out=p[:npv, :, 1, :n], in_=S[:npv, :, 1, :n], func=AF.Exp, scale=0.125), reads=[k(S)], writes=[(k(p), 1)])
                    fw.op("dve", lambda e_, p=p: e_.tensor_tensor(out=p[:npv, :, 1, :n], in0=p[:npv, :, 1, :n], in1=self.maskb[:npv, 1, :n].unsqueeze(1).to_broadcast([npv, 4, n]), op=ALU.mult),
                          reads=[(k(p), 1), "maskb"], writes=[(k(p), 1)])
                def part2(n=n, npv=npv, hasp=hasp, p=p, va=va, os_=os_, dst=dst):
                    O = pO.next()
                    for h in range(4):
                        fw.op("pe", lambda e_, h=h, O=O: e_.matmul(O[:n, h, 0:65], lhsT=p[:n, h, 0, :n], rhs=va[:n, 0, h, :], start=True, stop=not hasp),
                              reads=[(k(p), 0), k(va)], writes=[k(O)])
                        if hasp:
                            fw.op("pe", lambda e_, h=h, O=O: e_.matmul(O[:n, h, 0:65], lhsT=p[:npv, h, 1, :n], rhs=va[:npv, 1, h, :], start=False, stop=True),
                                  reads=[(k(p), 1), k(va)], writes=[k(O)])
                    fw.op("act", lambda e_, O=O: e_.copy(out=os_[:n, :].rearrange("p (h c) -> p h c", h=4), in_=O[:n, :, 0:65]), reads=[k(O)], writes=[k(os_)])
                    fw.dma("pool", dst, os_[:n, :], reads=[k(os_)], writes=["D:dwa"])
                if pend2[0] is not None:
                    pend2[0]()
                pend2[0] = part2
            if pend2[0] is not None:
                pend2[0]()
        with self.phase() as es:
            A = self.sbn(es, "dcA", [128, 3, 260], F32, 2)
            G = self.sbn(es, "dcG", [128, 256], F32, 2)
            rl = self.sbn(es, "dcrl", [128, 4], F32, 2)
            Y = self.sbn(es, "dcY", [128, 256], F32, 2)
            U = self.sbn(es, "dcU", [128, 256], BF16, 2)
            for (r0, n) in self.tiles:
                a, g_, r_, y_, u_ = A.next(), G.next(), rl.next(), Y.next(), U.next()
                fw.dma("sp", a[:n], self.dwa_d[:, r0:r0 + n, :].rearrange("g p c -> p g c"), reads=["D:dwa"], writes=[k(a)])
                fw.dma("sp", g_[:n, :], self.h_d[r0:r0 + n, GB:GB + 256], reads=["D:h"], writes=[k(g_)])
                fw.op("dve", lambda e_, a=a: e_.tensor_tensor(out=a[:n, 0, :], in0=a[:n, 0, :], in1=a[:n, 1, :], op=ALU.add), reads=[k(a)], writes=[k(a)])
                fw.op("dve", lambda e_, a=a: e_.tensor_tensor(out=a[:n, 0, :], in0=a[:n, 0, :], in1=a[:n, 2, :], op=ALU.add), reads=[k(a)], writes=[k(a)])
                a4 = a[:n, 0, :].rearrange("p (h c) -> p h c", h=4)
                fw.op("dve", lambda e_, a4=a4, r_=r_: e_.reciprocal(out=r_[:n, :], in_=a4[:, :, 64]), reads=[k(a)], writes=[k(r_)])
                fw.op("act", lambda e_, g_=g_: e_.activation(out=g_[:n, :], in_=g_[:n, :], func=AF.Silu), reads=[k(g_)], writes=[k(g_)])
                fw.op("dve", lambda e_, a4=a4, r_=r_, y_=y_: e_.tensor_tensor(out=y_[:n, :].rearrange("p (h c) -> p h c", h=4), in0=a4[:, :, 0:64], in1=r_[:n, :].unsqueeze(2).to_broadcast([n, 4, 64]), op=ALU.mult),
                      reads=[k(a), k(r_)], writes=[k(y_)])
                fw.op("pool", lambda e_, y_=y_, g_=g_, u_=u_: e_.tensor_tensor(out=u_[:n, :], in0=y_[:n, :], in1=g_[:n, :], op=ALU.mult), reads=[k(y_), k(g_)], writes=[k(u_)])
                fw.dma("pool", self.u_d[r0:r0 + n, A_W:A_W + 256], u_[:n, :], reads=[k(u_)], writes=["D:u"])

    def rwkv(self, e):
        fw, k = self.fw, self.key
        SEQ = self.SEQ
        NEG = -float(np.exp(-0.5))
        with self.phase() as es:
            def bc(name, src, w):
                t = self.sb(es, name, [128, w], F32)
                fw.dma("sp", t[:], src.partition_broadcast(128), writes=[name])
                return t
            mu = bc("rwmu", self.rw["mu"][e], A_SHIFT)
            w0 = bc("rww0", self.rw["w0"][e], A_W)
            a0 = bc("rwa0", self.rw["a0"][e], A_W)
            k_k = bc("rwkk", self.rw["k_k"][e], A_W)
            k_a = bc("rwka", self.rw["k_a"][e], A_W)
            r_k = bc("rwrk", self.rw["r_k"][e], A_W)
            lg = bc("rwlg", self.rw["lnx_g"][e], A_W)
            lb = bc("rwlb", self.rw["lnx_b"][e], A_W)
            wup = self.sb(es, "rwwup", [64, 2, A_W], F32)
            fw.dma("sp", wup[:, 0, :], self.rw["w_up"][e], writes=["rwwup"])
            fw.dma("sp", wup[:, 1, :], self.rw["a_up"][e], writes=["rwwup"])
            ones = self.sb(es, "rwones", [128, 1], F32)
            fw.op("pool", lambda e_: e_.memset(ones[:], 1.0), writes=["rwones"])
            H = self.sbn(es, "rwH", [128, A_SHIFT], F32, 1)
            hs = self.sb(es, "rwhs", [128, A_SHIFT], F32)
            lT = self.sb(es, "rwlT", [64, 2, 128], F32)
            sw = self.sb(es, "rwsw", [128, A_W], F32)
            av = self.sb(es, "rwa", [128, A_W], F32)
            kk = self.sb(es, "rwkkv", [128, A_W], F32)
            tmp = self.sb(es, "rwtmp", [128, A_W], F32)
            tmp2 = self.sb(es, "rwtmp2", [128, A_W], F32)
            s12p = self.sb(es, "rws12p", [128, 2, 12], F32)
            kmod = self.sb(es, "rwkmod", [128, A_W], F32)
            cs = self.sb(es, "rwcs", [128, A_W], F32)
            E3 = self.sb(es, "rwE3", [128, 3, A_W], BF16)
            raw = self.sb(es, "rwraw", [128, 12, 128], BF16)
            Pm = {nm: self.sb(es, "rwM" + nm, [128, 12, 128], BF16) for nm in ("P", "PT", "Pn", "PTn")}
            setsA, setsB = [], []
            for si in range(3):
                d = {}
                d["X4"] = self.sb(es, f"rwX4{si}", [128, 4, A_W], BF16)
                d["vb"] = self.sb(es, f"rwvb{si}", [128, A_W], BF16)
                d["gC"] = self.sb(es, f"rwgC{si}", [64, 12], F32)
                d["bon"] = self.sb(es, f"rwbon{si}", [128, A_W], F32)
                d["sg"] = self.sb(es, f"rwsg{si}", [128, A_W], F32)
                setsA.append(d)
            for si in range(2):
                d = {}
                d["XT"] = self.sb(es, f"rwXT{si}", [64, 4, 12, 128], BF16)
                for nm in ("AkT", "BbT", "BkT", "WT"):
                    d[nm] = self.sb(es, f"rwM{nm}{si}", [128, 12, 128], BF16)
                setsB.append(d)
            RH = self.sb(es, "rwRH", [128, 12, 64], BF16)
            Un = self.sb(es, "rwUn", [128, 12, 64], BF16)
            Z = self.sb(es, "rwZ", [64, 12, 64], F32)
            Zb = self.sb(es, "rwZb", [64, 12, 64], BF16)
            Y = self.sb(es, "rwY", [128, A_W], F32)
            Y2 = self.sb(es, "rwY2", [128, A_W], F32)
            Ssb = Y2[0:64, :].rearrange("p (h c) -> p h c", h=12)
            s12 = self.sb(es, "rws12", [128, 3, 12], F32)
            U = self.sbn(es, "rwU", [128, A_W], BF16, 2)
            print("rwkv sbuf remaining", self.nc.sbuf_bytes_remaining)
            pb = self.psn(es, "rwpb", [128, 512], F32, 8)

            def tt(en, out, in0, in1, op, reads, writes):
                fw.op(en, lambda e_: e_.tensor_tensor(out=out, in0=in0, in1=in1, op=op), reads=reads, writes=writes)

            def acopy(dst, src, reads, writes, scale=None):
                if scale is None:
                    fw.op("act", lambda e_: e_.copy(out=dst, in_=src), reads=reads, writes=writes)
                else:
                    fw.op("act", lambda e_: e_.mul(out=dst, in_=src, mul=float(scale)), reads=reads, writes=writes)

            def dcopy(dst, src, reads, writes):
                fw.op("dve", lambda e_: e_.tensor_copy(out=dst, in_=src), reads=reads, writes=writes)

            v3 = lambda t_: t_.rearrange("p (h c) -> p h c", h=12)
            allk = lambda nm: [nm, (nm, 0), (nm, 1), (nm, 2)]

            def bank6(n_, bf=False):
                B = pb.next()
                if bf:
                    return B, B[:n_, :].bitcast(BF16)[:, 0:512].rearrange("p (h c) -> p h c", h=4)
                return B, B[:n_, 0:512].rearrange("p (h c) -> p h c", h=4)

            def stageA(ci, r0, n, seg, S):
                X4, gC, vb, bon, sg = S["X4"], S["gC"], S["vb"], S["bon"], S["sg"]
                kX4 = k(X4)
                h_ = H.next()
                fw.dma("sp", h_[:n, :], self.h_d[r0:r0 + n, 0:A_SHIFT], reads=["D:h"], writes=[k(h_)])
                fw.dma("sp", sg[:n, :], self.h_d[r0:r0 + n, A_SHIFT:3200], reads=["D:h"], writes=[k(sg)])
                if ci == 0:
                    if seg == 0:
                        fw.op("pool", lambda e_: e_.memset(hs[0:1, :], 0.0), writes=["rwhs"])
                    else:
                        fw.dma("sp", hs[0:1, :], self.st_shift[e:e + 1, :], writes=["rwhs"])
                    if n > 1:
                        fw.dma("sp", hs[1:n, :], self.h_d[r0:r0 + n - 1, 0:A_SHIFT], reads=["D:h"], writes=[("rwhs", 1)])
                else:
                    fw.dma("sp", hs[:n, :], self.h_d[r0 - 1:r0 + n - 1, 0:A_SHIFT], reads=["D:h"], writes=["rwhs", ("rwhs", 1)])
                hpk = ["rwhs", ("rwhs", 1)]
                tt("dve", hs[:n, :], hs[:n, :], h_[:n, 0:A_SHIFT], ALU.subtract, hpk + [k(h_)], hpk)
                tt("pool", hs[:n, :], hs[:n, :], mu[:n, :], ALU.mult, hpk + ["rwmu"], hpk)
                tt("dve", hs[:n, :], hs[:n, :], h_[:n, 0:A_SHIFT], ALU.add, hpk + [k(h_)], hpk)
                r_, k_, v_ = hs[:n, 0:768], hs[:n, 768:1536], hs[:n, 1536:2304]
                fw.op("act", lambda e_: e_.activation(out=sg[:n, :], in_=sg[:n, :], func=AF.Silu), reads=[k(sg)], writes=[k(sg)])
                fw.op("act", lambda e_: e_.activation(out=hs[:n, 2304:2368], in_=hs[:n, 2304:2368], func=AF.Tanh), reads=["rwhs"], writes=["rwhs"])
                acopy(vb[:n, :], v_, ["rwhs"], [k(vb)])
                yield
                B = pb.next()
                for j in range(2):
                    fw.op("pe", lambda e_, j=j, B=B: e_.transpose(out=B[:64, j * 128:j * 128 + n], in_=hs[:n, 2304 + j * 64:2368 + j * 64], identity=self.idf[:n, :n]), reads=["rwhs", "idf"], writes=[k(B)])
                dcopy(lT[:, :, :n], B[:64, 0:256].rearrange("p (a c) -> p a c", a=2)[:, :, :n], [k(B)], ["rwlT"])
                yield
                lb_ = {}
                for j in range(2):
                    for hf in range(2):
                        B = pb.next()
                        lb_[(j, hf)] = B
                        fw.op("pe", lambda e_, j=j, hf=hf, B=B: e_.matmul(B[:n, 0:384], lhsT=lT[:, j, :n], rhs=wup[:, j, hf * 384:(hf + 1) * 384], start=True, stop=True), reads=["rwlT", "rwwup"], writes=[k(B)])
                for j, (dst, off, dk_, ok_) in enumerate(((sw, w0, "rwsw", "rww0"), (av, a0, "rwa", "rwa0"))):
                    for hf in range(2):
                        B = lb_[(j, hf)]
                        tt("dve", dst[:n, hf * 384:(hf + 1) * 384], B[:n, 0:384], off[:n, hf * 384:(hf + 1) * 384], ALU.add, [k(B), ok_], [dk_])
                yield
                fw.op("act", lambda e_: e_.activation(out=sw[:n, :], in_=sw[:n, :], func=AF.Sigmoid), reads=["rwsw"], writes=["rwsw"])
                fw.op("act", lambda e_: e_.activation(out=av[:n, :], in_=av[:n, :], func=AF.Sigmoid), reads=["rwa"], writes=["rwa"])
                yield
                tt("dve", kk[:n, :], k_, k_k[:n, :], ALU.mult, ["rwhs", "rwkk"], ["rwkkv"])
                tt("pool", tmp[:n, :], kk[:n, :], kk[:n, :], ALU.mult, ["rwkkv"], ["rwtmp"])
                fw.op("dve", lambda e_: e_.tensor_reduce(out=s12p[:n, 0, :], in_=v3(tmp[:n, :]), axis=AX.X, op=ALU.add), reads=["rwtmp"], writes=["rws12p"])
                fw.op("dve", lambda e_: e_.tensor_scalar_max(out=s12p[:n, 0, :], in0=s12p[:n, 0, :], scalar1=1e-24), reads=["rws12p"], writes=["rws12p"])
                yield
                fw.op("act", lambda e_: e_.sqrt(out=s12p[:n, 0, :], in_=s12p[:n, 0, :]), reads=["rws12p"], writes=["rws12p"])
                fw.op("dve", lambda e_: e_.reciprocal(out=s12p[:n, 0, :], in_=s12p[:n, 0, :]), reads=["rws12p"], writes=["rws12p"])
                tt("dve", v3(kk[:n, :]), v3(kk[:n, :]), s12p[:n, 0, :].unsqueeze(2).to_broadcast([n, 12, 64]), ALU.mult, ["rwkkv", "rws12p"], ["rwkkv"])
                fw.op("dve", lambda e_: e_.scalar_tensor_tensor(out=tmp[:n, :], in0=av[:n, :], scalar=-1.0, in1=k_a[:n, :], op0=ALU.add, op1=ALU.mult), reads=["rwa", "rwka"], writes=["rwtmp"])
                fw.op("dve", lambda e_: e_.scalar_tensor_tensor(out=kmod[:n, :], in0=tmp[:n, :], scalar=1.0, in1=k_, op0=ALU.add, op1=ALU.mult), reads=["rwtmp", "rwhs"], writes=["rwkmod"])
                tt("pool", tmp2[:n, :], r_, kmod[:n, :], ALU.mult, ["rwhs", "rwkmod"], ["rwtmp2"])
                tt("pool", tmp2[:n, :], tmp2[:n, :], r_k[:n, :], ALU.mult, ["rwtmp2", "rwrk"], ["rwtmp2"])
                fw.op("dve", lambda e_: e_.tensor_reduce(out=s12p[:n, 1, :], in_=v3(tmp2[:n, :]), axis=AX.X, op=ALU.add), reads=["rwtmp2"], writes=["rws12p"])
                tt("pool", v3(bon[:n, :]), v3(v_), s12p[:n, 1, :].unsqueeze(2).to_broadcast([n, 12, 64]), ALU.mult, ["rwhs", "rws12p"], [k(bon)])
                yield
                for hf in range(2):
                    B = pb.next()
                    fw.op("pe", lambda e_, hf=hf, B=B: e_.matmul(B[:n, 0:384], lhsT=self.maskf[:n, 0, :n], rhs=sw[:n, hf * 384:(hf + 1) * 384], start=True, stop=True), reads=["maskf", "rwsw"], writes=[k(B)])
                    if hf:
                        acopy(cs[:n, hf * 384:(hf + 1) * 384], B[:n, 0:384], [k(B)], [("rwcs", hf)])
                    else:
                        dcopy(cs[:n, hf * 384:(hf + 1) * 384], B[:n, 0:384], [k(B)], [("rwcs", hf)])
                csk = [("rwcs", 0), ("rwcs", 1)]
                yield
                fw.op("act", lambda e_: e_.activation(out=E3[:n, 0, :], in_=cs[:n, :], func=AF.Exp, scale=NEG), reads=csk, writes=[("rwE3", 0)])
                fw.op("act", lambda e_: e_.activation(out=E3[:n, 1, :], in_=cs[:n, :], func=AF.Exp, scale=-NEG), reads=csk, writes=[("rwE3", 1)])
                tt("dve", tmp[:n, :], cs[:n, :], sw[:n, :], ALU.subtract, csk + ["rwsw"], ["rwtmp"])
                fw.op("act", lambda e_: e_.activation(out=E3[:n, 2, :], in_=tmp[:n, :], func=AF.Exp, scale=NEG), reads=["rwtmp"], writes=[("rwE3", 2)])
                B = pb.next()
                for h in range(12):
                    fw.op("pe", lambda e_, h=h, B=B: e_.matmul(B[:64, h:h + 1], lhsT=sw[:n, h * 64:(h + 1) * 64], rhs=ones[:n, 0:1], start=True, stop=True), reads=["rwsw", "rwones"], writes=[k(B)])
                fw.op("act", lambda e_, B=B: e_.activation(out=gC[:, :], in_=B[:64, 0:12], func=AF.Exp, scale=NEG), reads=[k(B)], writes=[k(gC)])
                yield
                tt("dve", X4[:n, 0, :], kk[:n, :], E3[:n, 2, :], ALU.mult, ["rwkkv", ("rwE3", 2)], [(kX4, 0)])
                tt("pool", X4[:n, 1, :], r_, E3[:n, 0, :], ALU.mult, ["rwhs", ("rwE3", 0)], [(kX4, 1)])
                tt("dve", tmp[:n, :], kk[:n, :], av[:n, :], ALU.mult, ["rwkkv", "rwa"], ["rwtmp"])
                tt("pool", X4[:n, 2, :], tmp[:n, :], E3[:n, 1, :], ALU.mult, ["rwtmp", ("rwE3", 1)], [(kX4, 2)])
                tt("dve", X4[:n, 3, :], kmod[:n, :], E3[:n, 1, :], ALU.mult, ["rwkmod", ("rwE3", 1)], [(kX4, 3)])
                yield

            def stageB(n, nlev, SA, S):
                X4, XT = SA["X4"], S["XT"]
                kX4, kXT = k(X4), k(XT)
                for wi_, w_ in enumerate((2, 0, 3, 1)):
                    for hg in range(3):
                        B, Bv = bank6(64, bf=True)
                        for hh in range(4):
                            h = hg * 4 + hh
                            fw.op("pe", lambda e_, w_=w_, h=h, hh=hh, Bv=Bv: e_.transpose(out=Bv[:, hh, :n], in_=X4[:n, w_, h * 64:(h + 1) * 64], identity=self.idb[:n, :n]), reads=[(kX4, w_), "idb"], writes=[k(B)])
                        if (w_ + hg) % 2:
                            acopy(XT[:, w_, hg * 4:(hg + 1) * 4, :n], Bv[:, :, :n], [k(B)], [(kXT, w_)])
                        else:
                            dcopy(XT[:, w_, hg * 4:(hg + 1) * 4, :n], Bv[:, :, :n], [k(B)], [(kXT, w_)])
                    if wi_ % 2:
                        yield

                def prod(wl, wr, midx, sign, dst, dkey):
                    for hg in range(3):
                        B, Bv = bank6(n)
                        for hh in range(4):
                            h = hg * 4 + hh
                            fw.op("pe", lambda e_, h=h, hh=hh, Bv=Bv: e_.matmul(Bv[:, hh, :n], lhsT=XT[:, wl, h, :n], rhs=XT[:, wr, h, :n], start=True, stop=True), reads=[(kXT, wl), (kXT, wr)], writes=[k(B)])
                        hr = slice(hg * 4, (hg + 1) * 4)
                        acopy(raw[:n, hr, :n], Bv[:, :, :n], [k(B)], [("rwraw", hg)], scale=(sign if sign != 1.0 else None))
                        tt("pool", dst[:n, hr, :n], raw[:n, hr, :n], self.maskb[:n, midx, :n].unsqueeze(1).to_broadcast([n, 4, n]), ALU.mult, [("rwraw", hg), "maskb"], [dkey, (dkey, hg)])
                prod(2, 0, 2, -1.0, Pm["PT"], "rwMPT")
                prod(0, 2, 3, -1.0, Pm["P"], "rwMP")
                yield
                prod(3, 0, 2, 1.0, S["AkT"], k(S["AkT"]))
                prod(2, 1, 0, 1.0, S["BbT"], k(S["BbT"]))
                yield
                prod(3, 1, 0, 1.0, S["BkT"], k(S["BkT"]))
                WT, kWT = S["WT"], k(S["WT"])
                tt("dve", WT[:n, :, :n], Pm["PT"][:n, :, :n], self.idf[:n, :n].unsqueeze(1).to_broadcast([n, 12, n]), ALU.add, ["rwMPT", "idf"], allk(kWT))
                yield
                P, PT, Pn, PTn = "P", "PT", "Pn", "PTn"
                for lev in range(1, nlev):
                    for hg in range(3):
                        hr = slice(hg * 4, (hg + 1) * 4)
                        B, Bv = bank6(n)
                        for hh in range(4):
                            h = hg * 4 + hh
                            fw.op("pe", lambda e_, h=h, hh=hh, Bv=Bv, P=P, PT=PT: e_.matmul(Bv[:, hh, :n], lhsT=Pm[PT][:n, h, :n], rhs=Pm[P][:n, h, :n], start=True, stop=True), reads=allk("rwM" + P) + allk("rwM" + PT), writes=[k(B)])
                        acopy(Pm[Pn][:n, hr, :n], Bv[:, :, :n], [k(B)], [("rwM" + Pn, hg)])
                    yield
                    if lev < nlev - 1:
                        for hg in range(3):
                            hr = slice(hg * 4, (hg + 1) * 4)
                            B, Bv = bank6(n)
                            for hh in range(4):
                                h = hg * 4 + hh
                                fw.op("pe", lambda e_, h=h, hh=hh, Bv=Bv, P=P, PT=PT: e_.matmul(Bv[:, hh, :n], lhsT=Pm[P][:n, h, :n], rhs=Pm[PT][:n, h, :n], start=True, stop=True), reads=allk("rwM" + P) + allk("rwM" + PT), writes=[k(B)])
                            acopy(Pm[PTn][:n, hr, :n], Bv[:, :, :n], [k(B)], [("rwM" + PTn, hg)])
                        yield
                    for hg in range(3):
                        hr = slice(hg * 4, (hg + 1) * 4)
                        B, Bv = bank6(n)
                        for hh in range(4):
                            h = hg * 4 + hh
                            fw.op("pe", lambda e_, h=h, hh=hh, Bv=Bv, Pn=Pn: e_.matmul(Bv[:, hh, :n], lhsT=Pm[Pn][:n, h, :n], rhs=WT[:n, h, :n], start=True, stop=True), reads=[("rwM" + Pn, hg), (kWT, hg)], writes=[k(B)])
                        tt("dve", WT[:n, hr, :n], WT[:n, hr, :n], Bv[:, :, :n], ALU.add, [(kWT, hg), k(B)], [(kWT, hg)])
                    yield
                    P, Pn = Pn, P
                    PT, PTn = PTn, PT

            def solve(r0, n, SA, S):
                X4, XT, gC, vb, bon, sg = SA["X4"], S["XT"], SA["gC"], SA["vb"], SA["bon"], SA["sg"]
                kX4, kXT = k(X4), k(XT)
                AkT, BbT, BkT, WT = S["AkT"], S["BbT"], S["BkT"], S["WT"]
                for hg in range(3):
                    hr = slice(hg * 4, (hg + 1) * 4)
                    B, Bv = bank6(n)
                    for hh in range(4):
                        h = hg * 4 + hh
                        fw.op("pe", lambda e_, h=h, hh=hh, Bv=Bv: e_.matmul(Bv[:, hh, 0:64], lhsT=XT[:, 0, h, :n], rhs=Zb[:, h, :], start=True, stop=False), reads=[(kXT, 0), "rwZb"], writes=[k(B)])
                        fw.op("pe", lambda e_, h=h, hh=hh, Bv=Bv: e_.matmul(Bv[:, hh, 0:64], lhsT=AkT[:n, h, :n], rhs=vb[:n, h * 64:(h + 1) * 64], start=False, stop=True), reads=[k(AkT), k(vb)], writes=[k(B)])
                    if hg % 2:
                        acopy(RH[:n, hr, :], Bv[:, :, 0:64], [k(B)], [("rwRH", hg)])
                    else:
                        dcopy(RH[:n, hr, :], Bv[:, :, 0:64], [k(B)], [("rwRH", hg)])
                yield
                for hg in range(3):
                    hr = slice(hg * 4, (hg + 1) * 4)
                    B, Bv = bank6(n)
                    for hh in range(4):
                        h = hg * 4 + hh
                        fw.op("pe", lambda e_, h=h, hh=hh, Bv=Bv: e_.matmul(Bv[:, hh, 0:64], lhsT=WT[:n, h, :n], rhs=RH[:n, h, :], start=True, stop=True), reads=[k(WT), (k(WT), hg), ("rwRH", hg)], writes=[k(B)])
                    acopy(Un[:n, hr, :], Bv[:, :, 0:64], [k(B)], [("rwUn", hg)], scale=-1.0)
                yield
                for hg in range(3):
                    hr = slice(hg * 4, (hg + 1) * 4)
                    B = pb.next()
                    Bv = B[:64, 0:512].rearrange("p (h c) -> p h c", h=4)[:, :, 0:64]
                    B2, Bv2 = bank6(n)
                    for hh in range(4):
                        h = hg * 4 + hh
                        fw.op("pe", lambda e_, h=h, hh=hh, Bv2=Bv2: e_.matmul(Bv2[:, hh, 0:64], lhsT=XT[:, 1, h, :n], rhs=Zb[:, h, :], start=True, stop=False), reads=[(kXT, 1), "rwZb"], writes=[k(B2)])
                        fw.op("pe", lambda e_, h=h, hh=hh, Bv2=Bv2: e_.matmul(Bv2[:, hh, 0:64], lhsT=BbT[:n, h, :n], rhs=Un[:n, h, :], start=False, stop=False), reads=[k(BbT), ("rwUn", hg)], writes=[k(B2)])
                        fw.op("pe", lambda e_, h=h, hh=hh, Bv2=Bv2: e_.matmul(Bv2[:, hh, 0:64], lhsT=BkT[:n, h, :n], rhs=vb[:n, h * 64:(h + 1) * 64], start=False, stop=True), reads=[k(BkT), k(vb)], writes=[k(B2)])
                    for hh in range(4):
                        h = hg * 4 + hh
                        fw.op("pe", lambda e_, h=h, hh=hh, Bv=Bv: e_.matmul(Bv[:, hh, 0:64], lhsT=X4[:n, 2, h * 64:(h + 1) * 64], rhs=Un[:n, h, :], start=True, stop=False), reads=[(kX4, 2), ("rwUn", hg)], writes=[k(B)])
                        fw.op("pe", lambda e_, h=h, hh=hh, Bv=Bv: e_.matmul(Bv[:, hh, 0:64], lhsT=X4[:n, 3, h * 64:(h + 1) * 64], rhs=vb[:n, h * 64:(h + 1) * 64], start=False, stop=True), reads=[(kX4, 3), k(vb)], writes=[k(B)])
                    tt("dve", Z[:, hr, :], Z[:, hr, :], Bv, ALU.add, ["rwZ", k(B)], ["rwZ"])
                    tt("pool", Z[:, hr, :], Z[:, hr, :], gC[:, hr].unsqueeze(2).to_broadcast([64, 4, 64]), ALU.mult, ["rwZ", k(gC)], ["rwZ"])
                    acopy(Zb[:, hr, :], Z[:, hr, :], ["rwZ"], ["rwZb"])
                    dcopy(Y[:n, hg * 256:(hg + 1) * 256].rearrange("p (h c) -> p h c", h=4), Bv2[:, :, 0:64], [k(B2)], [("rwY", hg)])
                yield
                yk = [("rwY", 0), ("rwY", 1), ("rwY", 2)]
                b12 = lambda col: s12[:n, col, :].unsqueeze(2).to_broadcast([n, 12, 64])
                fw.op("dve", lambda e_: e_.tensor_reduce(out=s12[:n, 0, :], in_=v3(Y[:n, :]), axis=AX.X, op=ALU.add), reads=yk, writes=["rws12"])
                tt("pool", Y2[:n, :], Y[:n, :], Y[:n, :], ALU.mult, yk, ["rwY2"])
                fw.op("dve", lambda e_: e_.tensor_reduce(out=s12[:n, 1, :], in_=v3(Y2[:n, :]), axis=AX.X, op=ALU.add), reads=["rwY2"], writes=["rws12"])
                fw.op("dve", lambda e_: e_.tensor_scalar_mul(out=s12[:n, 0, :], in0=s12[:n, 0, :], scalar1=1.0 / 64), reads=["rws12"], writes=["rws12"])
                tt("dve", s12[:n, 2, :], s12[:n, 0, :], s12[:n, 0, :], ALU.mult, ["rws12"], ["rws12"])
                fw.op("dve", lambda e_: e_.scalar_tensor_tensor(out=s12[:n, 1, :], in0=s12[:n, 1, :], scalar=1.0 / 64, in1=s12[:n, 2, :], op0=ALU.mult, op1=ALU.subtract), reads=["rws12"], writes=["rws12"])
                fw.op("dve", lambda e_: e_.tensor_scalar_add(out=s12[:n, 1, :], in0=s12[:n, 1, :], scalar1=64e-5), reads=["rws12"], writes=["rws12"])
                fw.op("act", lambda e_: e_.sqrt(out=s12[:n, 1, :], in_=s12[:n, 1, :]), reads=["rws12"], writes=["rws12"])
                fw.op("dve", lambda e_: e_.reciprocal(out=s12[:n, 1, :], in_=s12[:n, 1, :]), reads=["rws12"], writes=["rws12"])
                yield
                tt("dve", v3(Y[:n, :]), v3(Y[:n, :]), b12(0), ALU.subtract, yk + ["rws12"], yk)
                tt("dve", v3(Y[:n, :]), v3(Y[:n, :]), b12(1), ALU.mult, yk + ["rws12"], yk)
                tt("pool", Y[:n, :], Y[:n, :], lg[:n, :], ALU.mult, yk + ["rwlg"], yk)
                tt("pool", Y[:n, :], Y[:n, :], lb[:n, :], ALU.add, yk + ["rwlb"], yk)
                tt("dve", Y[:n, :], Y[:n, :], bon[:n, :], ALU.add, yk + [k(bon)], yk)
                u_ = U.next()
                tt("pool", u_[:n, :], Y[:n, :], sg[:n, :], ALU.mult, yk + [k(sg)], [k(u_)])
                fw.dma("pool", self.u_d[r0:r0 + n, 0:A_W], u_[:n, :], reads=[k(u_)], writes=["D:u"])
                yield

            def drive(gens, ratios):
                gens = list(gens)
                alive = [g is not None for g in gens]
                while any(alive):
                    for gi, g in enumerate(gens):
                        for _ in range(ratios[gi]):
                            if alive[gi]:
                                try:
                                    next(g)
                                except StopIteration:
                                    alive[gi] = False

            for seg in range(2):
                if seg == 0:
                    C = min(128, SEQ)
                    chunks = [(r, C) for r in range(0, SEQ, C)]
                    fw.op("pool", lambda e_: e_.memset(Z[:], 0.0), writes=["rwZ"])
                    Sout = self.o_rwkv_p[e]
                else:
                    C = TS
                    chunks = [(SEQ, TS)]
                    Sout = self.o_rwkv_s[e]
                    fw.dma("sp", Ssb[:], self.st_rwkv[e].rearrange("h i j -> i h j"), writes=["rwS"])
                    for hg in range(3):
                        B = pb.next()
                        Bv = B[:64, 0:512].rearrange("p (h c) -> p h c", h=4)[:, :, 0:64]
                        for hh in range(4):
                            fw.op("pe", lambda e_, hh=hh, hg=hg, Bv=Bv: e_.transpose(out=Bv[:, hh, :], in_=Ssb[:, hg * 4 + hh, :], identity=self.idf[:64, :64]), reads=["rwS", "idf"], writes=[k(B)])
                        dcopy(Z[:, hg * 4:(hg + 1) * 4, :], Bv, [k(B)], ["rwZ"])
                fw.op("dve", lambda e_: e_.tensor_copy(out=Zb[:], in_=Z[:]), reads=["rwZ"], writes=["rwZb"])
                nlev = int(np.log2(C))
                nch = len(chunks)
                gA = lambda ci: stageA(ci, chunks[ci][0], chunks[ci][1], seg, setsA[ci % 3]) if ci < nch else None
                gB = lambda ci: stageB(chunks[ci][1], nlev, setsA[ci % 3], setsB[ci % 2]) if ci < nch else None
                gS = lambda ci: solve(chunks[ci][0], chunks[ci][1], setsA[ci % 3], setsB[ci % 2]) if 0 <= ci < nch else None
                drive([gA(0)], [1])
                drive([gA(1), gB(0)], [2, 4])
                for ci in range(nch):
                    drive([gA(ci + 2), gB(ci + 1), gS(ci)], [2, 4, 1])
                for hg in range(3):
                    B = pb.next()
                    Bv = B[:64, 0:512].rearrange("p (h c) -> p h c", h=4)[:, :, 0:64]
                    for hh in range(4):
                        fw.op("pe", lambda e_, hh=hh, hg=hg, Bv=Bv: e_.transpose(out=Bv[:, hh, :], in_=Z[:, hg * 4 + hh, :], identity=self.idf[:64, :64]), reads=["rwZ", "idf"], writes=[k(B)])
                    dcopy(Ssb[:, hg * 4:(hg + 1) * 4, :], Bv, [k(B)], ["rwS"])
                fw.dma("pool", Sout.rearrange("h i j -> i h j"), Ssb[:], reads=["rwS"], writes=["D:orw"])

    def odd_mixers(self, o):
        fw, k = self.fw, self.key
        SEQ = self.SEQ
        lg = [float(np.log(1.0 - 2.0 ** (-5.0 - h))) for h in range(C_H)]
        with self.phase() as es:
            sc = self.sb(es, "rtsc", [128, 2, C_H], F32)
            fw.dma("sp", sc[:], self.c_retsc, writes=["rtsc"])
            R = self.sb(es, "rtR", [128, C_H, 2, 256], F32)
            Rb = self.sb(es, "rtRb", [128, C_H, 2, 256], BF16)
            qkvg = self.sbn(es, "rtin", [128, 4, C_W], F32, 2)
            rot = self.sbn(es, "rtrot", [128, 2, 256], F32, 2)
            t1 = self.sb(es, "rtt1", [128, 12, 256], F32)
            t2 = self.sb(es, "rtt2", [128, 12, 256], F32)
            qkb = self.sbn(es, "rtqkb", [128, 12, 256], BF16, 2)
            vb = self.sbn(es, "rtvb", [128, C_W], BF16, 2)
            qT = self.sbn(es, "rtqT", [128, 12, 128], BF16, 2)
            kT = self.sbn(es, "rtkT", [128, 12, 128], BF16, 2)
            Sb = self.sbn(es, "rtSb", [128, C_H, 128], BF16, 2)
            sq = self.sb(es, "rtsq", [128, C_W], F32)
            ss = self.sb(es, "rtss", [128, C_H], F32)
            sgr = self.sbn(es, "rtsg", [128, C_W], F32, 2)
            yt = self.sb(es, "rtyt", [128, C_W], F32)
            ub = self.sbn(es, "rtub", [128, C_W], BF16, 2)
            pt = self.psn(es, "rtpt", [128, 4, 128], BF16, 1)
            pS = self.psn(es, "rtpS", [128, C_H, 128], F32, 1)
            pO = self.psn(es, "rtpO", [128, C_H, 256], F32, 1)
            pR = self.psn(es, "rtpR", [128, 2, 256], F32, 2)
            for seg in range(2):
                if seg == 0:
                    fw.op("pool", lambda e: e.memset(R[:], 0.0), writes=["rtR"])
                    chunks = self.tiles[:-1]
                    Rout = self.o_ret_p[o]
                else:
                    fw.dma("sp", R[:], self.st_ret[o].rearrange("h (c p) e -> p h c e", p=128), writes=["rtR"])
                    chunks = self.tiles[-1:]
                    Rout = self.o_ret_s[o]
                fw.op("dve", lambda e: e.tensor_copy(out=Rb[:], in_=R[:]), reads=["rtR"], writes=["rtRb"])
                def prefix(r0, n, T):
                    X, rt, QK, V, QT, KT, S_, U, sg = T
                    fw.dma("sp", X[:n, 0:2, :], self.h_d[r0:r0 + n, 0:2 * C_W].rearrange("p (a c) -> p a c", a=2), reads=["D:h"], writes=[(k(X), 0)])
                    fw.dma("sp", X[:n, 2:4, :], self.h_d[r0:r0 + n, 2 * C_W:4 * C_W].rearrange("p (a c) -> p a c", a=2), reads=["D:h"], writes=[(k(X), 1)])
                    fw.dma("sp", rt[:n], self.c_rot[r0:r0 + n], writes=[k(rt)])
                    x12 = X[:n, 0:2, :].rearrange("p a (h c) -> p (a h) c", h=C_H)
                    x12p = X[:n, 0:2, :].rearrange("p a (h c two) -> p (a h) c two", h=C_H, two=2)
                    t2p = t2[:n].rearrange("p a (c two) -> p a c two", two=2)
                    rtp = rt[:n].rearrange("p a (c two) -> p a c two", two=2)
                    fw.op("pool", lambda e, x12=x12, rt=rt: e.tensor_tensor(out=t1[:n], in0=x12, in1=rt[:n, 0, :].unsqueeze(1).to_broadcast([n, 12, 256]), op=ALU.mult),
                          reads=[(k(X), 0), k(rt)], writes=["rtt1"])
                    fw.op("dve", lambda e, x12p=x12p, t2p=t2p, rtp=rtp: e.tensor_tensor(out=t2p[:, :, :, 0], in0=x12p[:, :, :, 1], in1=rtp[:, 1, :, 0].unsqueeze(1).to_broadcast([n, 12, 128]), op=ALU.mult),
                          reads=[(k(X), 0), k(rt)], writes=[("rtt2", 0)])
                    fw.op("dve", lambda e, x12p=x12p, t2p=t2p, rtp=rtp: e.tensor_tensor(out=t2p[:, :, :, 1], in0=x12p[:, :, :, 0], in1=rtp[:, 1, :, 1].unsqueeze(1).to_broadcast([n, 12, 128]), op=ALU.mult),
                          reads=[(k(X), 0), k(rt)], writes=[("rtt2", 1)])
                    yield
                    fw.op("dve", lambda e: e.tensor_tensor(out=t1[:n], in0=t1[:n], in1=t2[:n], op=ALU.add), reads=["rtt1", ("rtt2", 0), ("rtt2", 1)], writes=["rtt1"])
                    fw.op("dve", lambda e, QK=QK: e.tensor_tensor(out=QK[:n], in0=t1[:n], in1=sc[:n].rearrange("p a h -> p (a h)").unsqueeze(2).to_broadcast([n, 12, 256]), op=ALU.mult),
                          reads=["rtt1", "rtsc"], writes=[k(QK)])
                    fw.op("act", lambda e, V=V, X=X: e.copy(out=V[:n, :], in_=X[:n, 2, :]), reads=[(k(X), 1)], writes=[k(V)])
                    qflat = QK[:, 0:6, :].rearrange("p h c -> p (h c)")
                    kflat = QK[:, 6:12, :].rearrange("p h c -> p (h c)")
                    yield
                    self.transposes(QT, qflat, 12, n, pt, k(QT), k(QK))
                    yield
                    self.transposes(KT, kflat, 12, n, pt, k(KT), k(QK))
                    yield
                    S = pS.next()
                    for h in range(C_H):
                        for c in range(2):
                            fw.op("pe", lambda e, h=h, c=c, S=S, KT=KT, QT=QT: e.matmul(S[:n, h, :n], lhsT=KT[:, h * 2 + c, :n], rhs=QT[:, h * 2 + c, :n], start=(c == 0), stop=(c == 1)),
                                  reads=[k(KT), k(QT)], writes=[k(S)])
                    fw.op("dve", lambda e, S=S, S_=S_: e.tensor_tensor(out=S_[:n, :, :n], in0=S[:n, :, :n], in1=self.maskf[:n, 0, :n].unsqueeze(1).to_broadcast([n, C_H, n]), op=ALU.mult),
                          reads=[k(S), "maskf"], writes=[k(S_)])
                    fw.op("act", lambda e, X=X, sg=sg: e.activation(out=sg[:n, :], in_=X[:n, 3, :], func=AF.Silu), reads=[(k(X), 1)], writes=[k(sg)])
                    yield

                def solve(r0, n, T):
                    X, rt, QK, V, QT, KT, S_, U, sg = T
                    O = pO.next()
                    for h in range(C_H):
                        fw.op("pe", lambda e, h=h, O=O, S_=S_, V=V: e.matmul(O[:n, h, :], lhsT=S_[:n, h, :n], rhs=V[:n, h * 256:(h + 1) * 256], start=True, stop=False),
                              reads=[k(S_), k(V)], writes=[k(O)])
                        for c in range(2):
                            fw.op("pe", lambda e, h=h, c=c, O=O, QT=QT: e.matmul(O[:n, h, :], lhsT=QT[:, h * 2 + c, :n], rhs=Rb[:, h, c, :], start=False, stop=(c == 1)),
                                  reads=[k(QT), "rtRb"], writes=[k(O)])
                    yield
                    for h in range(C_H):
                        dR = pR.next()
                        for c in range(2):
                            fw.op("pe", lambda e, h=h, c=c, dR=dR, QK=QK, V=V: e.matmul(dR[:, c, :], lhsT=QK[:n, 6 + h, c * 128:(c + 1) * 128], rhs=V[:n, h * 256:(h + 1) * 256], start=True, stop=True),
                                  reads=[k(QK), k(V)], writes=[k(dR)])
                        fw.op("dve", lambda e, h=h, dR=dR: e.tensor_tensor(out=R[:, h], in0=R[:, h], in1=dR[:], op=ALU.add), reads=[k(dR), "rtR"], writes=["rtR"])
                        gch = float(np.exp(lg[h] * n))
                        fw.op("act", lambda e, h=h, gch=gch: e.mul(out=Rb[:, h], in_=R[:, h], mul=gch), reads=["rtR"], writes=["rtRb"])
                        fw.op("act", lambda e, h=h, gch=gch: e.mul(out=R[:, h], in_=R[:, h], mul=gch), reads=["rtR"], writes=["rtR"])
                        if h % 2:
                            yield
                    yield
                    Of = O[:n].rearrange("p h c -> p (h c)")
                    fw.op("act", lambda e, Of=Of: e.activation(out=sq[:n, :], in_=Of, func=AF.Square), reads=[k(O)], writes=["rtsq"])
                    fw.op("dve", lambda e: e.tensor_reduce(out=ss[:n, :], in_=sq[:n, :].rearrange("p (h c) -> p h c", h=C_H), axis=AX.X, op=ALU.add), reads=["rtsq"], writes=["rtss"])
                    fw.op("dve", lambda e: e.tensor_scalar(out=ss[:n, :], in0=ss[:n, :], scalar1=1.0 / 256.0, scalar2=1e-6, op0=ALU.mult, op1=ALU.add), reads=["rtss"], writes=["rtss"])
                    fw.op("act", lambda e: e.sqrt(out=ss[:n, :], in_=ss[:n, :]), reads=["rtss"], writes=["rtss"])
                    fw.op("dve", lambda e: e.reciprocal(out=ss[:n, :], in_=ss[:n, :]), reads=["rtss"], writes=["rtss"])
                    fw.op("dve", lambda e, O=O: e.tensor_tensor(out=yt[:n, :].rearrange("p (h c) -> p h c", h=C_H), in0=O[:n], in1=ss[:n, :].unsqueeze(2).to_broadcast([n, C_H, 256]), op=ALU.mult),
                          reads=[k(O), "rtss"], writes=["rtyt"])
                    fw.op("pool", lambda e, U=U, sg=sg: e.tensor_tensor(out=U[:n, :], in0=yt[:n, :], in1=sg[:n, :], op=ALU.mult), reads=["rtyt", k(sg)], writes=[k(U)])
                    fw.dma("pool", self.u_d[r0:r0 + n, 0:C_W], U[:n, :], reads=[k(U)], writes=["D:u"])

                    yield

                def drive2(g1, g2):
                    al = [g1 is not None, g2 is not None]
                    gs = [g1, g2]
                    while any(al):
                        for gi in range(2):
                            if al[gi]:
                                try:
                                    next(gs[gi])
                                except StopIteration:
                                    al[gi] = False
                Ts = [(qkvg.next(), rot.next(), qkb.next(), vb.next(), qT.next(), kT.next(), Sb.next(), ub.next(), sgr.next()) for _ in chunks]
                drive2(prefix(chunks[0][0], chunks[0][1], Ts[0]), None)
                for ci in range(len(chunks)):
                    gp = prefix(chunks[ci + 1][0], chunks[ci + 1][1], Ts[ci + 1]) if ci + 1 < len(chunks) else None
                    drive2(gp, solve(chunks[ci][0], chunks[ci][1], Ts[ci]))
                fw.dma("pool", Rout.rearrange("h (c p) e -> p h c e", p=128), R[:], reads=["rtR"], writes=["D:oret"])


PAST_LEN = 16384


def _consts(SEQ):
    NT = SEQ + TS
    c = {}
    c["c_ident"] = np.eye(128, dtype=np.float32)
    pos = np.concatenate([np.arange(SEQ, dtype=np.float32), np.arange(TS, dtype=np.float32) + np.float32(PAST_LEN)])
    angle = (1.0 / (np.float32(10000.0) ** np.linspace(0.0, 1.0, 128, dtype=np.float32))).astype(np.float32)
    ph = (pos[:, None] * np.repeat(angle, 2)[None]).astype(np.float32)
    sgn = np.tile(np.array([-1.0, 1.0], np.float32), 128)
    c["c_rot"] = np.stack([np.cos(ph), np.sin(ph) * sgn[None]], axis=1).astype(np.float32)
    lg = np.log(1.0 - 2.0 ** (-5.0 - np.arange(C_H, dtype=np.float32))).astype(np.float32)
    idx = np.arange(128, dtype=np.float32)
    xi = np.exp(lg[None, :] * (idx[:, None] + 1.0))
    kz = np.exp(-lg[None, :] * (idx[:, None] + 1.0)) * (256.0 ** -0.5)
    c["c_retsc"] = np.stack([xi, kz], axis=1).astype(np.float32)
    j = np.arange(128)[:, None]
    i = np.arange(128)[None, :]
    c["c_masks"] = np.stack([(j <= i), (j >= i), (j < i), (j > i)], axis=1).astype(np.float32)
    return c


def make_in_maps(inp, SEQ, DEPTH, n_cores):
    LE, LO = (DEPTH + 1) // 2, DEPTH // 2
    cst = _consts(SEQ)
    f = lambda a: np.ascontiguousarray(np.asarray(a, dtype=np.float32))
    maps = []
    for c in range(n_cores):
        b = c % inp["x_prompt"].shape[0]
        m = {}
        m["x_in"] = f(np.concatenate([inp["x_prompt"][b, :SEQ], inp["x_sample"][c]], axis=0))
        m["memp"] = f(inp["mem_prompt"][b])
        m["st_rwkv"] = f(inp["state_rwkv"][:LE, c])
        m["st_shift"] = f(inp["state_rwkv_shift"][:LE, c])
        for g in range(3):
            a = inp[f"cache_dwa_g{g}"][:LE, c]
            m[f"c_g{g}"] = f(a.reshape(a.shape[0], a.shape[1], 512))
        m["st_ret"] = f(inp["state_ret"][:max(LO, 1), c])
        m["c_mem"] = f(inp["cache_mem_kv"][:DEPTH, c].reshape(DEPTH, 256, 1024))
        m["w_in_even"] = f(inp["w_in_even"][:LE])
        m["w_out_even"] = f(inp["w_out_even"][:LE])
        m["w_in_odd"] = f(inp["w_in_odd"][:max(LO, 1)])
        m["w_out_odd"] = f(inp["w_out_odd"][:max(LO, 1)])
        m["w_mem_kv"] = f(inp["w_mem_kv"][:DEPTH])
        m["ln_g"] = f(inp["ln_g"][:DEPTH])
        m["ln_b"] = f(inp["ln_b"][:DEPTH])
        for nm in ("mu", "w0", "w_up", "a0", "a_up", "k_k", "k_a", "lnx_g", "lnx_b"):
            m["rwkv_" + nm] = f(inp["rwkv_" + nm][:LE])
        m["rwkv_r_k"] = f(np.asarray(inp["rwkv_r_k"])[:LE].reshape(LE, A_W))
        m.update(cst)
        maps.append(m)
    return maps


_NC_CACHE = {}


def run(inp, SEQ, DEPTH, n_cores, debug=False, kinds=None):
    key = (SEQ, DEPTH, debug, str(kinds))
    if key not in _NC_CACHE:
        _NC_CACHE[key] = Builder(SEQ, DEPTH, debug=debug, kinds=kinds).build()
    nc = _NC_CACHE[key]
    maps = make_in_maps(inp, SEQ, DEPTH, n_cores)
    res = run_bass_kernel_spmd(nc, maps, core_ids=list(range(n_cores)))
    return res.results


def kernel(**inp):
    inp = {k: np.asarray(v) for k, v in inp.items()}
    SEQ, DEPTH = inp["x_prompt"].shape[1], 4
    BP = inp["x_prompt"].shape[0]
    NS = inp["x_sample"].shape[0]
    r = run(inp, SEQ, DEPTH, NS)
    LE, LO = 2, 2
    P = range(BP)
    S = range(NS)
    st = lambda name, cores: np.stack([r[c][name] for c in cores], axis=1)
    y_p = np.stack([r[b]["y"][:SEQ] for b in P], 0)
    y_s = np.stack([r[c]["y"][SEQ:] for c in S], 0)
    outs = [y_p, y_s, st("o_rwkv_p", P), st("o_rwkv_s", S), st("o_shift_p", P), st("o_shift_s", S)]
    for g in range(3):
        a = st(f"o_g{g}_p", P)
        outs.append(a.reshape(LE, BP, a.shape[2], 2, 4, 64))
        a = st(f"o_g{g}_s", S)
        outs.append(a.reshape(LE, NS, TS, 2, 4, 64))
    outs.append(st("o_ret_p", P))
    outs.append(st("o_ret_s", S))
    outs.append(st("o_mem_p", P).reshape(DEPTH, BP, 256, 2, 4, 128))
    return tuple(np.ascontiguousarray(o.astype(np.float32)) for o in outs)
```

```python
import contextlib
import os
import numpy as np
import concourse.bass as bass
import concourse.mybir as mybir
from concourse.bass_utils import run_bass_kernel_spmd

F32 = mybir.dt.float32
BF16 = mybir.dt.bfloat16
AF = mybir.ActivationFunctionType
ALU = mybir.AluOpType
AX = mybir.AxisListType

D = 2048
KC = D // 128
TS = 4
ALPHA_FULL_DEPTH = 4
LN_EPS = 1e-5
A_H, A_N = 12, 64
A_W = 768
A_SHIFT = 3 * A_W + 128
B_W = 768
B_OUT = 256
M_W = 512
C_H, C_D = 6, 256
C_W = 1536
EVEN_IN = A_SHIFT + A_W + 3 * B_W + B_OUT + 2 * M_W
EVEN_OUT = A_W + B_OUT + M_W
ODD_IN = 4 * C_W + 2 * M_W
ODD_OUT = C_W + M_W
B_CONFIGS = ((128, 1), (512, 4), (2048, 16))


class FW:
    def __init__(self, nc, n_dma_slots=14):
        self.nc = nc
        self.engs = {"pe": nc.tensor, "act": nc.scalar, "dve": nc.vector, "pool": nc.gpsimd, "sp": nc.sync}
        self.sem, self.cnt, self._ctx = {}, {}, []
        for e in ("pe", "act", "dve", "pool"):
            cm = nc.semaphore("s_" + e)
            self.sem[e] = cm.__enter__()
            self._ctx.append(cm)
            self.cnt[e] = 0
        self.slots = {}
        for q in ("sp", "act", "pool"):
            lst = []
            for i in range(n_dma_slots):
                cm = nc.semaphore(f"d_{q}{i}")
                s = cm.__enter__()
                self._ctx.append(cm)
                lst.append([s, 0])
            self.slots[q] = [lst, 0]
        self.known = {e: {} for e in self.engs}
        self.lastw, self.readers = {}, {}
        self.dw, self.dr = {}, {}
        self.n_inst = 0
        self.n_wait = 0

    @staticmethod
    def _isd(k):
        return isinstance(k, str) and k.startswith("D:") or (isinstance(k, tuple) and isinstance(k[0], str) and k[0].startswith("D:"))

    def _deps(self, reads, writes):
        deps = []
        for r in reads:
            if self._isd(r):
                deps.extend(self.dw.get(r, ()))
            elif r in self.lastw:
                deps.append(self.lastw[r])
        for w in writes:
            if self._isd(w):
                deps.extend(self.dr.get(w, ()))
            else:
                if w in self.lastw:
                    deps.append(self.lastw[w])
                deps.extend(self.readers.get(w, ()))
        return deps

    def _wait(self, e, deps, skip_sem=None):
        need = {}
        for (s, v) in deps:
            if skip_sem is not None and s is skip_sem:
                continue
            k = id(s)
            if k not in need or need[k][1] < v:
                need[k] = (s, v)
        kn = self.known[e]
        for k, (s, v) in need.items():
            if kn.get(k, 0) >= v:
                continue
            self.engs[e].wait_ge(s, v)
            self.n_wait += 1
            kn[k] = v

    def _record(self, tok, reads, writes):
        for r in reads:
            if self._isd(r):
                self.dr.setdefault(r, []).append(tok)
            else:
                self.readers.setdefault(r, []).append(tok)
        for w in writes:
            if self._isd(w):
                if self.dr.get(w):
                    self.dr[w] = []
                    self.dw[w] = []
                self.dw.setdefault(w, []).append(tok)
            else:
                self.lastw[w] = tok
                self.readers[w] = []

    def op(self, e, fn, reads=(), writes=()):
        deps = self._deps(reads, writes)
        self._wait(e, deps, skip_sem=self.sem["pe"] if e == "pe" else None)
        ins = fn(self.engs[e])
        self.cnt[e] += 1
        ins.then_inc(self.sem[e], 1)
        self.n_inst += 1
        tok = (self.sem[e], self.cnt[e])
        self._record(tok, reads, writes)
        return tok

    def dma(self, q, out, in_, reads=(), writes=()):
        deps = self._deps(reads, writes)
        lst, idx = self.slots[q]
        slot = lst[idx % len(lst)]
        self.slots[q][1] = idx + 1
        if slot[1] > 0:
            deps.append((slot[0], slot[1]))
        self._wait(q, deps)
        slot[1] += 16
        self.engs[q].dma_start(out=out, in_=in_).then_inc(slot[0], 16)
        self.n_inst += 1
        tok = (slot[0], slot[1])
        self._record(tok, reads, writes)
        return tok

    def all_toks(self):
        toks = []
        for e in ("pe", "act", "dve", "pool"):
            if self.cnt[e]:
                toks.append((self.sem[e], self.cnt[e]))
        for q in self.slots:
            for s, v in self.slots[q][0]:
                if v:
                    toks.append((s, v))
        return toks

    def barrier(self):
        if not hasattr(self, "bar"):
            cm = self.nc.semaphore("s_bar")
            self.bar = cm.__enter__()
            self._ctx.append(cm)
            self.barv = 0
        toks = self.all_toks()
        self._wait("sp", toks)
        self.barv += 1
        self.engs["sp"].sem_inc(self.bar, 1)
        for e in ("pe", "act", "dve", "pool"):
            self.engs[e].wait_ge(self.bar, self.barv)
            for (s_, v_) in toks:
                self.known[e][id(s_)] = max(self.known[e].get(id(s_), 0), v_)
        self.lastw, self.readers = {}, {}
        self.dw, self.dr = {}, {}

    def finish(self):
        toks = []
        for e in ("pe", "act", "dve", "pool"):
            if self.cnt[e]:
                toks.append((self.sem[e], self.cnt[e]))
        for q in self.slots:
            for s, v in self.slots[q][0]:
                if v:
                    toks.append((s, v))
        self._wait("sp", toks)

    def close(self):
        for cm in reversed(self._ctx):
            cm.__exit__(None, None, None)


class Rot:
    def __init__(self, tiles):
        self.tiles = tiles
        self.i = 0

    def next(self):
        t = self.tiles[self.i % len(self.tiles)]
        self.i += 1
        return t


class Builder:
    def __init__(self, SEQ, DEPTH, debug=False, kinds=None):
        self.SEQ, self.DEPTH, self.debug = SEQ, DEPTH, debug
        self.kinds = kinds or ["e" if l % 2 == 0 else "o" for l in range(DEPTH)]
        self.NT = SEQ + TS
        self.LE, self.LO = (DEPTH + 1) // 2, DEPTH // 2
        self.ALPHA = (2 * ALPHA_FULL_DEPTH) ** 0.25
        self.nc = nc = bass.Bass("TRN2", target_bir_lowering=False)
        self.fw = FW(nc)
        self.tiles = [(r, 128) for r in range(0, SEQ, 128)] + [(SEQ, TS)]
        self.keep = [min(w, SEQ) for w, _ in B_CONFIGS]
        self.io = {}
        self._cnt = 0

    @contextlib.contextmanager
    def phase(self):
        with contextlib.ExitStack() as es:
            yield es
            self.fw.barrier()

    def din(self, name, shape, dt=F32):
        t = self.nc.dram_tensor(name, list(shape), dt, kind="ExternalInput").ap()
        self.io[name] = t
        return t

    def dout(self, name, shape, dt=F32):
        t = self.nc.dram_tensor(name, list(shape), dt, kind="ExternalOutput").ap()
        self.io[name] = t
        return t

    def dscr(self, name, shape, dt=F32):
        kind = "ExternalOutput" if self.debug else "Internal"
        t = self.nc.dram_tensor(name, list(shape), dt, kind=kind).ap()
        self.io[name] = t
        return t

    def sb(self, es, name, shape, dt=F32):
        self._cnt += 1
        return es.enter_context(self.nc.sbuf_tensor(f"{name}_{self._cnt}", list(shape), dt))

    def ps(self, es, name, shape, dt=F32):
        self._cnt += 1
        t = es.enter_context(self.nc.psum_tensor(f"{name}_{self._cnt}", list(shape), dt))
        return t

    def sbn(self, es, name, shape, dt, n):
        return Rot([self.sb(es, f"{name}{i}", shape, dt) for i in range(n)])

    def psn(self, es, name, shape, dt, n):
        return Rot([self.ps(es, f"{name}{i}", shape, dt) for i in range(n)])

    @staticmethod
    def key(t):
        return t.name if hasattr(t, "name") else str(t)

    def declare(self):
        SEQ, NT, LE, LO, DEPTH = self.SEQ, self.NT, self.LE, self.LO, self.DEPTH
        d = self.din
        self.x_in = d("x_in", [NT, D])
        self.memp = d("memp", [256, D])
        self.st_rwkv = d("st_rwkv", [LE, A_H, 64, 64])
        self.st_shift = d("st_shift", [LE, A_SHIFT])
        self.c_g = [d(f"c_g{g}", [LE, B_CONFIGS[g][0], 512]) for g in range(3)]
        self.st_ret = d("st_ret", [max(LO, 1), C_H, 256, 256])
        self.c_mem = d("c_mem", [DEPTH, 256, 1024])
        self.w_in_even = d("w_in_even", [LE, D, EVEN_IN])
        self.w_out_even = d("w_out_even", [LE, EVEN_OUT, D])
        self.w_in_odd = d("w_in_odd", [max(LO, 1), D, ODD_IN])
        self.w_out_odd = d("w_out_odd", [max(LO, 1), ODD_OUT, D])
        self.w_mem_kv = d("w_mem_kv", [DEPTH, D, 1024])
        self.ln_g = d("ln_g", [DEPTH, D])
        self.ln_b = d("ln_b", [DEPTH, D])
        self.rw = {}
        for nm, shp in (("mu", [LE, A_SHIFT]), ("w0", [LE, A_W]), ("w_up", [LE, 64, A_W]), ("a0", [LE, A_W]),
                        ("a_up", [LE, 64, A_W]), ("k_k", [LE, A_W]), ("k_a", [LE, A_W]), ("r_k", [LE, A_W]),
                        ("lnx_g", [LE, A_W]), ("lnx_b", [LE, A_W])):
            self.rw[nm] = d("rwkv_" + nm, shp)
        self.c_ident = d("c_ident", [128, 128])
        self.c_rot = d("c_rot", [NT, 2, 256])
        self.c_retsc = d("c_retsc", [128, 2, C_H])
        self.c_masks = d("c_masks", [128, 4, 128])
        o = self.dout
        self.y = o("y", [NT, D])
        self.o_rwkv_p = o("o_rwkv_p", [LE, A_H, 64, 64])
        self.o_rwkv_s = o("o_rwkv_s", [LE, A_H, 64, 64])
        self.o_shift_p = o("o_shift_p", [LE, A_SHIFT])
        self.o_shift_s = o("o_shift_s", [LE, A_SHIFT])
        self.o_g_p = [o(f"o_g{g}_p", [LE, self.keep[g], 512]) for g in range(3)]
        self.o_g_s = [o(f"o_g{g}_s", [LE, TS, 512]) for g in range(3)]
        self.o_ret_p = o("o_ret_p", [max(LO, 1), C_H, 256, 256])
        self.o_ret_s = o("o_ret_s", [max(LO, 1), C_H, 256, 256])
        self.o_mem_p = o("o_mem_p", [DEPTH, 256, 1024])
        s = self.dscr
        self.xT_d = s("xT_d", [D, NT], BF16)
        self.memT_d = s("memT_d", [D, 256], BF16)
        self.h_d = s("h_d", [NT, ODD_IN])
        self.u_d = s("u_d", [NT, D], BF16)
        self.x_d = [s("x_d0", [NT, D]), s("x_d1", [NT, D])]
        self.dwa_d = s("dwa_d", [3, SEQ + TS, 260])

    def load_consts(self, es):
        fw = self.fw
        self.idf = self.sb(es, "idf", [128, 128], F32)
        self.idb = self.sb(es, "idb", [128, 128], BF16)
        fw.dma("sp", self.idf[:], self.c_ident, writes=["idf"])
        fw.op("dve", lambda e: e.tensor_copy(out=self.idb[:], in_=self.idf[:]), reads=["idf"], writes=["idb"])
        self.maskf = self.sb(es, "maskf", [128, 4, 128], F32)
        self.maskb = self.sb(es, "maskb", [128, 4, 128], BF16)
        fw.dma("sp", self.maskf[:], self.c_masks, writes=["maskf"])
        fw.op("dve", lambda e: e.tensor_copy(out=self.maskb[:], in_=self.maskf[:]), reads=["maskf"], writes=["maskb"])

    def transposes(self, dst, src, nblk, n, ptrot, dkey, skey, evac=("act", "dve")):
        fw = self.fw
        gi = 0
        for g0 in range(0, nblk, 4):
            gn = min(4, nblk - g0)
            pt = ptrot.next()
            pk = self.key(pt)
            for j in range(gn):
                k = g0 + j
                fw.op("pe", lambda e, k=k, j=j, pt=pt: e.transpose(out=pt[:, j, :n], in_=src[:n, k * 128:(k + 1) * 128], identity=self.idb[:n, :n]),
                      reads=[skey, "idb"], writes=[pk])
            en = evac[gi % len(evac)]
            gi += 1
            if en == "act":
                fw.op("act", lambda e, pt=pt, g0=g0, gn=gn: e.copy(out=dst[:, g0:g0 + gn, :n], in_=pt[:, :gn, :n]), reads=[pk], writes=[dkey])
            else:
                fw.op(en, lambda e, pt=pt, g0=g0, gn=gn: e.tensor_copy(out=dst[:, g0:g0 + gn, :n], in_=pt[:, :gn, :n]), reads=[pk], writes=[dkey])

    def phase_xT_from(self, src_dram, rows_tiles, dstT, dkeyname):
        fw = self.fw
        with self.phase() as es:
            xf = self.sbn(es, "p0xf", [128, D], F32, 2)
            xb = self.sbn(es, "p0xb", [128, D], BF16, 2)
            xT = self.sbn(es, "p0xT", [128, KC, 128], BF16, 2)
            pt = self.psn(es, "p0pt", [128, 4, 128], BF16, 2)
            for (r0, n) in rows_tiles:
                a, b_, c = xf.next(), xb.next(), xT.next()
                fw.dma("sp", a[:n, :], src_dram[r0:r0 + n, :], writes=[self.key(a)])
                fw.op("pool", lambda e, a=a, b_=b_: e.tensor_copy(out=b_[:n, :], in_=a[:n, :]), reads=[self.key(a)], writes=[self.key(b_)])
                self.transposes(c, b_, KC, n, pt, self.key(c), self.key(b_))
                fw.dma("pool", dstT.rearrange("(k p) t -> p k t", p=128)[:, :, r0:r0 + n], c[:, :, :n], reads=[self.key(c)], writes=[dkeyname])

    def proj(self, W, ncol, xT_src, xkey, blocks, dst, dkey):
        fw = self.fw
        with self.phase() as es:
            wst = self.sbn(es, "pjws", [128, KC // 2, 512], F32, 4)
            wbf = self.sbn(es, "pjwb", [128, KC, 512], BF16, 4)
            xTb = self.sbn(es, "pjx", [128, KC, 516], BF16, 2)
            ho = self.sbn(es, "pjho", [128, 512], F32, 4)
            pp = self.psn(es, "pjps", [128, 512], F32, 4)
            Wv = W.rearrange("(k p) n -> p k n", p=128)
            xv = xT_src.rearrange("(k p) t -> p k t", p=128)
            ei = 0
            h = KC // 2
            cgs = [(cg, min(512, ncol - cg)) for cg in range(0, ncol, 512)]
            def load_pair(pi):
                wbs = []
                for (cg, cw) in cgs[pi:pi + 2]:
                    wb = wbf.next()
                    for hf in range(2):
                        ws = wst.next()
                        fw.dma("sp" if hf == 0 else "act", ws[:, :, :cw], Wv[:, hf * h:(hf + 1) * h, cg:cg + cw], writes=[self.key(ws)])
                        for q in range(2):
                            en = "pool" if q % 2 else "dve"
                            k0 = hf * h + q * 4
                            fw.op(en, lambda e, q=q, ws=ws, wb=wb, k0=k0, cw=cw: e.tensor_copy(out=wb[:, k0:k0 + 4, :cw], in_=ws[:, q * 4:(q + 1) * 4, :cw]),
                                  reads=[self.key(ws)], writes=[(self.key(wb), k0 // 4)])
                    wbs.append((wb, cg, cw))
                return wbs

            nxt = load_pair(0)
            for pi in range(0, len(cgs), 2):
                wbs = nxt
                nxt = None
                for (c0, nb, tl) in blocks:
                    xt = xTb.next()
                    fw.dma("sp", xt[:, :, :nb], xv[:, :, c0:c0 + nb], reads=[xkey], writes=[self.key(xt)])
                    for (off, n, r0) in tl:
                        ps = [pp.next() for _ in wbs]
                        for k in range(KC):
                            for p, (wb, cg, cw) in zip(ps, wbs):
                                fw.op("pe", lambda e, k=k, p=p, xt=xt, wb=wb, off=off, n=n, cw=cw: e.matmul(p[:n, :cw], lhsT=xt[:, k, off:off + n], rhs=wb[:, k, :cw], start=(k == 0), stop=(k == KC - 1)),
                                      reads=[self.key(xt), (self.key(wb), k // 4)], writes=[self.key(p)])
                        for p, (wb, cg, cw) in zip(ps, wbs):
                            o = ho.next()
                            if ei % 2 == 0:
                                fw.op("act", lambda e, o=o, p=p, n=n, cw=cw: e.copy(out=o[:n, :cw], in_=p[:n, :cw]), reads=[self.key(p)], writes=[self.key(o)])
                            else:
                                fw.op("dve", lambda e, o=o, p=p, n=n, cw=cw: e.tensor_copy(out=o[:n, :cw], in_=p[:n, :cw]), reads=[self.key(p)], writes=[self.key(o)])
                            ei += 1
                            fw.dma("pool", dst[r0:r0 + n, cg:cg + cw], o[:n, :cw], reads=[self.key(o)], writes=[dkey])
                    if nxt is None and pi + 2 < len(cgs) and (c0, nb, tl) == blocks[min(1, len(blocks) - 1)]:
                        nxt = load_pair(pi + 2)

    def tok_blocks(self):
        SEQ = self.SEQ
        blocks = []
        for c0 in range(0, SEQ, 512):
            nb = min(512, SEQ - c0)
            tl = [(o, 128, c0 + o) for o in range(0, nb, 128)]
            blocks.append([c0, nb, tl])
        blocks[-1][1] += TS
        blocks[-1][2].append((blocks[-1][1] - TS, TS, SEQ))
        return [tuple(b) for b in blocks]

    def mem_prep(self, es, src, skey, name):
        fw = self.fw
        kT = self.sb(es, name + "kT", [128, 4, 256], BF16)
        va = self.sb(es, name + "va", [128, 2, 4, 129], BF16)
        with self.phase() as es2:
            kv = self.sb(es2, "mpkv", [128, 2, 1024], F32)
            kb = self.sb(es2, "mpkb", [128, 2, 512], BF16)
            pt = self.psn(es2, "mppt", [128, 4, 128], BF16, 2)
            fw.dma("sp", kv[:], src.rearrange("(c p) n -> p c n", p=128), reads=[skey], writes=["mpkv"])
            fw.op("dve", lambda e: e.tensor_copy(out=kb[:], in_=kv[:, :, 0:512]), reads=["mpkv"], writes=["mpkb"])
            fw.op("pool", lambda e: e.memset(va[:], 1.0), writes=[name + "va"])
            fw.op("pool", lambda e: e.tensor_copy(out=va[:, :, :, 0:128], in_=kv[:, :, 512:1024].rearrange("p c (h d) -> p c h d", h=4)),
                  reads=["mpkv"], writes=[name + "va"])
            for c in range(2):
                p = pt.next()
                for h in range(4):
                    fw.op("pe", lambda e, h=h, c=c, p=p: e.transpose(out=p[:, h, :], in_=kb[:, c, h * 128:(h + 1) * 128], identity=self.idb[:]),
                          reads=["mpkb", "idb"], writes=[self.key(p)])
                fw.op("act", lambda e, c=c, p=p: e.copy(out=kT[:, :, c * 128:(c + 1) * 128], in_=p[:]), reads=[self.key(p)], writes=[name + "kT"])
        return kT, va, name + "kT", name + "va"

    def mem_attn(self, tiles, kT, va, kTk, vak, qcol, ucol):
        fw = self.fw
        sc = 1.0 / np.sqrt(128.0)
        with self.phase() as es:
            qg = self.sbn(es, "maqg", [128, 1024], F32, 2)
            qb = self.sbn(es, "maqb", [128, 512], BF16, 2)
            qT = self.sbn(es, "maqT", [128, 4, 128], BF16, 2)
            pT = self.sbn(es, "mapT", [128, 8, 128], BF16, 2)
            rl = self.sbn(es, "marl", [128, 4], F32, 2)
            sg = self.sbn(es, "masg", [128, 512], F32, 2)
            om = self.sbn(es, "maom", [128, 512], F32, 2)
            ub = self.sbn(es, "maub", [128, 512], BF16, 2)
            pt = self.psn(es, "mapt", [128, 4, 128], BF16, 1)
            pss = self.psn(es, "mapss", [128, 8, 128], F32, 2)
            pso = self.psn(es, "mapso", [128, 4, 256], F32, 1)
            pend = [None]
            for (r0, n) in tiles:
                q, b_, t_, P, r_, s_, o_, u_ = qg.next(), qb.next(), qT.next(), pT.next(), rl.next(), sg.next(), om.next(), ub.next()
                k = self.key
                fw.dma("sp", q[:n, :], self.h_d[r0:r0 + n, qcol:qcol + 1024], reads=["D:h"], writes=[k(q)])
                fw.op("pool", lambda e, q=q, b_=b_: e.tensor_copy(out=b_[:n, :], in_=q[:n, 0:512]), reads=[k(q)], writes=[k(b_)])
                self.transposes(t_, b_, 4, n, pt, k(t_), k(b_), evac=("dve",))
                S = pss.next()
                for h in range(4):
                    for mc in range(2):
                        fw.op("pe", lambda e, h=h, mc=mc, S=S, t_=t_: e.matmul(S[:, h * 2 + mc, :n], lhsT=kT[:, h, mc * 128:(mc + 1) * 128], rhs=t_[:, h, :n], start=True, stop=True),
                              reads=[kTk, k(t_)], writes=[k(S)])
                fw.op("act", lambda e, S=S, P=P: e.activation(out=P[:, :, :n], in_=S[:, :, :n], func=AF.Exp, scale=float(sc)), reads=[k(S)], writes=[k(P)])
                fw.op("act", lambda e, q=q, s_=s_: e.activation(out=s_[:n, :], in_=q[:n, 512:1024], func=AF.Silu), reads=[k(q)], writes=[k(s_)])

                def part2(r0=r0, n=n, P=P, r_=r_, s_=s_, o_=o_, u_=u_):
                    O = pso.next()
                    for h in range(4):
                        for mc in range(2):
                            fw.op("pe", lambda e, h=h, mc=mc, O=O: e.matmul(O[:n, h, 0:129], lhsT=P[:, h * 2 + mc, :n], rhs=va[:, mc, h, :], start=(mc == 0), stop=(mc == 1)),
                                  reads=[vak, k(P)], writes=[k(O)])
                    fw.op("dve", lambda e, O=O: e.reciprocal(out=r_[:n, :], in_=O[:n, :, 128]), reads=[k(O)], writes=[k(r_)])
                    fw.op("dve", lambda e, O=O: e.tensor_tensor(out=o_[:n, :].rearrange("p (h d) -> p h d", h=4), in0=O[:n, :, 0:128],
                                                                in1=r_[:n, :].unsqueeze(2).to_broadcast([n, 4, 128]), op=ALU.mult),
                          reads=[k(O), k(r_)], writes=[k(o_)])
                    fw.op("pool", lambda e: e.tensor_tensor(out=u_[:n, :], in0=o_[:n, :], in1=s_[:n, :], op=ALU.mult), reads=[k(o_), k(s_)], writes=[k(u_)])
                    fw.dma("pool", self.u_d[r0:r0 + n, ucol:ucol + 512], u_[:n, :], reads=[k(u_)], writes=["D:u"])
                if pend[0] is not None:
                    pend[0]()
                pend[0] = part2
            if pend[0] is not None:
                pend[0]()

    def out_ln(self, l, Wout, UW, x_src, xskey, last):
        fw = self.fw
        nk = UW // 128
        SD, FM, AD = self.nc.vector.BN_STATS_DIM, self.nc.vector.BN_STATS_FMAX, self.nc.vector.BN_AGGR_DIM
        nch = D // FM
        with self.phase() as es:
            wo = self.sb(es, "olwo", [128, nk, D], BF16)
            wst = self.sbn(es, "olws", [128, D], F32, 2)
            gt = self.sb(es, "olg", [128, D], F32)
            bt = self.sb(es, "olb", [128, D], F32)
            fw.dma("sp", gt[:], self.ln_g[l].partition_broadcast(128), writes=["olg"])
            fw.dma("sp", bt[:], self.ln_b[l].partition_broadcast(128), writes=["olb"])
            for kk in range(nk):
                w = wst.next()
                fw.dma("sp" if kk % 2 else "act", w[:], Wout[kk * 128:(kk + 1) * 128, :], writes=[self.key(w)])
                fw.op("pool" if kk % 2 else "dve", lambda e, w=w, kk=kk: e.tensor_copy(out=wo[:, kk, :], in_=w[:]), reads=[self.key(w)], writes=[("olwo", kk)])
            wkeys = [("olwo", kk) for kk in range(nk)]
            ut = self.sbn(es, "olu", [128, UW], BF16, 2)
            uT = self.sbn(es, "oluT", [128, nk, 128], BF16, 2)
            xt = self.sbn(es, "olx", [128, D], F32, 2)
            zt = self.sbn(es, "olz", [128, D], F32, 3)
            st = self.sbn(es, "olst", [128, nch, SD], F32, 2)
            mv = self.sbn(es, "olmv", [128, AD], F32, 2)
            rs = self.sbn(es, "olrs", [128, 1], F32, 2)
            xb = self.sbn(es, "olxb", [128, D], BF16, 2)
            xT = self.sbn(es, "olxT", [128, KC, 128], BF16, 2)
            pt = self.psn(es, "olpt", [128, 4, 128], BF16, 2)
            po = self.psn(es, "olpo", [128, 4, 512], F32, 1)
            k = self.key
            pend = None

            def tail(r0, n, z):
                b_, T_ = xb.next(), xT.next()
                fw.op("act", lambda e, b_=b_, z=z: e.copy(out=b_[:n, :], in_=z[:n, :]), reads=[k(z)], writes=[k(b_)])
                self.transposes(T_, b_, KC, n, pt, k(T_), k(b_))
                fw.dma("pool", self.xT_d.rearrange("(k p) t -> p k t", p=128)[:, :, r0:r0 + n], T_[:, :, :n], reads=[k(T_)], writes=["D:xT"])

            for (r0, n) in self.tiles:
                u, uT_, x_, z, s_, m_, r_ = ut.next(), uT.next(), xt.next(), zt.next(), st.next(), mv.next(), rs.next()
                fw.dma("sp", u[:n, :], self.u_d[r0:r0 + n, 0:UW], reads=["D:u"], writes=[k(u)])
                fw.dma("sp", x_[:n, :], x_src[r0:r0 + n, :], reads=[xskey], writes=[k(x_)])
                self.transposes(uT_, u, nk, n, pt, k(uT_), k(u))
                P = po.next()
                for kk in range(nk):
                    for nb in range(4):
                        fw.op("pe", lambda e, nb=nb, kk=kk, P=P, uT_=uT_: e.matmul(P[:n, nb, :], lhsT=uT_[:, kk, :n], rhs=wo[:, kk, nb * 512:(nb + 1) * 512], start=(kk == 0), stop=(kk == nk - 1)),
                              reads=[k(uT_), ("olwo", kk)], writes=[k(P)])
                if pend is not None:
                    tail(*pend)
                    pend = None
                fw.op("dve", lambda e, z=z, x_=x_, P=P: e.scalar_tensor_tensor(out=z[:n, :], in0=x_[:n, :], scalar=float(self.ALPHA), in1=P[:n].rearrange("p a b -> p (a b)"), op0=ALU.mult, op1=ALU.add),
                      reads=[k(x_), k(P)], writes=[k(z)])
                for c in range(nch):
                    fw.op("dve", lambda e, c=c, s_=s_, z=z: e.bn_stats(out=s_[:n, c, :], in_=z[:n, c * FM:(c + 1) * FM]), reads=[k(z)], writes=[k(s_)])
                fw.op("dve", lambda e, s_=s_, m_=m_: e.bn_aggr(out=m_[:n, :], in_=s_[:n]), reads=[k(s_)], writes=[k(m_)])
                fw.op("dve", lambda e, r_=r_, m_=m_: e.tensor_scalar_add(out=r_[:n, :], in0=m_[:n, 1:2], scalar1=LN_EPS), reads=[k(m_)], writes=[k(r_)])
                fw.op("act", lambda e, r_=r_: e.sqrt(out=r_[:n, :], in_=r_[:n, :]), reads=[k(r_)], writes=[k(r_)])
                fw.op("dve", lambda e, r_=r_: e.reciprocal(out=r_[:n, :], in_=r_[:n, :]), reads=[k(r_)], writes=[k(r_)])
                fw.op("dve", lambda e, z=z, m_=m_, r_=r_: e.tensor_scalar(out=z[:n, :], in0=z[:n, :], scalar1=m_[:n, 0:1], scalar2=r_[:n, 0:1], op0=ALU.subtract, op1=ALU.mult),
                      reads=[k(z), k(m_), k(r_)], writes=[k(z)])
                fw.op("pool", lambda e, z=z: e.tensor_tensor(out=z[:n, :], in0=z[:n, :], in1=gt[:n, :], op=ALU.mult), reads=[k(z), "olg"], writes=[k(z)])
                fw.op("pool", lambda e, z=z: e.tensor_tensor(out=z[:n, :], in0=z[:n, :], in1=bt[:n, :], op=ALU.add), reads=[k(z), "olb"], writes=[k(z)])
                if last:
                    fw.dma("pool", self.y[r0:r0 + n, :], z[:n, :], reads=[k(z)], writes=["D:y"])
                else:
                    fw.dma("pool", self.x_d[l % 2][r0:r0 + n, :], z[:n, :], reads=[k(z)], writes=[f"D:x{l % 2}"])
                    pend = (r0, n, z)
            if pend is not None:
                tail(*pend)

    def build(self, stop_after=None):
        fw = self.fw
        self.declare()
        with self.phase() as es:
            self.load_consts(es)
            self.phase_xT_from(self.x_in, self.tiles, self.xT_d, "D:xT")
            self.phase_xT_from(self.memp, [(0, 128), (128, 128)], self.memT_d, "D:memT")
            blocks = self.tok_blocks()
            mblocks = [(0, 256, [(0, 128, 0), (128, 128, 128)])]
            for l in range(self.DEPTH):
                even = self.kinds[l] == "e"
                i2 = sum(1 for q in self.kinds[:l] if q == self.kinds[l])
                W = self.w_in_even[i2] if even else self.w_in_odd[i2]
                ncol = EVEN_IN if even else ODD_IN
                self.proj(W, ncol, self.xT_d, "D:xT", blocks, self.h_d, "D:h")
                self.proj(self.w_mem_kv[l], 1024, self.memT_d, "D:memT", mblocks, self.o_mem_p[l], "D:omem")
                qcol = ncol - 1024
                ucol = (EVEN_OUT if even else ODD_OUT) - 512
                with self.phase() as es2:
                    kT, va, kTk, vak = self.mem_prep(es2, self.o_mem_p[l], "D:omem", "mp")
                    self.mem_attn(self.tiles[:-1], kT, va, kTk, vak, qcol, ucol)
                with self.phase() as es2:
                    kT, va, kTk, vak = self.mem_prep(es2, self.c_mem[l], "D:cmem", "ms")
                    self.mem_attn(self.tiles[-1:], kT, va, kTk, vak, qcol, ucol)
                if even:
                    self.even_mixers(i2)
                else:
                    self.odd_mixers(i2)
                x_src, xk = (self.x_in, "D:xin") if l == 0 else (self.x_d[(l - 1) % 2], f"D:x{(l - 1) % 2}")
                self.out_ln(l, self.w_out_even[i2] if even else self.w_out_odd[i2], EVEN_OUT if even else ODD_OUT, x_src, xk, l == self.DEPTH - 1)
            fw.finish()
        fw.close()
        return self.nc

    def even_mixers(self, e):
        self.dwa(e)
        self.rwkv(e)

    def dwa(self, e):
        fw, k = self.fw, self.key
        SEQ = self.SEQ
        QC, KCOL, VC, GB = 3200, 3968, 4736, 5504
        hp_ = self.h_d[0:SEQ, :]
        for g in range(3):
            kp = self.keep[g]
            for j, col in enumerate((KCOL + g * 256, VC + g * 256)):
                fw.dma("act", self.o_g_p[g][e][:, j * 256:(j + 1) * 256], self.h_d[SEQ - kp:SEQ, col:col + 256], reads=["D:h"], writes=["D:ogp"])
                fw.dma("act", self.o_g_s[g][e][:, j * 256:(j + 1) * 256], self.h_d[SEQ:SEQ + TS, col:col + 256], reads=["D:h"], writes=["D:ogs"])
        fw.dma("act", self.o_shift_p[e], self.h_d[SEQ - 1, 0:A_SHIFT], reads=["D:h"], writes=["D:osh"])
        fw.dma("act", self.o_shift_s[e], self.h_d[SEQ + TS - 1, 0:A_SHIFT], reads=["D:h"], writes=["D:osh"])
        units = []
        for g, (win, d) in enumerate(B_CONFIGS):
            qc, kc, vc = QC + g * 256, KCOL + g * 256, VC + g * 256
            Lc = SEQ // d
            n = min(128, Lc)
            hv = hp_.rearrange("(m d) c -> d m c", d=d)
            ov = self.dwa_d[g][0:SEQ, :].rearrange("(m d) c -> d m c", d=d)
            for r in range(d):
                for m0 in range(0, Lc, n):
                    cur = hv[r, m0:m0 + n]
                    prev = hv[r, m0 - n:m0] if m0 > 0 else None
                    units.append((cur[:, qc:qc + 256], cur[:, kc:kc + 256], cur[:, vc:vc + 256], n,
                                  None if prev is None else prev[:, kc:kc + 256], None if prev is None else prev[:, vc:vc + 256], n, ov[r, m0:m0 + n], "D:h"))
            cg = self.c_g[g][e]
            if d == 1:
                cur = self.h_d[SEQ:SEQ + TS]
                units.append((cur[:, qc:qc + 256], cur[:, kc:kc + 256], cur[:, vc:vc + 256], TS, cg[:, 0:256], cg[:, 256:512], 128, self.dwa_d[g][SEQ:SEQ + TS, :], "D:cg"))
            else:
                cv = cg.rearrange("(m d) c -> d m c", d=d)
                for i in range(TS):
                    cur = self.h_d[SEQ + i:SEQ + i + 1]
                    units.append((cur[:, qc:qc + 256], cur[:, kc:kc + 256], cur[:, vc:vc + 256], 1, cv[i][:, 0:256], cv[i][:, 256:512], 128, self.dwa_d[g][SEQ + i:SEQ + i + 1, :], "D:cg"))
        import os
        bis = os.environ.get("DWA_BIS", "")
        if bis == "none":
            units = []
        elif bis == "prompt":
            units = [u for u in units if u[8] == "D:h"]
        elif bis == "g0":
            units = units[:4]
        elif bis == "samp":
            units = [u for u in units if u[8] != "D:h"]
        with self.phase() as es:
            Xc = self.sbn(es, "dwXc", [128, 3, 256], F32, 2)
            Xp = self.sbn(es, "dwXp", [128, 2, 256], F32, 2)
            qkb = self.sbn(es, "dwqkb", [128, 3, 256], BF16, 2)
            Va = self.sbn(es, "dwVa", [128, 2, 4, 65], BF16, 2)
            T = self.sbn(es, "dwT", [64, 3, 4, 128], BF16, 2)
            P = self.sbn(es, "dwP", [128, 4, 2, 128], BF16, 2)
            Os = self.sbn(es, "dwOs", [128, 260], F32, 2)
            pt = self.psn(es, "dwpt", [128, 4, 128], BF16, 2)
            pS = self.psn(es, "dwpS", [128, 4, 2, 128], F32, 2)
            pO = self.psn(es, "dwpO", [128, 4, 128], F32, 2)
            pend2 = [None]
            for (qs, ks, vs, n, pks, pvs, npv, dst, pkey) in units:
                xc, xp, qb, va, t, p, os_ = Xc.next(), Xp.next(), qkb.next(), Va.next(), T.next(), P.next(), Os.next()
                hasp = pks is not None
                fw.dma("sp", xc[:n, 0, :], qs, reads=["D:h"], writes=[(k(xc), 0)])
                fw.dma("sp", xc[:n, 1, :], ks, reads=["D:h"], writes=[(k(xc), 1)])
                fw.dma("sp", xc[:n, 2, :], vs, reads=["D:h"], writes=[(k(xc), 2)])
                fw.op("pool", lambda e_, va=va: e_.memset(va[:], 1.0), writes=[k(va)])
                fw.op("dve", lambda e_, xc=xc, qb=qb: e_.tensor_copy(out=qb[:n, 0:2, :], in_=xc[:n, 0:2, :]), reads=[(k(xc), 0), (k(xc), 1)], writes=[(k(qb), 0)])
                fw.op("pool", lambda e_, xc=xc, va=va: e_.tensor_copy(out=va[:n, 0, :, 0:64], in_=xc[:n, 2, :].rearrange("p (h c) -> p h c", h=4)), reads=[(k(xc), 2)], writes=[k(va)])
                if hasp:
                    fw.dma("sp", xp[:npv, 0, :], pks, reads=[pkey], writes=[(k(xp), 0)])
                    fw.dma("sp", xp[:npv, 1, :], pvs, reads=[pkey], writes=[(k(xp), 1)])
                    fw.op("dve", lambda e_, xp=xp, qb=qb: e_.tensor_copy(out=qb[:npv, 2, :], in_=xp[:npv, 0, :]), reads=[(k(xp), 0)], writes=[(k(qb), 1)])
                    fw.op("pool", lambda e_, xp=xp, va=va: e_.tensor_copy(out=va[:npv, 1, :, 0:64], in_=xp[:npv, 1, :].rearrange("p (h c) -> p h c", h=4)), reads=[(k(xp), 1)], writes=[k(va)])
                stage = int(os.environ.get("DWA_STAGE", "9"))
                if stage < 2:
                    continue
                for w_, (nn, rk) in enumerate(((n, 0), (n, 0), (npv, 1))):
                    if w_ == 2 and not hasp:
                        continue
                    pp_ = pt.next()
                    for b_ in range(4):
                        fw.op("pe", lambda e_, w_=w_, b_=b_, pp_=pp_, qb=qb, nn=nn: e_.transpose(out=pp_[0:64, b_, :nn], in_=qb[:nn, w_, b_ * 64:(b_ + 1) * 64], identity=self.idb[:nn, :nn]),
                              reads=[(k(qb), rk), "idb"], writes=[k(pp_)])
                    fw.op("act" if w_ % 2 else "dve", (lambda e_, w_=w_, pp_=pp_, t=t, nn=nn: e_.copy(out=t[:, w_, :, :nn], in_=pp_[0:64, :, :nn])) if w_ % 2 else
                          (lambda e_, w_=w_, pp_=pp_, t=t, nn=nn: e_.tensor_copy(out=t[:, w_, :, :nn], in_=pp_[0:64, :, :nn])), reads=[k(pp_)], writes=[(k(t), w_)])
                if stage < 3:
                    continue
                S = pS.next()
                for h in range(4):
                    fw.op("pe", lambda e_, h=h, S=S, t=t: e_.matmul(S[:n, h, 0, :n], lhsT=t[:, 1, h, :n], rhs=t[:, 0, h, :n], start=True, stop=True),
                          reads=[(k(t), 0), (k(t), 1)], writes=[k(S)])
                    if hasp:
                        fw.op("pe", lambda e_, h=h, S=S, t=t: e_.matmul(S[:npv, h, 1, :n], lhsT=t[:, 2, h, :npv], rhs=t[:, 0, h, :n], start=True, stop=True),
                              reads=[(k(t), 0), (k(t), 2)], writes=[k(S)])
                if stage < 4:
                    continue
                fw.op("act", lambda e_, S=S, p=p: e_.activation(out=p[:n, :, 0, :n], in_=S[:n, :, 0, :n], func=AF.Exp, scale=0.125), reads=[k(S)], writes=[(k(p), 0)])
                fw.op("pool", lambda e_, p=p: e_.tensor_tensor(out=p[:n, :, 0, :n], in0=p[:n, :, 0, :n], in1=self.maskb[:n, 0, :n].unsqueeze(1).to_broadcast([n, 4, n]), op=ALU.mult),
                      reads=[(k(p), 0), "maskb"], writes=[(k(p), 0)])
                if hasp:
                    fw.op("act", lambda e_, S=S, p=p: e_.activation(out=p[:npv, :, 1, :n], in_=S[:npv, :, 1, :n], func=AF.Exp, scale=0.125), reads=[k(S)], writes=[(k(p), 1)])
                    fw.op("dve", lambda e_, p=p: e_.tensor_tensor(out=p[:npv, :, 1, :n], in0=p[:npv, :, 1, :n], in1=self.maskb[:npv, 1, :n].unsqueeze(1).to_broadcast([npv, 4, n]), op=ALU.mult),
                          reads=[(k(p), 1), "maskb"], writes=[(k(p), 1)])
                def part2(n=n, npv=npv, hasp=hasp, p=p, va=va, os_=os_, dst=dst):
                    O = pO.next()
                    for h in range(4):
                        fw.op("pe", lambda e_, h=h, O=O: e_.matmul(O[:n, h, 0:65], lhsT=p[:n, h, 0, :n], rhs=va[:n, 0, h, :], start=True, stop=not hasp),
                              reads=[(k(p), 0), k(va)], writes=[k(O)])
                        if hasp:
                            fw.op("pe", lambda e_, h=h, O=O: e_.matmul(O[:n, h, 0:65], lhsT=p[:npv, h, 1, :n], rhs=va[:npv, 1, h, :], start=False, stop=True),
                                  reads=[(k(p), 1), k(va)], writes=[k(O)])
                    fw.op("act", lambda e_, O=O: e_.copy(out=os_[:n, :].rearrange("p (h c) -> p h c", h=4), in_=O[:n, :, 0:65]), reads=[k(O)], writes=[k(os_)])
                    fw.dma("pool", dst, os_[:n, :], reads=[k(os_)], writes=["D:dwa"])
                if pend2[0] is not None:
                    pend2[0]()
                pend2[0] = part2
            if pend2[0] is not None:
                pend2[0]()
        with self.phase() as es:
            A = self.sbn(es, "dcA", [128, 3, 260], F32, 2)
            G = self.sbn(es, "dcG", [128, 256], F32, 2)
            rl = self.sbn(es, "dcrl", [128, 4], F32, 2)
            Y = self.sbn(es, "dcY", [128, 256], F32, 2)
            U = self.sbn(es, "dcU", [128, 256], BF16, 2)
            for (r0, n) in self.tiles:
                a, g_, r_, y_, u_ = A.next(), G.next(), rl.next(), Y.next(), U.next()
                fw.dma("sp", a[:n], self.dwa_d[:, r0:r0 + n, :].rearrange("g p c -> p g c"), reads=["D:dwa"], writes=[k(a)])
                fw.dma("sp", g_[:n, :], self.h_d[r0:r0 + n, GB:GB + 256], reads=["D:h"], writes=[k(g_)])
                fw.op("dve", lambda e_, a=a: e_.tensor_tensor(out=a[:n, 0, :], in0=a[:n, 0, :], in1=a[:n, 1, :], op=ALU.add), reads=[k(a)], writes=[k(a)])
                fw.op("dve", lambda e_, a=a: e_.tensor_tensor(out=a[:n, 0, :], in0=a[:n, 0, :], in1=a[:n, 2, :], op=ALU.add), reads=[k(a)], writes=[k(a)])
                a4 = a[:n, 0, :].rearrange("p (h c) -> p h c", h=4)
                fw.op("dve", lambda e_, a4=a4, r_=r_: e_.reciprocal(out=r_[:n, :], in_=a4[:, :, 64]), reads=[k(a)], writes=[k(r_)])
                fw.op("act", lambda e_, g_=g_: e_.activation(out=g_[:n, :], in_=g_[:n, :], func=AF.Silu), reads=[k(g_)], writes=[k(g_)])
                fw.op("dve", lambda e_, a4=a4, r_=r_, y_=y_: e_.tensor_tensor(out=y_[:n, :].rearrange("p (h c) -> p h c", h=4), in0=a4[:, :, 0:64], in1=r_[:n, :].unsqueeze(2).to_broadcast([n, 4, 64]), op=ALU.mult),
                      reads=[k(a), k(r_)], writes=[k(y_)])
                fw.op("pool", lambda e_, y_=y_, g_=g_, u_=u_: e_.tensor_tensor(out=u_[:n, :], in0=y_[:n, :], in1=g_[:n, :], op=ALU.mult), reads=[k(y_), k(g_)], writes=[k(u_)])
                fw.dma("pool", self.u_d[r0:r0 + n, A_W:A_W + 256], u_[:n, :], reads=[k(u_)], writes=["D:u"])

    def rwkv(self, e):
        fw, k = self.fw, self.key
        SEQ = self.SEQ
        NEG = -float(np.exp(-0.5))
        with self.phase() as es:
            def bc(name, src, w):
                t = self.sb(es, name, [128, w], F32)
                fw.dma("sp", t[:], src.partition_broadcast(128), writes=[name])
                return t
            mu = bc("rwmu", self.rw["mu"][e], A_SHIFT)
            w0 = bc("rww0", self.rw["w0"][e], A_W)
            a0 = bc("rwa0", self.rw["a0"][e], A_W)
            k_k = bc("rwkk", self.rw["k_k"][e], A_W)
            k_a = bc("rwka", self.rw["k_a"][e], A_W)
            r_k = bc("rwrk", self.rw["r_k"][e], A_W)
            lg = bc("rwlg", self.rw["lnx_g"][e], A_W)
            lb = bc("rwlb", self.rw["lnx_b"][e], A_W)
            wup = self.sb(es, "rwwup", [64, 2, A_W], F32)
            fw.dma("sp", wup[:, 0, :], self.rw["w_up"][e], writes=["rwwup"])
            fw.dma("sp", wup[:, 1, :], self.rw["a_up"][e], writes=["rwwup"])
            ones = self.sb(es, "rwones", [128, 1], F32)
            fw.op("pool", lambda e_: e_.memset(ones[:], 1.0), writes=["rwones"])
            H = self.sbn(es, "rwH", [128, A_SHIFT], F32, 1)
            hs = self.sb(es, "rwhs", [128, A_SHIFT], F32)
            lT = self.sb(es, "rwlT", [64, 2, 128], F32)
            sw = self.sb(es, "rwsw", [128, A_W], F32)
            av = self.sb(es, "rwa", [128, A_W], F32)
            kk = self.sb(es, "rwkkv", [128, A_W], F32)
            tmp = self.sb(es, "rwtmp", [128, A_W], F32)
            tmp2 = self.sb(es, "rwtmp2", [128, A_W], F32)
            s12p = self.sb(es, "rws12p", [128, 2, 12], F32)
            kmod = self.sb(es, "rwkmod", [128, A_W], F32)
            cs = self.sb(es, "rwcs", [128, A_W], F32)
            E3 = self.sb(es, "rwE3", [128, 3, A_W], BF16)
            raw = self.sb(es, "rwraw", [128, 12, 128], BF16)
            Pm = {nm: self.sb(es, "rwM" + nm, [128, 12, 128], BF16) for nm in ("P", "PT", "Pn", "PTn")}
            setsA, setsB = [], []
            for si in range(3):
                d = {}
                d["X4"] = self.sb(es, f"rwX4{si}", [128, 4, A_W], BF16)
                d["vb"] = self.sb(es, f"rwvb{si}", [128, A_W], BF16)
                d["gC"] = self.sb(es, f"rwgC{si}", [64, 12], F32)
                d["bon"] = self.sb(es, f"rwbon{si}", [128, A_W], F32)
                d["sg"] = self.sb(es, f"rwsg{si}", [128, A_W], F32)
                setsA.append(d)
            for si in range(2):
                d = {}
                d["XT"] = self.sb(es, f"rwXT{si}", [64, 4, 12, 128], BF16)
                for nm in ("AkT", "BbT", "BkT", "WT"):
                    d[nm] = self.sb(es, f"rwM{nm}{si}", [128, 12, 128], BF16)
                setsB.append(d)
            RH = self.sb(es, "rwRH", [128, 12, 64], BF16)
            Un = self.sb(es, "rwUn", [128, 12, 64], BF16)
            Z = self.sb(es, "rwZ", [64, 12, 64], F32)
            Zb = self.sb(es, "rwZb", [64, 12, 64], BF16)
            Y = self.sb(es, "rwY", [128, A_W], F32)
            Y2 = self.sb(es, "rwY2", [128, A_W], F32)
            Ssb = Y2[0:64, :].rearrange("p (h c) -> p h c", h=12)
            s12 = self.sb(es, "rws12", [128, 3, 12], F32)
            U = self.sbn(es, "rwU", [128, A_W], BF16, 2)
            print("rwkv sbuf remaining", self.nc.sbuf_bytes_remaining)
            pb = self.psn(es, "rwpb", [128, 512], F32, 8)

            def tt(en, out, in0, in1, op, reads, writes):
                fw.op(en, lambda e_: e_.tensor_tensor(out=out, in0=in0, in1=in1, op=op), reads=reads, writes=writes)

            def acopy(dst, src, reads, writes, scale=None):
                if scale is None:
                    fw.op("act", lambda e_: e_.copy(out=dst, in_=src), reads=reads, writes=writes)
                else:
                    fw.op("act", lambda e_: e_.mul(out=dst, in_=src, mul=float(scale)), reads=reads, writes=writes)

            def dcopy(dst, src, reads, writes):
                fw.op("dve", lambda e_: e_.tensor_copy(out=dst, in_=src), reads=reads, writes=writes)

            v3 = lambda t_: t_.rearrange("p (h c) -> p h c", h=12)
            allk = lambda nm: [nm, (nm, 0), (nm, 1), (nm, 2)]

            def bank6(n_, bf=False):
                B = pb.next()
                if bf:
                    return B, B[:n_, :].bitcast(BF16)[:, 0:512].rearrange("p (h c) -> p h c", h=4)
                return B, B[:n_, 0:512].rearrange("p (h c) -> p h c", h=4)

            def stageA(ci, r0, n, seg, S):
                X4, gC, vb, bon, sg = S["X4"], S["gC"], S["vb"], S["bon"], S["sg"]
                kX4 = k(X4)
                h_ = H.next()
                fw.dma("sp", h_[:n, :], self.h_d[r0:r0 + n, 0:A_SHIFT], reads=["D:h"], writes=[k(h_)])
                fw.dma("sp", sg[:n, :], self.h_d[r0:r0 + n, A_SHIFT:3200], reads=["D:h"], writes=[k(sg)])
                if ci == 0:
                    if seg == 0:
                        fw.op("pool", lambda e_: e_.memset(hs[0:1, :], 0.0), writes=["rwhs"])
                    else:
                        fw.dma("act", hs[0:1, :], self.st_shift[e:e + 1, :], writes=["rwhs"])
                    if n > 1:
                        fw.dma("act", hs[1:n, :], self.h_d[r0:r0 + n - 1, 0:A_SHIFT], reads=["D:h"], writes=[("rwhs", 1)])
                else:
                    fw.dma("act", hs[:n, :], self.h_d[r0 - 1:r0 + n - 1, 0:A_SHIFT], reads=["D:h"], writes=["rwhs", ("rwhs", 1)])
                hpk = ["rwhs", ("rwhs", 1)]
                tt("dve", hs[:n, :], hs[:n, :], h_[:n, 0:A_SHIFT], ALU.subtract, hpk + [k(h_)], hpk)
                tt("pool", hs[:n, :], hs[:n, :], mu[:n, :], ALU.mult, hpk + ["rwmu"], hpk)
                tt("dve", hs[:n, :], hs[:n, :], h_[:n, 0:A_SHIFT], ALU.add, hpk + [k(h_)], hpk)
                r_, k_, v_ = hs[:n, 0:768], hs[:n, 768:1536], hs[:n, 1536:2304]
                fw.op("act", lambda e_: e_.activation(out=sg[:n, :], in_=sg[:n, :], func=AF.Silu), reads=[k(sg)], writes=[k(sg)])
                fw.op("act", lambda e_: e_.activation(out=hs[:n, 2304:2368], in_=hs[:n, 2304:2368], func=AF.Tanh), reads=["rwhs"], writes=["rwhs"])
                acopy(vb[:n, :], v_, ["rwhs"], [k(vb)])
                yield
                B = pb.next()
                for j in range(2):
                    fw.op("pe", lambda e_, j=j, B=B: e_.transpose(out=B[:64, j * 128:j * 128 + n], in_=hs[:n, 2304 + j * 64:2368 + j * 64], identity=self.idf[:n, :n]), reads=["rwhs", "idf"], writes=[k(B)])
                dcopy(lT[:, :, :n], B[:64, 0:256].rearrange("p (a c) -> p a c", a=2)[:, :, :n], [k(B)], ["rwlT"])
                yield
                lb_ = {}
                for j in range(2):
                    for hf in range(2):
                        B = pb.next()
                        lb_[(j, hf)] = B
                        fw.op("pe", lambda e_, j=j, hf=hf, B=B: e_.matmul(B[:n, 0:384], lhsT=lT[:, j, :n], rhs=wup[:, j, hf * 384:(hf + 1) * 384], start=True, stop=True), reads=["rwlT", "rwwup"], writes=[k(B)])
                for j, (dst, off, dk_, ok_) in enumerate(((sw, w0, "rwsw", "rww0"), (av, a0, "rwa", "rwa0"))):
                    for hf in range(2):
                        B = lb_[(j, hf)]
                        tt("dve", dst[:n, hf * 384:(hf + 1) * 384], B[:n, 0:384], off[:n, hf * 384:(hf + 1) * 384], ALU.add, [k(B), ok_], [dk_])
                yield
                fw.op("act", lambda e_: e_.activation(out=sw[:n, :], in_=sw[:n, :], func=AF.Sigmoid), reads=["rwsw"], writes=["rwsw"])
                fw.op("act", lambda e_: e_.activation(out=av[:n, :], in_=av[:n, :], func=AF.Sigmoid), reads=["rwa"], writes=["rwa"])
                yield
                tt("dve", kk[:n, :], k_, k_k[:n, :], ALU.mult, ["rwhs", "rwkk"], ["rwkkv"])
                tt("pool", tmp[:n, :], kk[:n, :], kk[:n, :], ALU.mult, ["rwkkv"], ["rwtmp"])
                fw.op("dve", lambda e_: e_.tensor_reduce(out=s12p[:n, 0, :], in_=v3(tmp[:n, :]), axis=AX.X, op=ALU.add), reads=["rwtmp"], writes=["rws12p"])
                fw.op("dve", lambda e_: e_.tensor_scalar_max(out=s12p[:n, 0, :], in0=s12p[:n, 0, :], scalar1=1e-24), reads=["rws12p"], writes=["rws12p"])
                yield
                fw.op("act", lambda e_: e_.sqrt(out=s12p[:n, 0, :], in_=s12p[:n, 0, :]), reads=["rws12p"], writes=["rws12p"])
                fw.op("dve", lambda e_: e_.reciprocal(out=s12p[:n, 0, :], in_=s12p[:n, 0, :]), reads=["rws12p"], writes=["rws12p"])
                tt("dve", v3(kk[:n, :]), v3(kk[:n, :]), s12p[:n, 0, :].unsqueeze(2).to_broadcast([n, 12, 64]), ALU.mult, ["rwkkv", "rws12p"], ["rwkkv"])
                fw.op("dve", lambda e_: e_.scalar_tensor_tensor(out=tmp[:n, :], in0=av[:n, :], scalar=-1.0, in1=k_a[:n, :], op0=ALU.add, op1=ALU.mult), reads=["rwa", "rwka"], writes=["rwtmp"])
                fw.op("dve", lambda e_: e_.scalar_tensor_tensor(out=kmod[:n, :], in0=tmp[:n, :], scalar=1.0, in1=k_, op0=ALU.add, op1=ALU.mult), reads=["rwtmp", "rwhs"], writes=["rwkmod"])
                tt("pool", tmp2[:n, :], r_, kmod[:n, :], ALU.mult, ["rwhs", "rwkmod"], ["rwtmp2"])
                tt("pool", tmp2[:n, :], tmp2[:n, :], r_k[:n, :], ALU.mult, ["rwtmp2", "rwrk"], ["rwtmp2"])
                fw.op("dve", lambda e_: e_.tensor_reduce(out=s12p[:n, 1, :], in_=v3(tmp2[:n, :]), axis=AX.X, op=ALU.add), reads=["rwtmp2"], writes=["rws12p"])
                tt("pool", v3(bon[:n, :]), v3(v_), s12p[:n, 1, :].unsqueeze(2).to_broadcast([n, 12, 64]), ALU.mult, ["rwhs", "rws12p"], [k(bon)])
                yield
                for hf in range(2):
                    B = pb.next()
                    fw.op("pe", lambda e_, hf=hf, B=B: e_.matmul(B[:n, 0:384], lhsT=self.maskf[:n, 0, :n], rhs=sw[:n, hf * 384:(hf + 1) * 384], start=True, stop=True), reads=["maskf", "rwsw"], writes=[k(B)])
                    if hf:
                        acopy(cs[:n, hf * 384:(hf + 1) * 384], B[:n, 0:384], [k(B)], [("rwcs", hf)])
                    else:
                        dcopy(cs[:n, hf * 384:(hf + 1) * 384], B[:n, 0:384], [k(B)], [("rwcs", hf)])
                csk = [("rwcs", 0), ("rwcs", 1)]
                yield
                fw.op("act", lambda e_: e_.activation(out=E3[:n, 0, :], in_=cs[:n, :], func=AF.Exp, scale=NEG), reads=csk, writes=[("rwE3", 0)])
                fw.op("act", lambda e_: e_.activation(out=E3[:n, 1, :], in_=cs[:n, :], func=AF.Exp, scale=-NEG), reads=csk, writes=[("rwE3", 1)])
                tt("dve", tmp[:n, :], cs[:n, :], sw[:n, :], ALU.subtract, csk + ["rwsw"], ["rwtmp"])
                fw.op("act", lambda e_: e_.activation(out=E3[:n, 2, :], in_=tmp[:n, :], func=AF.Exp, scale=NEG), reads=["rwtmp"], writes=[("rwE3", 2)])
                B = pb.next()
                for h in range(12):
                    fw.op("pe", lambda e_, h=h, B=B: e_.matmul(B[:64, h:h + 1], lhsT=sw[:n, h * 64:(h + 1) * 64], rhs=ones[:n, 0:1], start=True, stop=True), reads=["rwsw", "rwones"], writes=[k(B)])
                fw.op("act", lambda e_, B=B: e_.activation(out=gC[:, :], in_=B[:64, 0:12], func=AF.Exp, scale=NEG), reads=[k(B)], writes=[k(gC)])
                yield
                tt("dve", X4[:n, 0, :], kk[:n, :], E3[:n, 2, :], ALU.mult, ["rwkkv", ("rwE3", 2)], [(kX4, 0)])
                tt("pool", X4[:n, 1, :], r_, E3[:n, 0, :], ALU.mult, ["rwhs", ("rwE3", 0)], [(kX4, 1)])
                tt("dve", tmp[:n, :], kk[:n, :], av[:n, :], ALU.mult, ["rwkkv", "rwa"], ["rwtmp"])
                tt("pool", X4[:n, 2, :], tmp[:n, :], E3[:n, 1, :], ALU.mult, ["rwtmp", ("rwE3", 1)], [(kX4, 2)])
                tt("dve", X4[:n, 3, :], kmod[:n, :], E3[:n, 1, :], ALU.mult, ["rwkmod", ("rwE3", 1)], [(kX4, 3)])
                yield

            def stageB(n, nlev, SA, S):
                X4, XT = SA["X4"], S["XT"]
                kX4, kXT = k(X4), k(XT)
                for wi_, w_ in enumerate((2, 0, 3, 1)):
                    for hg in range(3):
                        B, Bv = bank6(64, bf=True)
                        for hh in range(4):
                            h = hg * 4 + hh
                            fw.op("pe", lambda e_, w_=w_, h=h, hh=hh, Bv=Bv: e_.transpose(out=Bv[:, hh, :n], in_=X4[:n, w_, h * 64:(h + 1) * 64], identity=self.idb[:n, :n]), reads=[(kX4, w_), "idb"], writes=[k(B)])
                        if (w_ + hg) % 2:
                            acopy(XT[:, w_, hg * 4:(hg + 1) * 4, :n], Bv[:, :, :n], [k(B)], [(kXT, w_)])
                        else:
                            dcopy(XT[:, w_, hg * 4:(hg + 1) * 4, :n], Bv[:, :, :n], [k(B)], [(kXT, w_)])
                    if wi_ % 2:
                        yield

                def prod(wl, wr, midx, sign, dst, dkey):
                    for hg in range(3):
                        B, Bv = bank6(n)
                        for hh in range(4):
                            h = hg * 4 + hh
                            fw.op("pe", lambda e_, h=h, hh=hh, Bv=Bv: e_.matmul(Bv[:, hh, :n], lhsT=XT[:, wl, h, :n], rhs=XT[:, wr, h, :n], start=True, stop=True), reads=[(kXT, wl), (kXT, wr)], writes=[k(B)])
                        hr = slice(hg * 4, (hg + 1) * 4)
                        acopy(raw[:n, hr, :n], Bv[:, :, :n], [k(B)], [("rwraw", hg)], scale=(sign if sign != 1.0 else None))
                        tt("pool", dst[:n, hr, :n], raw[:n, hr, :n], self.maskb[:n, midx, :n].unsqueeze(1).to_broadcast([n, 4, n]), ALU.mult, [("rwraw", hg), "maskb"], [dkey, (dkey, hg)])
                prod(2, 0, 2, -1.0, Pm["PT"], "rwMPT")
                prod(0, 2, 3, -1.0, Pm["P"], "rwMP")
                yield
                prod(3, 0, 2, 1.0, S["AkT"], k(S["AkT"]))
                prod(2, 1, 0, 1.0, S["BbT"], k(S["BbT"]))
                yield
                prod(3, 1, 0, 1.0, S["BkT"], k(S["BkT"]))
                WT, kWT = S["WT"], k(S["WT"])
                tt("dve", WT[:n, :, :n], Pm["PT"][:n, :, :n], self.idf[:n, :n].unsqueeze(1).to_broadcast([n, 12, n]), ALU.add, ["rwMPT", "idf"], allk(kWT))
                yield
                P, PT, Pn, PTn = "P", "PT", "Pn", "PTn"
                for lev in range(1, nlev):
                    for hg in range(3):
                        hr = slice(hg * 4, (hg + 1) * 4)
                        B, Bv = bank6(n)
                        for hh in range(4):
                            h = hg * 4 + hh
                            fw.op("pe", lambda e_, h=h, hh=hh, Bv=Bv, P=P, PT=PT: e_.matmul(Bv[:, hh, :n], lhsT=Pm[PT][:n, h, :n], rhs=Pm[P][:n, h, :n], start=True, stop=True), reads=allk("rwM" + P) + allk("rwM" + PT), writes=[k(B)])
                        acopy(Pm[Pn][:n, hr, :n], Bv[:, :, :n], [k(B)], [("rwM" + Pn, hg)])
                    yield
                    if lev < nlev - 1:
                        for hg in range(3):
                            hr = slice(hg * 4, (hg + 1) * 4)
                            B, Bv = bank6(n)
                            for hh in range(4):
                                h = hg * 4 + hh
                                fw.op("pe", lambda e_, h=h, hh=hh, Bv=Bv, P=P, PT=PT: e_.matmul(Bv[:, hh, :n], lhsT=Pm[P][:n, h, :n], rhs=Pm[PT][:n, h, :n], start=True, stop=True), reads=allk("rwM" + P) + allk("rwM" + PT), writes=[k(B)])
                            acopy(Pm[PTn][:n, hr, :n], Bv[:, :, :n], [k(B)], [("rwM" + PTn, hg)])
                        yield
                    for hg in range(3):
                        hr = slice(hg * 4, (hg + 1) * 4)
                        B, Bv = bank6(n)
                        for hh in range(4):
                            h = hg * 4 + hh
                            fw.op("pe", lambda e_, h=h, hh=hh, Bv=Bv, Pn=Pn: e_.matmul(Bv[:, hh, :n], lhsT=Pm[Pn][:n, h, :n], rhs=WT[:n, h, :n], start=True, stop=True), reads=[("rwM" + Pn, hg), (kWT, hg)], writes=[k(B)])
                        tt("dve", WT[:n, hr, :n], WT[:n, hr, :n], Bv[:, :, :n], ALU.add, [(kWT, hg), k(B)], [(kWT, hg)])
                    yield
                    P, Pn = Pn, P
                    PT, PTn = PTn, PT

            def solve(r0, n, SA, S):
                X4, XT, gC, vb, bon, sg = SA["X4"], S["XT"], SA["gC"], SA["vb"], SA["bon"], SA["sg"]
                kX4, kXT = k(X4), k(XT)
                AkT, BbT, BkT, WT = S["AkT"], S["BbT"], S["BkT"], S["WT"]
                for hg in range(3):
                    hr = slice(hg * 4, (hg + 1) * 4)
                    B, Bv = bank6(n)
                    for hh in range(4):
                        h = hg * 4 + hh
                        fw.op("pe", lambda e_, h=h, hh=hh, Bv=Bv: e_.matmul(Bv[:, hh, 0:64], lhsT=XT[:, 0, h, :n], rhs=Zb[:, h, :], start=True, stop=False), reads=[(kXT, 0), "rwZb"], writes=[k(B)])
                        fw.op("pe", lambda e_, h=h, hh=hh, Bv=Bv: e_.matmul(Bv[:, hh, 0:64], lhsT=AkT[:n, h, :n], rhs=vb[:n, h * 64:(h + 1) * 64], start=False, stop=True), reads=[k(AkT), k(vb)], writes=[k(B)])
                    if hg % 2:
                        acopy(RH[:n, hr, :], Bv[:, :, 0:64], [k(B)], [("rwRH", hg)])
                    else:
                        dcopy(RH[:n, hr, :], Bv[:, :, 0:64], [k(B)], [("rwRH", hg)])
                yield
                for hg in range(3):
                    hr = slice(hg * 4, (hg + 1) * 4)
                    B, Bv = bank6(n)
                    for hh in range(4):
                        h = hg * 4 + hh
                        fw.op("pe", lambda e_, h=h, hh=hh, Bv=Bv: e_.matmul(Bv[:, hh, 0:64], lhsT=WT[:n, h, :n], rhs=RH[:n, h, :], start=True, stop=True), reads=[k(WT), (k(WT), hg), ("rwRH", hg)], writes=[k(B)])
                    acopy(Un[:n, hr, :], Bv[:, :, 0:64], [k(B)], [("rwUn", hg)], scale=-1.0)
                yield
                for hg in range(3):
                    hr = slice(hg * 4, (hg + 1) * 4)
                    B = pb.next()
                    Bv = B[:64, 0:512].rearrange("p (h c) -> p h c", h=4)[:, :, 0:64]
                    B2, Bv2 = bank6(n)
                    for hh in range(4):
                        h = hg * 4 + hh
                        fw.op("pe", lambda e_, h=h, hh=hh, Bv2=Bv2: e_.matmul(Bv2[:, hh, 0:64], lhsT=XT[:, 1, h, :n], rhs=Zb[:, h, :], start=True, stop=False), reads=[(kXT, 1), "rwZb"], writes=[k(B2)])
                        fw.op("pe", lambda e_, h=h, hh=hh, Bv2=Bv2: e_.matmul(Bv2[:, hh, 0:64], lhsT=BbT[:n, h, :n], rhs=Un[:n, h, :], start=False, stop=False), reads=[k(BbT), ("rwUn", hg)], writes=[k(B2)])
                        fw.op("pe", lambda e_, h=h, hh=hh, Bv2=Bv2: e_.matmul(Bv2[:, hh, 0:64], lhsT=BkT[:n, h, :n], rhs=vb[:n, h * 64:(h + 1) * 64], start=False, stop=True), reads=[k(BkT), k(vb)], writes=[k(B2)])
                    for hh in range(4):
                        h = hg * 4 + hh
                        fw.op("pe", lambda e_, h=h, hh=hh, Bv=Bv: e_.matmul(Bv[:, hh, 0:64], lhsT=X4[:n, 2, h * 64:(h + 1) * 64], rhs=Un[:n, h, :], start=True, stop=False), reads=[(kX4, 2), ("rwUn", hg)], writes=[k(B)])
                        fw.op("pe", lambda e_, h=h, hh=hh, Bv=Bv: e_.matmul(Bv[:, hh, 0:64], lhsT=X4[:n, 3, h * 64:(h + 1) * 64], rhs=vb[:n, h * 64:(h + 1) * 64], start=False, stop=True), reads=[(kX4, 3), k(vb)], writes=[k(B)])
                    tt("dve", Z[:, hr, :], Z[:, hr, :], Bv, ALU.add, ["rwZ", k(B)], ["rwZ"])
                    tt("pool", Z[:, hr, :], Z[:, hr, :], gC[:, hr].unsqueeze(2).to_broadcast([64, 4, 64]), ALU.mult, ["rwZ", k(gC)], ["rwZ"])
                    acopy(Zb[:, hr, :], Z[:, hr, :], ["rwZ"], ["rwZb"])
                    dcopy(Y[:n, hg * 256:(hg + 1) * 256].rearrange("p (h c) -> p h c", h=4), Bv2[:, :, 0:64], [k(B2)], [("rwY", hg)])
                yield
                yk = [("rwY", 0), ("rwY", 1), ("rwY", 2)]
                b12 = lambda col: s12[:n, col, :].unsqueeze(2).to_broadcast([n, 12, 64])
                fw.op("dve", lambda e_: e_.tensor_reduce(out=s12[:n, 0, :], in_=v3(Y[:n, :]), axis=AX.X, op=ALU.add), reads=yk, writes=["rws12"])
                tt("pool", Y2[:n, :], Y[:n, :], Y[:n, :], ALU.mult, yk, ["rwY2"])
                fw.op("dve", lambda e_: e_.tensor_reduce(out=s12[:n, 1, :], in_=v3(Y2[:n, :]), axis=AX.X, op=ALU.add), reads=["rwY2"], writes=["rws12"])
                fw.op("dve", lambda e_: e_.tensor_scalar_mul(out=s12[:n, 0, :], in0=s12[:n, 0, :], scalar1=1.0 / 64), reads=["rws12"], writes=["rws12"])
                tt("dve", s12[:n, 2, :], s12[:n, 0, :], s12[:n, 0, :], ALU.mult, ["rws12"], ["rws12"])
                fw.op("dve", lambda e_: e_.scalar_tensor_tensor(out=s12[:n, 1, :], in0=s12[:n, 1, :], scalar=1.0 / 64, in1=s12[:n, 2, :], op0=ALU.mult, op1=ALU.subtract), reads=["rws12"], writes=["rws12"])
                fw.op("dve", lambda e_: e_.tensor_scalar_add(out=s12[:n, 1, :], in0=s12[:n, 1, :], scalar1=64e-5), reads=["rws12"], writes=["rws12"])
                fw.op("act", lambda e_: e_.sqrt(out=s12[:n, 1, :], in_=s12[:n, 1, :]), reads=["rws12"], writes=["rws12"])
                fw.op("dve", lambda e_: e_.reciprocal(out=s12[:n, 1, :], in_=s12[:n, 1, :]), reads=["rws12"], writes=["rws12"])
                yield
                tt("dve", v3(Y[:n, :]), v3(Y[:n, :]), b12(0), ALU.subtract, yk + ["rws12"], yk)
                tt("dve", v3(Y[:n, :]), v3(Y[:n, :]), b12(1), ALU.mult, yk + ["rws12"], yk)
                tt("pool", Y[:n, :], Y[:n, :], lg[:n, :], ALU.mult, yk + ["rwlg"], yk)
                tt("pool", Y[:n, :], Y[:n, :], lb[:n, :], ALU.add, yk + ["rwlb"], yk)
                tt("dve", Y[:n, :], Y[:n, :], bon[:n, :], ALU.add, yk + [k(bon)], yk)
                u_ = U.next()
                tt("pool", u_[:n, :], Y[:n, :], sg[:n, :], ALU.mult, yk + [k(sg)], [k(u_)])
                fw.dma("pool", self.u_d[r0:r0 + n, 0:A_W], u_[:n, :], reads=[k(u_)], writes=["D:u"])
                yield

            def drive(gens, ratios):
                gens = list(gens)
                alive = [g is not None for g in gens]
                while any(alive):
                    for gi, g in enumerate(gens):
                        for _ in range(ratios[gi]):
                            if alive[gi]:
                                try:
                                    next(g)
                                except StopIteration:
                                    alive[gi] = False

            for seg in range(2):
                if seg == 0:
                    C = min(128, SEQ)
                    chunks = [(r, C) for r in range(0, SEQ, C)]
                    fw.op("pool", lambda e_: e_.memset(Z[:], 0.0), writes=["rwZ"])
                    Sout = self.o_rwkv_p[e]
                else:
                    C = TS
                    chunks = [(SEQ, TS)]
                    Sout = self.o_rwkv_s[e]
                    fw.dma("sp", Ssb[:], self.st_rwkv[e].rearrange("h i j -> i h j"), writes=["rwS"])
                    for hg in range(3):
                        B = pb.next()
                        Bv = B[:64, 0:512].rearrange("p (h c) -> p h c", h=4)[:, :, 0:64]
                        for hh in range(4):
                            fw.op("pe", lambda e_, hh=hh, hg=hg, Bv=Bv: e_.transpose(out=Bv[:, hh, :], in_=Ssb[:, hg * 4 + hh, :], identity=self.idf[:64, :64]), reads=["rwS", "idf"], writes=[k(B)])
                        dcopy(Z[:, hg * 4:(hg + 1) * 4, :], Bv, [k(B)], ["rwZ"])
                fw.op("dve", lambda e_: e_.tensor_copy(out=Zb[:], in_=Z[:]), reads=["rwZ"], writes=["rwZb"])
                nlev = int(np.log2(C))
                nch = len(chunks)
                gA = lambda ci: stageA(ci, chunks[ci][0], chunks[ci][1], seg, setsA[ci % 3]) if ci < nch else None
                gB = lambda ci: stageB(chunks[ci][1], nlev, setsA[ci % 3], setsB[ci % 2]) if ci < nch else None
                gS = lambda ci: solve(chunks[ci][0], chunks[ci][1], setsA[ci % 3], setsB[ci % 2]) if 0 <= ci < nch else None
                drive([gA(0)], [1])
                drive([gA(1), gB(0)], [2, 4])
                for ci in range(nch):
                    drive([gA(ci + 2), gB(ci + 1), gS(ci)], [2, 4, 1])
                for hg in range(3):
                    B = pb.next()
                    Bv = B[:64, 0:512].rearrange("p (h c) -> p h c", h=4)[:, :, 0:64]
                    for hh in range(4):
                        fw.op("pe", lambda e_, hh=hh, hg=hg, Bv=Bv: e_.transpose(out=Bv[:, hh, :], in_=Z[:, hg * 4 + hh, :], identity=self.idf[:64, :64]), reads=["rwZ", "idf"], writes=[k(B)])
                    dcopy(Ssb[:, hg * 4:(hg + 1) * 4, :], Bv, [k(B)], ["rwS"])
                fw.dma("pool", Sout.rearrange("h i j -> i h j"), Ssb[:], reads=["rwS"], writes=["D:orw"])

    def odd_mixers(self, o):
        fw, k = self.fw, self.key
        SEQ = self.SEQ
        lg = [float(np.log(1.0 - 2.0 ** (-5.0 - h))) for h in range(C_H)]
        with self.phase() as es:
            sc = self.sb(es, "rtsc", [128, 2, C_H], F32)
            fw.dma("sp", sc[:], self.c_retsc, writes=["rtsc"])
            R = self.sb(es, "rtR", [128, C_H, 2, 256], F32)
            Rb = self.sb(es, "rtRb", [128, C_H, 2, 256], BF16)
            qkvg = self.sbn(es, "rtin", [128, 4, C_W], F32, 2)
            rot = self.sbn(es, "rtrot", [128, 2, 256], F32, 2)
            t1 = self.sb(es, "rtt1", [128, 12, 256], F32)
            t2 = self.sb(es, "rtt2", [128, 12, 256], F32)
            qkb = self.sbn(es, "rtqkb", [128, 12, 256], BF16, 2)
            vb = self.sbn(es, "rtvb", [128, C_W], BF16, 2)
            qT = self.sbn(es, "rtqT", [128, 12, 128], BF16, 2)
            kT = self.sbn(es, "rtkT", [128, 12, 128], BF16, 2)
            Sb = self.sbn(es, "rtSb", [128, C_H, 128], BF16, 2)
            sq = self.sb(es, "rtsq", [128, C_W], F32)
            ss = self.sb(es, "rtss", [128, C_H], F32)
            sgr = self.sbn(es, "rtsg", [128, C_W], F32, 2)
            yt = self.sb(es, "rtyt", [128, C_W], F32)
            ub = self.sbn(es, "rtub", [128, C_W], BF16, 2)
            pt = self.psn(es, "rtpt", [128, 4, 128], BF16, 1)
            pS = self.psn(es, "rtpS", [128, C_H, 128], F32, 1)
            pO = self.psn(es, "rtpO", [128, C_H, 256], F32, 1)
            pR = self.psn(es, "rtpR", [128, 2, 256], F32, 2)
            for seg in range(2):
                if seg == 0:
                    fw.op("pool", lambda e: e.memset(R[:], 0.0), writes=["rtR"])
                    chunks = self.tiles[:-1]
                    Rout = self.o_ret_p[o]
                else:
                    fw.dma("sp", R[:], self.st_ret[o].rearrange("h (c p) e -> p h c e", p=128), writes=["rtR"])
                    chunks = self.tiles[-1:]
                    Rout = self.o_ret_s[o]
                fw.op("dve", lambda e: e.tensor_copy(out=Rb[:], in_=R[:]), reads=["rtR"], writes=["rtRb"])
                def prefix(r0, n, T):
                    X, rt, QK, V, QT, KT, S_, U, sg = T
                    fw.dma("sp", X[:n, 0:2, :], self.h_d[r0:r0 + n, 0:2 * C_W].rearrange("p (a c) -> p a c", a=2), reads=["D:h"], writes=[(k(X), 0)])
                    fw.dma("act", X[:n, 2:4, :], self.h_d[r0:r0 + n, 2 * C_W:4 * C_W].rearrange("p (a c) -> p a c", a=2), reads=["D:h"], writes=[(k(X), 1)])
                    fw.dma("sp", rt[:n], self.c_rot[r0:r0 + n], writes=[k(rt)])
                    x12 = X[:n, 0:2, :].rearrange("p a (h c) -> p (a h) c", h=C_H)
                    x12p = X[:n, 0:2, :].rearrange("p a (h c two) -> p (a h) c two", h=C_H, two=2)
                    t2p = t2[:n].rearrange("p a (c two) -> p a c two", two=2)
                    rtp = rt[:n].rearrange("p a (c two) -> p a c two", two=2)
                    fw.op("pool", lambda e, x12=x12, rt=rt: e.tensor_tensor(out=t1[:n], in0=x12, in1=rt[:n, 0, :].unsqueeze(1).to_broadcast([n, 12, 256]), op=ALU.mult),
                          reads=[(k(X), 0), k(rt)], writes=["rtt1"])
                    fw.op("dve", lambda e, x12p=x12p, t2p=t2p, rtp=rtp: e.tensor_tensor(out=t2p[:, :, :, 0], in0=x12p[:, :, :, 1], in1=rtp[:, 1, :, 0].unsqueeze(1).to_broadcast([n, 12, 128]), op=ALU.mult),
                          reads=[(k(X), 0), k(rt)], writes=[("rtt2", 0)])
                    fw.op("dve", lambda e, x12p=x12p, t2p=t2p, rtp=rtp: e.tensor_tensor(out=t2p[:, :, :, 1], in0=x12p[:, :, :, 0], in1=rtp[:, 1, :, 1].unsqueeze(1).to_broadcast([n, 12, 128]), op=ALU.mult),
                          reads=[(k(X), 0), k(rt)], writes=[("rtt2", 1)])
                    yield
                    fw.op("dve", lambda e: e.tensor_tensor(out=t1[:n], in0=t1[:n], in1=t2[:n], op=ALU.add), reads=["rtt1", ("rtt2", 0), ("rtt2", 1)], writes=["rtt1"])
                    fw.op("dve", lambda e, QK=QK: e.tensor_tensor(out=QK[:n], in0=t1[:n], in1=sc[:n].rearrange("p a h -> p (a h)").unsqueeze(2).to_broadcast([n, 12, 256]), op=ALU.mult),
                          reads=["rtt1", "rtsc"], writes=[k(QK)])
                    fw.op("act", lambda e, V=V, X=X: e.copy(out=V[:n, :], in_=X[:n, 2, :]), reads=[(k(X), 1)], writes=[k(V)])
                    qflat = QK[:, 0:6, :].rearrange("p h c -> p (h c)")
                    kflat = QK[:, 6:12, :].rearrange("p h c -> p (h c)")
                    yield
                    self.transposes(QT, qflat, 12, n, pt, k(QT), k(QK))
                    yield
                    self.transposes(KT, kflat, 12, n, pt, k(KT), k(QK))
                    yield
                    S = pS.next()
                    for h in range(C_H):
                        for c in range(2):
                            fw.op("pe", lambda e, h=h, c=c, S=S, KT=KT, QT=QT: e.matmul(S[:n, h, :n], lhsT=KT[:, h * 2 + c, :n], rhs=QT[:, h * 2 + c, :n], start=(c == 0), stop=(c == 1)),
                                  reads=[k(KT), k(QT)], writes=[k(S)])
                    fw.op("dve", lambda e, S=S, S_=S_: e.tensor_tensor(out=S_[:n, :, :n], in0=S[:n, :, :n], in1=self.maskf[:n, 0, :n].unsqueeze(1).to_broadcast([n, C_H, n]), op=ALU.mult),
                          reads=[k(S), "maskf"], writes=[k(S_)])
                    fw.op("act", lambda e, X=X, sg=sg: e.activation(out=sg[:n, :], in_=X[:n, 3, :], func=AF.Silu), reads=[(k(X), 1)], writes=[k(sg)])
                    yield

                def solve(r0, n, T):
                    X, rt, QK, V, QT, KT, S_, U, sg = T
                    O = pO.next()
                    for h in range(C_H):
                        fw.op("pe", lambda e, h=h, O=O, S_=S_, V=V: e.matmul(O[:n, h, :], lhsT=S_[:n, h, :n], rhs=V[:n, h * 256:(h + 1) * 256], start=True, stop=False),
                              reads=[k(S_), k(V)], writes=[k(O)])
                        for c in range(2):
                            fw.op("pe", lambda e, h=h, c=c, O=O, QT=QT: e.matmul(O[:n, h, :], lhsT=QT[:, h * 2 + c, :n], rhs=Rb[:, h, c, :], start=False, stop=(c == 1)),
                                  reads=[k(QT), "rtRb"], writes=[k(O)])
                    yield
                    for h in range(C_H):
                        dR = pR.next()
                        for c in range(2):
                            fw.op("pe", lambda e, h=h, c=c, dR=dR, QK=QK, V=V: e.matmul(dR[:, c, :], lhsT=QK[:n, 6 + h, c * 128:(c + 1) * 128], rhs=V[:n, h * 256:(h + 1) * 256], start=True, stop=True),
                                  reads=[k(QK), k(V)], writes=[k(dR)])
                        fw.op("dve", lambda e, h=h, dR=dR: e.tensor_tensor(out=R[:, h], in0=R[:, h], in1=dR[:], op=ALU.add), reads=[k(dR), "rtR"], writes=["rtR"])
                        gch = float(np.exp(lg[h] * n))
                        fw.op("act", lambda e, h=h, gch=gch: e.mul(out=Rb[:, h], in_=R[:, h], mul=gch), reads=["rtR"], writes=["rtRb"])
                        fw.op("act", lambda e, h=h, gch=gch: e.mul(out=R[:, h], in_=R[:, h], mul=gch), reads=["rtR"], writes=["rtR"])
                        if h % 2:
                            yield
                    yield
                    Of = O[:n].rearrange("p h c -> p (h c)")
                    fw.op("act", lambda e, Of=Of: e.activation(out=sq[:n, :], in_=Of, func=AF.Square), reads=[k(O)], writes=["rtsq"])
                    fw.op("dve", lambda e: e.tensor_reduce(out=ss[:n, :], in_=sq[:n, :].rearrange("p (h c) -> p h c", h=C_H), axis=AX.X, op=ALU.add), reads=["rtsq"], writes=["rtss"])
                    fw.op("dve", lambda e: e.tensor_scalar(out=ss[:n, :], in0=ss[:n, :], scalar1=1.0 / 256.0, scalar2=1e-6, op0=ALU.mult, op1=ALU.add), reads=["rtss"], writes=["rtss"])
                    fw.op("act", lambda e: e.sqrt(out=ss[:n, :], in_=ss[:n, :]), reads=["rtss"], writes=["rtss"])
                    fw.op("dve", lambda e: e.reciprocal(out=ss[:n, :], in_=ss[:n, :]), reads=["rtss"], writes=["rtss"])
                    fw.op("dve", lambda e, O=O: e.tensor_tensor(out=yt[:n, :].rearrange("p (h c) -> p h c", h=C_H), in0=O[:n], in1=ss[:n, :].unsqueeze(2).to_broadcast([n, C_H, 256]), op=ALU.mult),
                          reads=[k(O), "rtss"], writes=["rtyt"])
                    fw.op("pool", lambda e, U=U, sg=sg: e.tensor_tensor(out=U[:n, :], in0=yt[:n, :], in1=sg[:n, :], op=ALU.mult), reads=["rtyt", k(sg)], writes=[k(U)])
                    fw.dma("pool", self.u_d[r0:r0 + n, 0:C_W], U[:n, :], reads=[k(U)], writes=["D:u"])

                    yield

                def drive2(g1, g2):
                    al = [g1 is not None, g2 is not None]
                    gs = [g1, g2]
                    while any(al):
                        for gi in range(2):
                            if al[gi]:
                                try:
                                    next(gs[gi])
                                except StopIteration:
                                    al[gi] = False
                Ts = [(qkvg.next(), rot.next(), qkb.next(), vb.next(), qT.next(), kT.next(), Sb.next(), ub.next(), sgr.next()) for _ in chunks]
                drive2(prefix(chunks[0][0], chunks[0][1], Ts[0]), None)
                for ci in range(len(chunks)):
                    gp = prefix(chunks[ci + 1][0], chunks[ci + 1][1], Ts[ci + 1]) if ci + 1 < len(chunks) else None
                    drive2(gp, solve(chunks[ci][0], chunks[ci][1], Ts[ci]))
                fw.dma("pool", Rout.rearrange("h (c p) e -> p h c e", p=128), R[:], reads=["rtR"], writes=["D:oret"])


PAST_LEN = 16384


def _consts(SEQ):
    NT = SEQ + TS
    c = {}
    c["c_ident"] = np.eye(128, dtype=np.float32)
    pos = np.concatenate([np.arange(SEQ, dtype=np.float32), np.arange(TS, dtype=np.float32) + np.float32(PAST_LEN)])
    angle = (1.0 / (np.float32(10000.0) ** np.linspace(0.0, 1.0, 128, dtype=np.float32))).astype(np.float32)
    ph = (pos[:, None] * np.repeat(angle, 2)[None]).astype(np.float32)
    sgn = np.tile(np.array([-1.0, 1.0], np.float32), 128)
    c["c_rot"] = np.stack([np.cos(ph), np.sin(ph) * sgn[None]], axis=1).astype(np.float32)
    lg = np.log(1.0 - 2.0 ** (-5.0 - np.arange(C_H, dtype=np.float32))).astype(np.float32)
    idx = np.arange(128, dtype=np.float32)
    xi = np.exp(lg[None, :] * (idx[:, None] + 1.0))
    kz = np.exp(-lg[None, :] * (idx[:, None] + 1.0)) * (256.0 ** -0.5)
    c["c_retsc"] = np.stack([xi, kz], axis=1).astype(np.float32)
    j = np.arange(128)[:, None]
    i = np.arange(128)[None, :]
    c["c_masks"] = np.stack([(j <= i), (j >= i), (j < i), (j > i)], axis=1).astype(np.float32)
    return c


def make_in_maps(inp, SEQ, DEPTH, n_cores):
    LE, LO = (DEPTH + 1) // 2, DEPTH // 2
    cst = _consts(SEQ)
    f = lambda a: np.ascontiguousarray(np.asarray(a, dtype=np.float32))
    maps = []
    for c in range(n_cores):
        b = c % inp["x_prompt"].shape[0]
        m = {}
        m["x_in"] = f(np.concatenate([inp["x_prompt"][b, :SEQ], inp["x_sample"][c]], axis=0))
        m["memp"] = f(inp["mem_prompt"][b])
        m["st_rwkv"] = f(inp["state_rwkv"][:LE, c])
        m["st_shift"] = f(inp["state_rwkv_shift"][:LE, c])
        for g in range(3):
            a = inp[f"cache_dwa_g{g}"][:LE, c]
            m[f"c_g{g}"] = f(a.reshape(a.shape[0], a.shape[1], 512))
        m["st_ret"] = f(inp["state_ret"][:max(LO, 1), c])
        m["c_mem"] = f(inp["cache_mem_kv"][:DEPTH, c].reshape(DEPTH, 256, 1024))
        m["w_in_even"] = f(inp["w_in_even"][:LE])
        m["w_out_even"] = f(inp["w_out_even"][:LE])
        m["w_in_odd"] = f(inp["w_in_odd"][:max(LO, 1)])
        m["w_out_odd"] = f(inp["w_out_odd"][:max(LO, 1)])
        m["w_mem_kv"] = f(inp["w_mem_kv"][:DEPTH])
        m["ln_g"] = f(inp["ln_g"][:DEPTH])
        m["ln_b"] = f(inp["ln_b"][:DEPTH])
        for nm in ("mu", "w0", "w_up", "a0", "a_up", "k_k", "k_a", "lnx_g", "lnx_b"):
            m["rwkv_" + nm] = f(inp["rwkv_" + nm][:LE])
        m["rwkv_r_k"] = f(np.asarray(inp["rwkv_r_k"])[:LE].reshape(LE, A_W))
        m.update(cst)
        maps.append(m)
    return maps


_NC_CACHE = {}


def run(inp, SEQ, DEPTH, n_cores, debug=False, kinds=None):
    key = (SEQ, DEPTH, debug, str(kinds))
    if key not in _NC_CACHE:
        _NC_CACHE[key] = Builder(SEQ, DEPTH, debug=debug, kinds=kinds).build()
    nc = _NC_CACHE[key]
    maps = make_in_maps(inp, SEQ, DEPTH, n_cores)
    res = run_bass_kernel_spmd(nc, maps, core_ids=list(range(n_cores)))
    return res.results


def kernel(**inp):
    inp = {k: np.asarray(v) for k, v in inp.items()}
    SEQ, DEPTH = inp["x_prompt"].shape[1], 4
    BP = inp["x_prompt"].shape[0]
    NS = inp["x_sample"].shape[0]
    r = run(inp, SEQ, DEPTH, NS)
    LE, LO = 2, 2
    P = range(BP)
    S = range(NS)
    st = lambda name, cores: np.stack([r[c][name] for c in cores], axis=1)
    y_p = np.stack([r[b]["y"][:SEQ] for b in P], 0)
    y_s = np.stack([r[c]["y"][SEQ:] for c in S], 0)
    outs = [y_p, y_s, st("o_rwkv_p", P), st("o_rwkv_s", S), st("o_shift_p", P), st("o_shift_s", S)]
    for g in range(3):
        a = st(f"o_g{g}_p", P)
        outs.append(a.reshape(LE, BP, a.shape[2], 2, 4, 64))
        a = st(f"o_g{g}_s", S)
        outs.append(a.reshape(LE, NS, TS, 2, 4, 64))
    outs.append(st("o_ret_p", P))
    outs.append(st("o_ret_s", S))
    outs.append(st("o_mem_p", P).reshape(DEPTH, BP, 256, 2, 4, 128))
    return tuple(np.ascontiguousarray(o.astype(np.float32)) for o in outs)
```

```python
import contextlib
import os
import numpy as np
import concourse.bass as bass
import concourse.mybir as mybir
from concourse.bass_utils import run_bass_kernel_spmd

F32 = mybir.dt.float32
BF16 = mybir.dt.bfloat16
AF = mybir.ActivationFunctionType
ALU = mybir.AluOpType
AX = mybir.AxisListType

D = 2048
KC = D // 128
TS = 4
ALPHA_FULL_DEPTH = 4
LN_EPS = 1e-5
A_H, A_N = 12, 64
A_W = 768
A_SHIFT = 3 * A_W + 128
B_W = 768
B_OUT = 256
M_W = 512
C_H, C_D = 6, 256
C_W = 1536
EVEN_IN = A_SHIFT + A_W + 3 * B_W + B_OUT + 2 * M_W
EVEN_OUT = A_W + B_OUT + M_W
ODD_IN = 4 * C_W + 2 * M_W
ODD_OUT = C_W + M_W
B_CONFIGS = ((128, 1), (512, 4), (2048, 16))


class FW:
    def __init__(self, nc, n_dma_slots=14):
        self.nc = nc
        self.engs = {"pe": nc.tensor, "act": nc.scalar, "dve": nc.vector, "pool": nc.gpsimd, "sp": nc.sync}
        self.sem, self.cnt, self._ctx = {}, {}, []
        for e in ("pe", "act", "dve", "pool"):
            cm = nc.semaphore("s_" + e)
            self.sem[e] = cm.__enter__()
            self._ctx.append(cm)
            self.cnt[e] = 0
        self.slots = {}
        for q in ("sp", "act", "pool"):
            lst = []
            for i in range(n_dma_slots):
                cm = nc.semaphore(f"d_{q}{i}")
                s = cm.__enter__()
                self._ctx.append(cm)
                lst.append([s, 0])
            self.slots[q] = [lst, 0]
        self.known = {e: {} for e in self.engs}
        self.lastw, self.readers = {}, {}
        self.dw, self.dr = {}, {}
        self.n_inst = 0
        self.n_wait = 0

    @staticmethod
    def _isd(k):
        return isinstance(k, str) and k.startswith("D:") or (isinstance(k, tuple) and isinstance(k[0], str) and k[0].startswith("D:"))

    def _deps(self, reads, writes):
        deps = []
        for r in reads:
            if self._isd(r):
                deps.extend(self.dw.get(r, ()))
            elif r in self.lastw:
                deps.append(self.lastw[r])
        for w in writes:
            if self._isd(w):
                deps.extend(self.dr.get(w, ()))
            else:
                if w in self.lastw:
                    deps.append(self.lastw[w])
                deps.extend(self.readers.get(w, ()))
        return deps

    def _wait(self, e, deps, skip_sem=None):
        need = {}
        for (s, v) in deps:
            if skip_sem is not None and s is skip_sem:
                continue
            k = id(s)
            if k not in need or need[k][1] < v:
                need[k] = (s, v)
        kn = self.known[e]
        for k, (s, v) in need.items():
            if kn.get(k, 0) >= v:
                continue
            self.engs[e].wait_ge(s, v)
            self.n_wait += 1
            kn[k] = v

    def _record(self, tok, reads, writes):
        for r in reads:
            if self._isd(r):
                self.dr.setdefault(r, []).append(tok)
            else:
                self.readers.setdefault(r, []).append(tok)
        for w in writes:
            if self._isd(w):
                if self.dr.get(w):
                    self.dr[w] = []
                    self.dw[w] = []
                self.dw.setdefault(w, []).append(tok)
            else:
                self.lastw[w] = tok
                self.readers[w] = []

    def op(self, e, fn, reads=(), writes=()):
        deps = self._deps(reads, writes)
        self._wait(e, deps, skip_sem=self.sem["pe"] if e == "pe" else None)
        ins = fn(self.engs[e])
        self.cnt[e] += 1
        ins.then_inc(self.sem[e], 1)
        self.n_inst += 1
        tok = (self.sem[e], self.cnt[e])
        self._record(tok, reads, writes)
        return tok

    def dma(self, q, out, in_, reads=(), writes=()):
        deps = self._deps(reads, writes)
        lst, idx = self.slots[q]
        slot = lst[idx % len(lst)]
        self.slots[q][1] = idx + 1
        if slot[1] > 0:
            deps.append((slot[0], slot[1]))
        self._wait(q, deps)
        slot[1] += 16
        self.engs[q].dma_start(out=out, in_=in_).then_inc(slot[0], 16)
        self.n_inst += 1
        tok = (slot[0], slot[1])
        self._record(tok, reads, writes)
        return tok

    def all_toks(self):
        toks = []
        for e in ("pe", "act", "dve", "pool"):
            if self.cnt[e]:
                toks.append((self.sem[e], self.cnt[e]))
        for q in self.slots:
            for s, v in self.slots[q][0]:
                if v:
                    toks.append((s, v))
        return toks

    def barrier(self):
        if not hasattr(self, "bar"):
            cm = self.nc.semaphore("s_bar")
            self.bar = cm.__enter__()
            self._ctx.append(cm)
            self.barv = 0
        toks = self.all_toks()
        self._wait("sp", toks)
        self.barv += 1
        self.engs["sp"].sem_inc(self.bar, 1)
        for e in ("pe", "act", "dve", "pool"):
            self.engs[e].wait_ge(self.bar, self.barv)
            for (s_, v_) in toks:
                self.known[e][id(s_)] = max(self.known[e].get(id(s_), 0), v_)
        self.lastw, self.readers = {}, {}
        self.dw, self.dr = {}, {}

    def finish(self):
        toks = []
        for e in ("pe", "act", "dve", "pool"):
            if self.cnt[e]:
                toks.append((self.sem[e], self.cnt[e]))
        for q in self.slots:
            for s, v in self.slots[q][0]:
                if v:
                    toks.append((s, v))
        self._wait("sp", toks)

    def close(self):
        for cm in reversed(self._ctx):
            cm.__exit__(None, None, None)


class Rot:
    def __init__(self, tiles):
        self.tiles = tiles
        self.i = 0

    def next(self):
        t = self.tiles[self.i % len(self.tiles)]
        self.i += 1
        return t


class Builder:
    def __init__(self, SEQ, DEPTH, debug=False, kinds=None):
        self.SEQ, self.DEPTH, self.debug = SEQ, DEPTH, debug
        self.kinds = kinds or ["e" if l % 2 == 0 else "o" for l in range(DEPTH)]
        self.NT = SEQ + TS
        self.LE, self.LO = (DEPTH + 1) // 2, DEPTH // 2
        self.ALPHA = (2 * ALPHA_FULL_DEPTH) ** 0.25
        self.nc = nc = bass.Bass("TRN2", target_bir_lowering=False)
        self.fw = FW(nc)
        self.tiles = [(r, 128) for r in range(0, SEQ, 128)] + [(SEQ, TS)]
        self.keep = [min(w, SEQ) for w, _ in B_CONFIGS]
        self.io = {}
        self._cnt = 0

    @contextlib.contextmanager
    def phase(self):
        with contextlib.ExitStack() as es:
            yield es
            self.fw.barrier()

    def din(self, name, shape, dt=F32):
        t = self.nc.dram_tensor(name, list(shape), dt, kind="ExternalInput").ap()
        self.io[name] = t
        return t

    def dout(self, name, shape, dt=F32):
        t = self.nc.dram_tensor(name, list(shape), dt, kind="ExternalOutput").ap()
        self.io[name] = t
        return t

    def dscr(self, name, shape, dt=F32):
        kind = "ExternalOutput" if self.debug else "Internal"
        t = self.nc.dram_tensor(name, list(shape), dt, kind=kind).ap()
        self.io[name] = t
        return t

    def sb(self, es, name, shape, dt=F32):
        self._cnt += 1
        return es.enter_context(self.nc.sbuf_tensor(f"{name}_{self._cnt}", list(shape), dt))

    def ps(self, es, name, shape, dt=F32):
        self._cnt += 1
        t = es.enter_context(self.nc.psum_tensor(f"{name}_{self._cnt}", list(shape), dt))
        return t

    def sbn(self, es, name, shape, dt, n):
        return Rot([self.sb(es, f"{name}{i}", shape, dt) for i in range(n)])

    def psn(self, es, name, shape, dt, n):
        return Rot([self.ps(es, f"{name}{i}", shape, dt) for i in range(n)])

    @staticmethod
    def key(t):
        return t.name if hasattr(t, "name") else str(t)

    def declare(self):
        SEQ, NT, LE, LO, DEPTH = self.SEQ, self.NT, self.LE, self.LO, self.DEPTH
        d = self.din
        self.x_in = d("x_in", [NT, D])
        self.memp = d("memp", [256, D])
        self.st_rwkv = d("st_rwkv", [LE, A_H, 64, 64])
        self.st_shift = d("st_shift", [LE, A_SHIFT])
        self.c_g = [d(f"c_g{g}", [LE, B_CONFIGS[g][0], 512]) for g in range(3)]
        self.st_ret = d("st_ret", [max(LO, 1), C_H, 256, 256])
        self.c_mem = d("c_mem", [DEPTH, 256, 1024])
        self.w_in_even = d("w_in_even", [LE, D, EVEN_IN])
        self.w_out_even = d("w_out_even", [LE, EVEN_OUT, D])
        self.w_in_odd = d("w_in_odd", [max(LO, 1), D, ODD_IN])
        self.w_out_odd = d("w_out_odd", [max(LO, 1), ODD_OUT, D])
        self.w_mem_kv = d("w_mem_kv", [DEPTH, D, 1024])
        self.ln_g = d("ln_g", [DEPTH, D])
        self.ln_b = d("ln_b", [DEPTH, D])
        self.rw = {}
        for nm, shp in (("mu", [LE, A_SHIFT]), ("w0", [LE, A_W]), ("w_up", [LE, 64, A_W]), ("a0", [LE, A_W]),
                        ("a_up", [LE, 64, A_W]), ("k_k", [LE, A_W]), ("k_a", [LE, A_W]), ("r_k", [LE, A_W]),
                        ("lnx_g", [LE, A_W]), ("lnx_b", [LE, A_W])):
            self.rw[nm] = d("rwkv_" + nm, shp)
        self.c_ident = d("c_ident", [128, 128])
        self.c_rot = d("c_rot", [NT, 2, 256])
        self.c_retsc = d("c_retsc", [128, 2, C_H])
        self.c_masks = d("c_masks", [128, 4, 128])
        o = self.dout
        self.y = o("y", [NT, D])
        self.o_rwkv_p = o("o_rwkv_p", [LE, A_H, 64, 64])
        self.o_rwkv_s = o("o_rwkv_s", [LE, A_H, 64, 64])
        self.o_shift_p = o("o_shift_p", [LE, A_SHIFT])
        self.o_shift_s = o("o_shift_s", [LE, A_SHIFT])
        self.o_g_p = [o(f"o_g{g}_p", [LE, self.keep[g], 512]) for g in range(3)]
        self.o_g_s = [o(f"o_g{g}_s", [LE, TS, 512]) for g in range(3)]
        self.o_ret_p = o("o_ret_p", [max(LO, 1), C_H, 256, 256])
        self.o_ret_s = o("o_ret_s", [max(LO, 1), C_H, 256, 256])
        self.o_mem_p = o("o_mem_p", [DEPTH, 256, 1024])
        s = self.dscr
        self.xT_d = s("xT_d", [D, NT], BF16)
        self.memT_d = s("memT_d", [D, 256], BF16)
        self.h_d = s("h_d", [NT, ODD_IN])
        self.u_d = s("u_d", [NT, D], BF16)
        self.x_d = [s("x_d0", [NT, D]), s("x_d1", [NT, D])]
        self.dwa_d = s("dwa_d", [3, SEQ + TS, 260])

    def load_consts(self, es):
        fw = self.fw
        self.idf = self.sb(es, "idf", [128, 128], F32)
        self.idb = self.sb(es, "idb", [128, 128], BF16)
        fw.dma("sp", self.idf[:], self.c_ident, writes=["idf"])
        fw.op("dve", lambda e: e.tensor_copy(out=self.idb[:], in_=self.idf[:]), reads=["idf"], writes=["idb"])
        self.maskf = self.sb(es, "maskf", [128, 4, 128], F32)
        self.maskb = self.sb(es, "maskb", [128, 4, 128], BF16)
        fw.dma("sp", self.maskf[:], self.c_masks, writes=["maskf"])
        fw.op("dve", lambda e: e.tensor_copy(out=self.maskb[:], in_=self.maskf[:]), reads=["maskf"], writes=["maskb"])

    def transposes(self, dst, src, nblk, n, ptrot, dkey, skey, evac=("act", "dve")):
        fw = self.fw
        gi = 0
        for g0 in range(0, nblk, 4):
            gn = min(4, nblk - g0)
            pt = ptrot.next()
            pk = self.key(pt)
            for j in range(gn):
                k = g0 + j
                fw.op("pe", lambda e, k=k, j=j, pt=pt: e.transpose(out=pt[:, j, :n], in_=src[:n, k * 128:(k + 1) * 128], identity=self.idb[:n, :n]),
                      reads=[skey, "idb"], writes=[pk])
            en = evac[gi % len(evac)]
            gi += 1
            if en == "act":
                fw.op("act", lambda e, pt=pt, g0=g0, gn=gn: e.copy(out=dst[:, g0:g0 + gn, :n], in_=pt[:, :gn, :n]), reads=[pk], writes=[dkey])
            else:
                fw.op(en, lambda e, pt=pt, g0=g0, gn=gn: e.tensor_copy(out=dst[:, g0:g0 + gn, :n], in_=pt[:, :gn, :n]), reads=[pk], writes=[dkey])

    def phase_xT_from(self, src_dram, rows_tiles, dstT, dkeyname):
        fw = self.fw
        with self.phase() as es:
            xf = self.sbn(es, "p0xf", [128, D], F32, 2)
            xb = self.sbn(es, "p0xb", [128, D], BF16, 2)
            xT = self.sbn(es, "p0xT", [128, KC, 128], BF16, 2)
            pt = self.psn(es, "p0pt", [128, 4, 128], BF16, 2)
            for (r0, n) in rows_tiles:
                a, b_, c = xf.next(), xb.next(), xT.next()
                fw.dma("sp", a[:n, :], src_dram[r0:r0 + n, :], writes=[self.key(a)])
                fw.op("pool", lambda e, a=a, b_=b_: e.tensor_copy(out=b_[:n, :], in_=a[:n, :]), reads=[self.key(a)], writes=[self.key(b_)])
                self.transposes(c, b_, KC, n, pt, self.key(c), self.key(b_))
                fw.dma("pool", dstT.rearrange("(k p) t -> p k t", p=128)[:, :, r0:r0 + n], c[:, :, :n], reads=[self.key(c)], writes=[dkeyname])

    def proj(self, W, ncol, xT_src, xkey, blocks, dst, dkey):
        fw = self.fw
        with self.phase() as es:
            wst = self.sbn(es, "pjws", [128, KC // 2, 512], F32, 4)
            wbf = self.sbn(es, "pjwb", [128, KC, 512], BF16, 4)
            xTb = self.sbn(es, "pjx", [128, KC, 516], BF16, 2)
            ho = self.sbn(es, "pjho", [128, 512], F32, 4)
            pp = self.psn(es, "pjps", [128, 512], F32, 4)
            Wv = W.rearrange("(k p) n -> p k n", p=128)
            xv = xT_src.rearrange("(k p) t -> p k t", p=128)
            ei = 0
            h = KC // 2
            cgs = [(cg, min(512, ncol - cg)) for cg in range(0, ncol, 512)]
            def load_pair(pi):
                wbs = []
                for (cg, cw) in cgs[pi:pi + 2]:
                    wb = wbf.next()
                    for hf in range(2):
                        ws = wst.next()
                        fw.dma("sp", ws[:, :, :cw], Wv[:, hf * h:(hf + 1) * h, cg:cg + cw], writes=[self.key(ws)])
                        for q in range(2):
                            en = "pool" if q % 2 else "dve"
                            k0 = hf * h + q * 4
                            fw.op(en, lambda e, q=q, ws=ws, wb=wb, k0=k0, cw=cw: e.tensor_copy(out=wb[:, k0:k0 + 4, :cw], in_=ws[:, q * 4:(q + 1) * 4, :cw]),
                                  reads=[self.key(ws)], writes=[(self.key(wb), k0 // 4)])
                    wbs.append((wb, cg, cw))
                return wbs

            nxt = load_pair(0)
            for pi in range(0, len(cgs), 2):
                wbs = nxt
                nxt = None
                for (c0, nb, tl) in blocks:
                    xt = xTb.next()
                    fw.dma("sp", xt[:, :, :nb], xv[:, :, c0:c0 + nb], reads=[xkey], writes=[self.key(xt)])
                    for (off, n, r0) in tl:
                        ps = [pp.next() for _ in wbs]
                        for k in range(KC):
                            for p, (wb, cg, cw) in zip(ps, wbs):
                                fw.op("pe", lambda e, k=k, p=p, xt=xt, wb=wb, off=off, n=n, cw=cw: e.matmul(p[:n, :cw], lhsT=xt[:, k, off:off + n], rhs=wb[:, k, :cw], start=(k == 0), stop=(k == KC - 1)),
                                      reads=[self.key(xt), (self.key(wb), k // 4)], writes=[self.key(p)])
                        for p, (wb, cg, cw) in zip(ps, wbs):
                            o = ho.next()
                            if ei % 2 == 0:
                                fw.op("act", lambda e, o=o, p=p, n=n, cw=cw: e.copy(out=o[:n, :cw], in_=p[:n, :cw]), reads=[self.key(p)], writes=[self.key(o)])
                            else:
                                fw.op("dve", lambda e, o=o, p=p, n=n, cw=cw: e.tensor_copy(out=o[:n, :cw], in_=p[:n, :cw]), reads=[self.key(p)], writes=[self.key(o)])
                            ei += 1
                            fw.dma("pool", dst[r0:r0 + n, cg:cg + cw], o[:n, :cw], reads=[self.key(o)], writes=[dkey])
                    if nxt is None and pi + 2 < len(cgs) and (c0, nb, tl) == blocks[min(1, len(blocks) - 1)]:
                        nxt = load_pair(pi + 2)

    def tok_blocks(self):
        SEQ = self.SEQ
        blocks = []
        for c0 in range(0, SEQ, 512):
            nb = min(512, SEQ - c0)
            tl = [(o, 128, c0 + o) for o in range(0, nb, 128)]
            blocks.append([c0, nb, tl])
        blocks[-1][1] += TS
        blocks[-1][2].append((blocks[-1][1] - TS, TS, SEQ))
        return [tuple(b) for b in blocks]

    def mem_prep(self, es, src, skey, name):
        fw = self.fw
        kT = self.sb(es, name + "kT", [128, 4, 256], BF16)
        va = self.sb(es, name + "va", [128, 2, 4, 129], BF16)
        with self.phase() as es2:
            kv = self.sb(es2, "mpkv", [128, 2, 1024], F32)
            kb = self.sb(es2, "mpkb", [128, 2, 512], BF16)
            pt = self.psn(es2, "mppt", [128, 4, 128], BF16, 2)
            fw.dma("sp", kv[:], src.rearrange("(c p) n -> p c n", p=128), reads=[skey], writes=["mpkv"])
            fw.op("dve", lambda e: e.tensor_copy(out=kb[:], in_=kv[:, :, 0:512]), reads=["mpkv"], writes=["mpkb"])
            fw.op("pool", lambda e: e.memset(va[:], 1.0), writes=[name + "va"])
            fw.op("pool", lambda e: e.tensor_copy(out=va[:, :, :, 0:128], in_=kv[:, :, 512:1024].rearrange("p c (h d) -> p c h d", h=4)),
                  reads=["mpkv"], writes=[name + "va"])
            for c in range(2):
                p = pt.next()
                for h in range(4):
                    fw.op("pe", lambda e, h=h, c=c, p=p: e.transpose(out=p[:, h, :], in_=kb[:, c, h * 128:(h + 1) * 128], identity=self.idb[:]),
                          reads=["mpkb", "idb"], writes=[self.key(p)])
                fw.op("act", lambda e, c=c, p=p: e.copy(out=kT[:, :, c * 128:(c + 1) * 128], in_=p[:]), reads=[self.key(p)], writes=[name + "kT"])
        return kT, va, name + "kT", name + "va"

    def mem_attn(self, tiles, kT, va, kTk, vak, qcol, ucol):
        fw = self.fw
        sc = 1.0 / np.sqrt(128.0)
        with self.phase() as es:
            qg = self.sbn(es, "maqg", [128, 1024], F32, 2)
            qb = self.sbn(es, "maqb", [128, 512], BF16, 2)
            qT = self.sbn(es, "maqT", [128, 4, 128], BF16, 2)
            pT = self.sbn(es, "mapT", [128, 8, 128], BF16, 2)
            rl = self.sbn(es, "marl", [128, 4], F32, 2)
            sg = self.sbn(es, "masg", [128, 512], F32, 2)
            om = self.sbn(es, "maom", [128, 512], F32, 2)
            ub = self.sbn(es, "maub", [128, 512], BF16, 2)
            pt = self.psn(es, "mapt", [128, 4, 128], BF16, 1)
            pss = self.psn(es, "mapss", [128, 8, 128], F32, 2)
            pso = self.psn(es, "mapso", [128, 4, 256], F32, 1)
            pend = [None]
            for (r0, n) in tiles:
                q, b_, t_, P, r_, s_, o_, u_ = qg.next(), qb.next(), qT.next(), pT.next(), rl.next(), sg.next(), om.next(), ub.next()
                k = self.key
                fw.dma("sp", q[:n, :], self.h_d[r0:r0 + n, qcol:qcol + 1024], reads=["D:h"], writes=[k(q)])
                fw.op("pool", lambda e, q=q, b_=b_: e.tensor_copy(out=b_[:n, :], in_=q[:n, 0:512]), reads=[k(q)], writes=[k(b_)])
                self.transposes(t_, b_, 4, n, pt, k(t_), k(b_), evac=("dve",))
                S = pss.next()
                for h in range(4):
                    for mc in range(2):
                        fw.op("pe", lambda e, h=h, mc=mc, S=S, t_=t_: e.matmul(S[:, h * 2 + mc, :n], lhsT=kT[:, h, mc * 128:(mc + 1) * 128], rhs=t_[:, h, :n], start=True, stop=True),
                              reads=[kTk, k(t_)], writes=[k(S)])
                fw.op("act", lambda e, S=S, P=P: e.activation(out=P[:, :, :n], in_=S[:, :, :n], func=AF.Exp, scale=float(sc)), reads=[k(S)], writes=[k(P)])
                fw.op("act", lambda e, q=q, s_=s_: e.activation(out=s_[:n, :], in_=q[:n, 512:1024], func=AF.Silu), reads=[k(q)], writes=[k(s_)])

                def part2(r0=r0, n=n, P=P, r_=r_, s_=s_, o_=o_, u_=u_):
                    O = pso.next()
                    for h in range(4):
                        for mc in range(2):
                            fw.op("pe", lambda e, h=h, mc=mc, O=O: e.matmul(O[:n, h, 0:129], lhsT=P[:, h * 2 + mc, :n], rhs=va[:, mc, h, :], start=(mc == 0), stop=(mc == 1)),
                                  reads=[vak, k(P)], writes=[k(O)])
                    fw.op("dve", lambda e, O=O: e.reciprocal(out=r_[:n, :], in_=O[:n, :, 128]), reads=[k(O)], writes=[k(r_)])
                    fw.op("dve", lambda e, O=O: e.tensor_tensor(out=o_[:n, :].rearrange("p (h d) -> p h d", h=4), in0=O[:n, :, 0:128],
                                                                in1=r_[:n, :].unsqueeze(2).to_broadcast([n, 4, 128]), op=ALU.mult),
                          reads=[k(O), k(r_)], writes=[k(o_)])
                    fw.op("pool", lambda e: e.tensor_tensor(out=u_[:n, :], in0=o_[:n, :], in1=s_[:n, :], op=ALU.mult), reads=[k(o_), k(s_)], writes=[k(u_)])
                    fw.dma("pool", self.u_d[r0:r0 + n, ucol:ucol + 512], u_[:n, :], reads=[k(u_)], writes=["D:u"])
                if pend[0] is not None:
                    pend[0]()
                pend[0] = part2
            if pend[0] is not None:
                pend[0]()

    def out_ln(self, l, Wout, UW, x_src, xskey, last):
        fw = self.fw
        nk = UW // 128
        SD, FM, AD = self.nc.vector.BN_STATS_DIM, self.nc.vector.BN_STATS_FMAX, self.nc.vector.BN_AGGR_DIM
        nch = D // FM
        with self.phase() as es:
            wo = self.sb(es, "olwo", [128, nk, D], BF16)
            wst = self.sbn(es, "olws", [128, D], F32, 2)
            gt = self.sb(es, "olg", [128, D], F32)
            bt = self.sb(es, "olb", [128, D], F32)
            fw.dma("sp", gt[:], self.ln_g[l].partition_broadcast(128), writes=["olg"])
            fw.dma("sp", bt[:], self.ln_b[l].partition_broadcast(128), writes=["olb"])
            for kk in range(nk):
                w = wst.next()
                fw.dma("sp" if kk % 2 else "act", w[:], Wout[kk * 128:(kk + 1) * 128, :], writes=[self.key(w)])
                fw.op("pool" if kk % 2 else "dve", lambda e, w=w, kk=kk: e.tensor_copy(out=wo[:, kk, :], in_=w[:]), reads=[self.key(w)], writes=[("olwo", kk)])
            wkeys = [("olwo", kk) for kk in range(nk)]
            ut = self.sbn(es, "olu", [128, UW], BF16, 2)
            uT = self.sbn(es, "oluT", [128, nk, 128], BF16, 2)
            xt = self.sbn(es, "olx", [128, D], F32, 2)
            zt = self.sbn(es, "olz", [128, D], F32, 3)
            st = self.sbn(es, "olst", [128, nch, SD], F32, 2)
            mv = self.sbn(es, "olmv", [128, AD], F32, 2)
            rs = self.sbn(es, "olrs", [128, 1], F32, 2)
            xb = self.sbn(es, "olxb", [128, D], BF16, 2)
            xT = self.sbn(es, "olxT", [128, KC, 128], BF16, 2)
            pt = self.psn(es, "olpt", [128, 4, 128], BF16, 2)
            po = self.psn(es, "olpo", [128, 4, 512], F32, 1)
            k = self.key
            pend = None

            def tail(r0, n, z):
                b_, T_ = xb.next(), xT.next()
                fw.op("act", lambda e, b_=b_, z=z: e.copy(out=b_[:n, :], in_=z[:n, :]), reads=[k(z)], writes=[k(b_)])
                self.transposes(T_, b_, KC, n, pt, k(T_), k(b_))
                fw.dma("pool", self.xT_d.rearrange("(k p) t -> p k t", p=128)[:, :, r0:r0 + n], T_[:, :, :n], reads=[k(T_)], writes=["D:xT"])

            for (r0, n) in self.tiles:
                u, uT_, x_, z, s_, m_, r_ = ut.next(), uT.next(), xt.next(), zt.next(), st.next(), mv.next(), rs.next()
                fw.dma("sp", u[:n, :], self.u_d[r0:r0 + n, 0:UW], reads=["D:u"], writes=[k(u)])
                fw.dma("sp", x_[:n, :], x_src[r0:r0 + n, :], reads=[xskey], writes=[k(x_)])
                self.transposes(uT_, u, nk, n, pt, k(uT_), k(u))
                P = po.next()
                for kk in range(nk):
                    for nb in range(4):
                        fw.op("pe", lambda e, nb=nb, kk=kk, P=P, uT_=uT_: e.matmul(P[:n, nb, :], lhsT=uT_[:, kk, :n], rhs=wo[:, kk, nb * 512:(nb + 1) * 512], start=(kk == 0), stop=(kk == nk - 1)),
                              reads=[k(uT_), ("olwo", kk)], writes=[k(P)])
                if pend is not None:
                    tail(*pend)
                    pend = None
                fw.op("dve", lambda e, z=z, x_=x_, P=P: e.scalar_tensor_tensor(out=z[:n, :], in0=x_[:n, :], scalar=float(self.ALPHA), in1=P[:n].rearrange("p a b -> p (a b)"), op0=ALU.mult, op1=ALU.add),
                      reads=[k(x_), k(P)], writes=[k(z)])
                for c in range(nch):
                    fw.op("dve", lambda e, c=c, s_=s_, z=z: e.bn_stats(out=s_[:n, c, :], in_=z[:n, c * FM:(c + 1) * FM]), reads=[k(z)], writes=[k(s_)])
                fw.op("dve", lambda e, s_=s_, m_=m_: e.bn_aggr(out=m_[:n, :], in_=s_[:n]), reads=[k(s_)], writes=[k(m_)])
                fw.op("dve", lambda e, r_=r_, m_=m_: e.tensor_scalar_add(out=r_[:n, :], in0=m_[:n, 1:2], scalar1=LN_EPS), reads=[k(m_)], writes=[k(r_)])
                fw.op("act", lambda e, r_=r_: e.sqrt(out=r_[:n, :], in_=r_[:n, :]), reads=[k(r_)], writes=[k(r_)])
                fw.op("dve", lambda e, r_=r_: e.reciprocal(out=r_[:n, :], in_=r_[:n, :]), reads=[k(r_)], writes=[k(r_)])
                fw.op("dve", lambda e, z=z, m_=m_, r_=r_: e.tensor_scalar(out=z[:n, :], in0=z[:n, :], scalar1=m_[:n, 0:1], scalar2=r_[:n, 0:1], op0=ALU.subtract, op1=ALU.mult),
                      reads=[k(z), k(m_), k(r_)], writes=[k(z)])
                fw.op("pool", lambda e, z=z: e.tensor_tensor(out=z[:n, :], in0=z[:n, :], in1=gt[:n, :], op=ALU.mult), reads=[k(z), "olg"], writes=[k(z)])
                fw.op("pool", lambda e, z=z: e.tensor_tensor(out=z[:n, :], in0=z[:n, :], in1=bt[:n, :], op=ALU.add), reads=[k(z), "olb"], writes=[k(z)])
                if last:
                    fw.dma("pool", self.y[r0:r0 + n, :], z[:n, :], reads=[k(z)], writes=["D:y"])
                else:
                    fw.dma("pool", self.x_d[l % 2][r0:r0 + n, :], z[:n, :], reads=[k(z)], writes=[f"D:x{l % 2}"])
                    pend = (r0, n, z)
            if pend is not None:
                tail(*pend)

    def build(self, stop_after=None):
        fw = self.fw
        self.declare()
        with self.phase() as es:
            self.load_consts(es)
            self.phase_xT_from(self.x_in, self.tiles, self.xT_d, "D:xT")
            self.phase_xT_from(self.memp, [(0, 128), (128, 128)], self.memT_d, "D:memT")
            blocks = self.tok_blocks()
            mblocks = [(0, 256, [(0, 128, 0), (128, 128, 128)])]
            for l in range(self.DEPTH):
                even = self.kinds[l] == "e"
                i2 = sum(1 for q in self.kinds[:l] if q == self.kinds[l])
                W = self.w_in_even[i2] if even else self.w_in_odd[i2]
                ncol = EVEN_IN if even else ODD_IN
                self.proj(W, ncol, self.xT_d, "D:xT", blocks, self.h_d, "D:h")
                self.proj(self.w_mem_kv[l], 1024, self.memT_d, "D:memT", mblocks, self.o_mem_p[l], "D:omem")
                qcol = ncol - 1024
                ucol = (EVEN_OUT if even else ODD_OUT) - 512
                with self.phase() as es2:
                    kT, va, kTk, vak = self.mem_prep(es2, self.o_mem_p[l], "D:omem", "mp")
                    self.mem_attn(self.tiles[:-1], kT, va, kTk, vak, qcol, ucol)
                with self.phase() as es2:
                    kT, va, kTk, vak = self.mem_prep(es2, self.c_mem[l], "D:cmem", "ms")
                    self.mem_attn(self.tiles[-1:], kT, va, kTk, vak, qcol, ucol)
                if even:
                    self.even_mixers(i2)
                else:
                    self.odd_mixers(i2)
                x_src, xk = (self.x_in, "D:xin") if l == 0 else (self.x_d[(l - 1) % 2], f"D:x{(l - 1) % 2}")
                self.out_ln(l, self.w_out_even[i2] if even else self.w_out_odd[i2], EVEN_OUT if even else ODD_OUT, x_src, xk, l == self.DEPTH - 1)
            fw.finish()
        fw.close()
        return self.nc

    def even_mixers(self, e):
        self.dwa(e)
        self.rwkv(e)

    def dwa(self, e):
        fw, k = self.fw, self.key
        SEQ = self.SEQ
        QC, KCOL, VC, GB = 3200, 3968, 4736, 5504
        hp_ = self.h_d[0:SEQ, :]
        for g in range(3):
            kp = self.keep[g]
            for j, col in enumerate((KCOL + g * 256, VC + g * 256)):
                fw.dma("act", self.o_g_p[g][e][:, j * 256:(j + 1) * 256], self.h_d[SEQ - kp:SEQ, col:col + 256], reads=["D:h"], writes=["D:ogp"])
                fw.dma("act", self.o_g_s[g][e][:, j * 256:(j + 1) * 256], self.h_d[SEQ:SEQ + TS, col:col + 256], reads=["D:h"], writes=["D:ogs"])
        fw.dma("act", self.o_shift_p[e], self.h_d[SEQ - 1, 0:A_SHIFT], reads=["D:h"], writes=["D:osh"])
        fw.dma("act", self.o_shift_s[e], self.h_d[SEQ + TS - 1, 0:A_SHIFT], reads=["D:h"], writes=["D:osh"])
        units = []
        for g, (win, d) in enumerate(B_CONFIGS):
            qc, kc, vc = QC + g * 256, KCOL + g * 256, VC + g * 256
            Lc = SEQ // d
            n = min(128, Lc)
            hv = hp_.rearrange("(m d) c -> d m c", d=d)
            ov = self.dwa_d[g][0:SEQ, :].rearrange("(m d) c -> d m c", d=d)
            for r in range(d):
                for m0 in range(0, Lc, n):
                    cur = hv[r, m0:m0 + n]
                    prev = hv[r, m0 - n:m0] if m0 > 0 else None
                    units.append((cur[:, qc:qc + 256], cur[:, kc:kc + 256], cur[:, vc:vc + 256], n,
                                  None if prev is None else prev[:, kc:kc + 256], None if prev is None else prev[:, vc:vc + 256], n, ov[r, m0:m0 + n], "D:h"))
            cg = self.c_g[g][e]
            if d == 1:
                cur = self.h_d[SEQ:SEQ + TS]
                units.append((cur[:, qc:qc + 256], cur[:, kc:kc + 256], cur[:, vc:vc + 256], TS, cg[:, 0:256], cg[:, 256:512], 128, self.dwa_d[g][SEQ:SEQ + TS, :], "D:cg"))
            else:
                cv = cg.rearrange("(m d) c -> d m c", d=d)
                for i in range(TS):
                    cur = self.h_d[SEQ + i:SEQ + i + 1]
                    units.append((cur[:, qc:qc + 256], cur[:, kc:kc + 256], cur[:, vc:vc + 256], 1, cv[i][:, 0:256], cv[i][:, 256:512], 128, self.dwa_d[g][SEQ + i:SEQ + i + 1, :], "D:cg"))
        import os
        bis = os.environ.get("DWA_BIS", "")
        if bis == "none":
            units = []
        elif bis == "prompt":
            units = [u for u in units if u[8] == "D:h"]
        elif bis == "g0":
            units = units[:4]
        elif bis == "samp":
            units = [u for u in units if u[8] != "D:h"]
        with self.phase() as es:
            Xc = self.sbn(es, "dwXc", [128, 3, 256], F32, 2)
            Xp = self.sbn(es, "dwXp", [128, 2, 256], F32, 2)
            qkb = self.sbn(es, "dwqkb", [128, 3, 256], BF16, 2)
            Va = self.sbn(es, "dwVa", [128, 2, 4, 65], BF16, 2)
            T = self.sbn(es, "dwT", [64, 3, 4, 128], BF16, 2)
            P = self.sbn(es, "dwP", [128, 4, 2, 128], BF16, 2)
            Os = self.sbn(es, "dwOs", [128, 260], F32, 2)
            pt = self.psn(es, "dwpt", [128, 4, 128], BF16, 2)
            pS = self.psn(es, "dwpS", [128, 4, 2, 128], F32, 2)
            pO = self.psn(es, "dwpO", [128, 4, 128], F32, 2)
            pend2 = [None]
            for (qs, ks, vs, n, pks, pvs, npv, dst, pkey) in units:
                xc, xp, qb, va, t, p, os_ = Xc.next(), Xp.next(), qkb.next(), Va.next(), T.next(), P.next(), Os.next()
                hasp = pks is not None
                fw.dma("sp", xc[:n, 0, :], qs, reads=["D:h"], writes=[(k(xc), 0)])
                fw.dma("sp", xc[:n, 1, :], ks, reads=["D:h"], writes=[(k(xc), 1)])
                fw.dma("sp", xc[:n, 2, :], vs, reads=["D:h"], writes=[(k(xc), 2)])
                fw.op("pool", lambda e_, va=va: e_.memset(va[:], 1.0), writes=[k(va)])
                fw.op("dve", lambda e_, xc=xc, qb=qb: e_.tensor_copy(out=qb[:n, 0:2, :], in_=xc[:n, 0:2, :]), reads=[(k(xc), 0), (k(xc), 1)], writes=[(k(qb), 0)])
                fw.op("pool", lambda e_, xc=xc, va=va: e_.tensor_copy(out=va[:n, 0, :, 0:64], in_=xc[:n, 2, :].rearrange("p (h c) -> p h c", h=4)), reads=[(k(xc), 2)], writes=[k(va)])
                if hasp:
                    fw.dma("sp", xp[:npv, 0, :], pks, reads=[pkey], writes=[(k(xp), 0)])
                    fw.dma("sp", xp[:npv, 1, :], pvs, reads=[pkey], writes=[(k(xp), 1)])
                    fw.op("dve", lambda e_, xp=xp, qb=qb: e_.tensor_copy(out=qb[:npv, 2, :], in_=xp[:npv, 0, :]), reads=[(k(xp), 0)], writes=[(k(qb), 1)])
                    fw.op("pool", lambda e_, xp=xp, va=va: e_.tensor_copy(out=va[:npv, 1, :, 0:64], in_=xp[:npv, 1, :].rearrange("p (h c) -> p h c", h=4)), reads=[(k(xp), 1)], writes=[k(va)])
                stage = int(os.environ.get("DWA_STAGE", "9"))
                if stage < 2:
                    continue
                for w_, (nn, rk) in enumerate(((n, 0), (n, 0), (npv, 1))):
                    if w_ == 2 and not hasp:
                        continue
                    pp_ = pt.next()
                    for b_ in range(4):
                        fw.op("pe", lambda e_, w_=w_, b_=b_, pp_=pp_, qb=qb, nn=nn: e_.transpose(out=pp_[0:64, b_, :nn], in_=qb[:nn, w_, b_ * 64:(b_ + 1) * 64], identity=self.idb[:nn, :nn]),
                              reads=[(k(qb), rk), "idb"], writes=[k(pp_)])
                    fw.op("act" if w_ % 2 else "dve", (lambda e_, w_=w_, pp_=pp_, t=t, nn=nn: e_.copy(out=t[:, w_, :, :nn], in_=pp_[0:64, :, :nn])) if w_ % 2 else
                          (lambda e_, w_=w_, pp_=pp_, t=t, nn=nn: e_.tensor_copy(out=t[:, w_, :, :nn], in_=pp_[0:64, :, :nn])), reads=[k(pp_)], writes=[(k(t), w_)])
                if stage < 3:
                    continue
                S = pS.next()
                for h in range(4):
                    fw.op("pe", lambda e_, h=h, S=S, t=t: e_.matmul(S[:n, h, 0, :n], lhsT=t[:, 1, h, :n], rhs=t[:, 0, h, :n], start=True, stop=True),
                          reads=[(k(t), 0), (k(t), 1)], writes=[k(S)])
                    if hasp:
                        fw.op("pe", lambda e_, h=h, S=S, t=t: e_.matmul(S[:npv, h, 1, :n], lhsT=t[:, 2, h, :npv], rhs=t[:, 0, h, :n], start=True, stop=True),
                              reads=[(k(t), 0), (k(t), 2)], writes=[k(S)])
                if stage < 4:
                    continue
                fw.op("act", lambda e_, S=S, p=p: e_.activation(out=p[:n, :, 0, :n], in_=S[:n, :, 0, :n], func=AF.Exp, scale=0.125), reads=[k(S)], writes=[(k(p), 0)])
                fw.op("pool", lambda e_, p=p: e_.tensor_tensor(out=p[:n, :, 0, :n], in0=p[:n, :, 0, :n], in1=self.maskb[:n, 0, :n].unsqueeze(1).to_broadcast([n, 4, n]), op=ALU.mult),
                      reads=[(k(p), 0), "maskb"], writes=[(k(p), 0)])
                if hasp:
                    fw.op("act", lambda e_, S=S, p=p: e_.activation(out=p[:npv, :, 1, :n], in_=S[:npv, :, 1, :n], func=AF.Exp, scale=0.125), reads=[k(S)], writes=[(k(p), 1)])
                    fw.op("dve", lambda e_, p=p: e_.tensor_tensor(out=p[:npv, :, 1, :n], in0=p[:npv, :, 1, :n], in1=self.maskb[:npv, 1, :n].unsqueeze(1).to_broadcast([npv, 4, n]), op=ALU.mult),
                          reads=[(k(p), 1), "maskb"], writes=[(k(p), 1)])
                def part2(n=n, npv=npv, hasp=hasp, p=p, va=va, os_=os_, dst=dst):
                    O = pO.next()
                    for h in range(4):
                        fw.op("pe", lambda e_, h=h, O=O: e_.matmul(O[:n, h, 0:65], lhsT=p[:n, h, 0, :n], rhs=va[:n, 0, h, :], start=True, stop=not hasp),
                              reads=[(k(p), 0), k(va)], writes=[k(O)])
                        if hasp:
                            fw.op("pe", lambda e_, h=h, O=O: e_.matmul(O[:n, h, 0:65], lhsT=p[:npv, h, 1, :n], rhs=va[:npv, 1, h, :], start=False, stop=True),
                                  reads=[(k(p), 1), k(va)], writes=[k(O)])
                    fw.op("act", lambda e_, O=O: e_.copy(out=os_[:n, :].rearrange("p (h c) -> p h c", h=4), in_=O[:n, :, 0:65]), reads=[k(O)], writes=[k(os_)])
                    fw.dma("pool", dst, os_[:n, :], reads=[k(os_)], writes=["D:dwa"])
                if pend2[0] is not None:
                    pend2[0]()
                pend2[0] = part2
            if pend2[0] is not None:
                pend2[0]()
        with self.phase() as es:
            A = self.sbn(es, "dcA", [128, 3, 260], F32, 2)
            G = self.sbn(es, "dcG", [128, 256], F32, 2)
            rl = self.sbn(es, "dcrl", [128, 4], F32, 2)
            Y = self.sbn(es, "dcY", [128, 256], F32, 2)
            U = self.sbn(es, "dcU", [128, 256], BF16, 2)
            for (r0, n) in self.tiles:
                a, g_, r_, y_, u_ = A.next(), G.next(), rl.next(), Y.next(), U.next()
                fw.dma("sp", a[:n], self.dwa_d[:, r0:r0 + n, :].rearrange("g p c -> p g c"), reads=["D:dwa"], writes=[k(a)])
                fw.dma("sp", g_[:n, :], self.h_d[r0:r0 + n, GB:GB + 256], reads=["D:h"], writes=[k(g_)])
                fw.op("dve", lambda e_, a=a: e_.tensor_tensor(out=a[:n, 0, :], in0=a[:n, 0, :], in1=a[:n, 1, :], op=ALU.add), reads=[k(a)], writes=[k(a)])
                fw.op("dve", lambda e_, a=a: e_.tensor_tensor(out=a[:n, 0, :], in0=a[:n, 0, :], in1=a[:n, 2, :], op=ALU.add), reads=[k(a)], writes=[k(a)])
                a4 = a[:n, 0, :].rearrange("p (h c) -> p h c", h=4)
                fw.op("dve", lambda e_, a4=a4, r_=r_: e_.reciprocal(out=r_[:n, :], in_=a4[:, :, 64]), reads=[k(a)], writes=[k(r_)])
                fw.op("act", lambda e_, g_=g_: e_.activation(out=g_[:n, :], in_=g_[:n, :], func=AF.Silu), reads=[k(g_)], writes=[k(g_)])
                fw.op("dve", lambda e_, a4=a4, r_=r_, y_=y_: e_.tensor_tensor(out=y_[:n, :].rearrange("p (h c) -> p h c", h=4), in0=a4[:, :, 0:64], in1=r_[:n, :].unsqueeze(2).to_broadcast([n, 4, 64]), op=ALU.mult),
                      reads=[k(a), k(r_)], writes=[k(y_)])
                fw.op("pool", lambda e_, y_=y_, g_=g_, u_=u_: e_.tensor_tensor(out=u_[:n, :], in0=y_[:n, :], in1=g_[:n, :], op=ALU.mult), reads=[k(y_), k(g_)], writes=[k(u_)])
                fw.dma("pool", self.u_d[r0:r0 + n, A_W:A_W + 256], u_[:n, :], reads=[k(u_)], writes=["D:u"])

    def rwkv(self, e):
        fw, k = self.fw, self.key
        SEQ = self.SEQ
        NEG = -float(np.exp(-0.5))
        with self.phase() as es:
            def bc(name, src, w):
                t = self.sb(es, name, [128, w], F32)
                fw.dma("sp", t[:], src.partition_broadcast(128), writes=[name])
                return t
            mu = bc("rwmu", self.rw["mu"][e], A_SHIFT)
            w0 = bc("rww0", self.rw["w0"][e], A_W)
            a0 = bc("rwa0", self.rw["a0"][e], A_W)
            k_k = bc("rwkk", self.rw["k_k"][e], A_W)
            k_a = bc("rwka", self.rw["k_a"][e], A_W)
            r_k = bc("rwrk", self.rw["r_k"][e], A_W)
            lg = bc("rwlg", self.rw["lnx_g"][e], A_W)
            lb = bc("rwlb", self.rw["lnx_b"][e], A_W)
            wup = self.sb(es, "rwwup", [64, 2, A_W], F32)
            fw.dma("sp", wup[:, 0, :], self.rw["w_up"][e], writes=["rwwup"])
            fw.dma("sp", wup[:, 1, :], self.rw["a_up"][e], writes=["rwwup"])
            ones = self.sb(es, "rwones", [128, 1], F32)
            fw.op("pool", lambda e_: e_.memset(ones[:], 1.0), writes=["rwones"])
            H = self.sbn(es, "rwH", [128, A_SHIFT], F32, 1)
            hs = self.sb(es, "rwhs", [128, A_SHIFT], F32)
            lT = self.sb(es, "rwlT", [64, 2, 128], F32)
            sw = self.sb(es, "rwsw", [128, A_W], F32)
            av = self.sb(es, "rwa", [128, A_W], F32)
            kk = self.sb(es, "rwkkv", [128, A_W], F32)
            tmp = self.sb(es, "rwtmp", [128, A_W], F32)
            tmp2 = self.sb(es, "rwtmp2", [128, A_W], F32)
            s12p = self.sb(es, "rws12p", [128, 2, 12], F32)
            kmod = self.sb(es, "rwkmod", [128, A_W], F32)
            cs = self.sb(es, "rwcs", [128, A_W], F32)
            E3 = self.sb(es, "rwE3", [128, 3, A_W], BF16)
            raw = self.sb(es, "rwraw", [128, 12, 128], BF16)
            Pm = {nm: self.sb(es, "rwM" + nm, [128, 12, 128], BF16) for nm in ("P", "PT", "Pn", "PTn")}
            setsA, setsB = [], []
            for si in range(3):
                d = {}
                d["X4"] = self.sb(es, f"rwX4{si}", [128, 4, A_W], BF16)
                d["vb"] = self.sb(es, f"rwvb{si}", [128, A_W], BF16)
                d["gC"] = self.sb(es, f"rwgC{si}", [64, 12], F32)
                d["bon"] = self.sb(es, f"rwbon{si}", [128, A_W], F32)
                d["sg"] = self.sb(es, f"rwsg{si}", [128, A_W], F32)
                setsA.append(d)
            for si in range(2):
                d = {}
                d["XT"] = self.sb(es, f"rwXT{si}", [64, 4, 12, 128], BF16)
                for nm in ("AkT", "BbT", "BkT", "WT"):
                    d[nm] = self.sb(es, f"rwM{nm}{si}", [128, 12, 128], BF16)
                setsB.append(d)
            RH = self.sb(es, "rwRH", [128, 12, 64], BF16)
            Un = self.sb(es, "rwUn", [128, 12, 64], BF16)
            Z = self.sb(es, "rwZ", [64, 12, 64], F32)
            Zb = self.sb(es, "rwZb", [64, 12, 64], BF16)
            Y = self.sb(es, "rwY", [128, A_W], F32)
            Y2 = self.sb(es, "rwY2", [128, A_W], F32)
            Ssb = Y2[0:64, :].rearrange("p (h c) -> p h c", h=12)
            s12 = self.sb(es, "rws12", [128, 3, 12], F32)
            U = self.sbn(es, "rwU", [128, A_W], BF16, 2)
            print("rwkv sbuf remaining", self.nc.sbuf_bytes_remaining)
            pb = self.psn(es, "rwpb", [128, 512], F32, 8)

            def tt(en, out, in0, in1, op, reads, writes):
                fw.op(en, lambda e_: e_.tensor_tensor(out=out, in0=in0, in1=in1, op=op), reads=reads, writes=writes)

            def acopy(dst, src, reads, writes, scale=None):
                if scale is None:
                    fw.op("act", lambda e_: e_.copy(out=dst, in_=src), reads=reads, writes=writes)
                else:
                    fw.op("act", lambda e_: e_.mul(out=dst, in_=src, mul=float(scale)), reads=reads, writes=writes)

            def dcopy(dst, src, reads, writes):
                fw.op("dve", lambda e_: e_.tensor_copy(out=dst, in_=src), reads=reads, writes=writes)

            v3 = lambda t_: t_.rearrange("p (h c) -> p h c", h=12)
            allk = lambda nm: [nm, (nm, 0), (nm, 1), (nm, 2)]

            def bank6(n_, bf=False):
                B = pb.next()
                if bf:
                    return B, B[:n_, :].bitcast(BF16)[:, 0:512].rearrange("p (h c) -> p h c", h=4)
                return B, B[:n_, 0:512].rearrange("p (h c) -> p h c", h=4)

            def stageA(ci, r0, n, seg, S):
                X4, gC, vb, bon, sg = S["X4"], S["gC"], S["vb"], S["bon"], S["sg"]
                kX4 = k(X4)
                h_ = H.next()
                fw.dma("sp", h_[:n, :], self.h_d[r0:r0 + n, 0:A_SHIFT], reads=["D:h"], writes=[k(h_)])
                fw.dma("sp", sg[:n, :], self.h_d[r0:r0 + n, A_SHIFT:3200], reads=["D:h"], writes=[k(sg)])
                if ci == 0:
                    if seg == 0:
                        fw.op("pool", lambda e_: e_.memset(hs[0:1, :], 0.0), writes=["rwhs"])
                    else:
                        fw.dma("act", hs[0:1, :], self.st_shift[e:e + 1, :], writes=["rwhs"])
                    if n > 1:
                        fw.dma("act", hs[1:n, :], self.h_d[r0:r0 + n - 1, 0:A_SHIFT], reads=["D:h"], writes=[("rwhs", 1)])
                else:
                    fw.dma("act", hs[:n, :], self.h_d[r0 - 1:r0 + n - 1, 0:A_SHIFT], reads=["D:h"], writes=["rwhs", ("rwhs", 1)])
                hpk = ["rwhs", ("rwhs", 1)]
                tt("dve", hs[:n, :], hs[:n, :], h_[:n, 0:A_SHIFT], ALU.subtract, hpk + [k(h_)], hpk)
                tt("pool", hs[:n, :], hs[:n, :], mu[:n, :], ALU.mult, hpk + ["rwmu"], hpk)
                tt("dve", hs[:n, :], hs[:n, :], h_[:n, 0:A_SHIFT], ALU.add, hpk + [k(h_)], hpk)
                r_, k_, v_ = hs[:n, 0:768], hs[:n, 768:1536], hs[:n, 1536:2304]
                fw.op("act", lambda e_: e_.activation(out=sg[:n, :], in_=sg[:n, :], func=AF.Silu), reads=[k(sg)], writes=[k(sg)])
                fw.op("act", lambda e_: e_.activation(out=hs[:n, 2304:2368], in_=hs[:n, 2304:2368], func=AF.Tanh), reads=["rwhs"], writes=["rwhs"])
                acopy(vb[:n, :], v_, ["rwhs"], [k(vb)])
                yield
                B = pb.next()
                for j in range(2):
                    fw.op("pe", lambda e_, j=j, B=B: e_.transpose(out=B[:64, j * 128:j * 128 + n], in_=hs[:n, 2304 + j * 64:2368 + j * 64], identity=self.idf[:n, :n]), reads=["rwhs", "idf"], writes=[k(B)])
                dcopy(lT[:, :, :n], B[:64, 0:256].rearrange("p (a c) -> p a c", a=2)[:, :, :n], [k(B)], ["rwlT"])
                yield
                lb_ = {}
                for j in range(2):
                    for hf in range(2):
                        B = pb.next()
                        lb_[(j, hf)] = B
                        fw.op("pe", lambda e_, j=j, hf=hf, B=B: e_.matmul(B[:n, 0:384], lhsT=lT[:, j, :n], rhs=wup[:, j, hf * 384:(hf + 1) * 384], start=True, stop=True), reads=["rwlT", "rwwup"], writes=[k(B)])
                for j, (dst, off, dk_, ok_) in enumerate(((sw, w0, "rwsw", "rww0"), (av, a0, "rwa", "rwa0"))):
                    for hf in range(2):
                        B = lb_[(j, hf)]
                        tt("dve", dst[:n, hf * 384:(hf + 1) * 384], B[:n, 0:384], off[:n, hf * 384:(hf + 1) * 384], ALU.add, [k(B), ok_], [dk_])
                yield
                fw.op("act", lambda e_: e_.activation(out=sw[:n, :], in_=sw[:n, :], func=AF.Sigmoid), reads=["rwsw"], writes=["rwsw"])
                fw.op("act", lambda e_: e_.activation(out=av[:n, :], in_=av[:n, :], func=AF.Sigmoid), reads=["rwa"], writes=["rwa"])
                yield
                tt("dve", kk[:n, :], k_, k_k[:n, :], ALU.mult, ["rwhs", "rwkk"], ["rwkkv"])
                tt("pool", tmp[:n, :], kk[:n, :], kk[:n, :], ALU.mult, ["rwkkv"], ["rwtmp"])
                fw.op("dve", lambda e_: e_.tensor_reduce(out=s12p[:n, 0, :], in_=v3(tmp[:n, :]), axis=AX.X, op=ALU.add), reads=["rwtmp"], writes=["rws12p"])
                fw.op("dve", lambda e_: e_.tensor_scalar_max(out=s12p[:n, 0, :], in0=s12p[:n, 0, :], scalar1=1e-24), reads=["rws12p"], writes=["rws12p"])
                yield
                fw.op("act", lambda e_: e_.sqrt(out=s12p[:n, 0, :], in_=s12p[:n, 0, :]), reads=["rws12p"], writes=["rws12p"])
                fw.op("dve", lambda e_: e_.reciprocal(out=s12p[:n, 0, :], in_=s12p[:n, 0, :]), reads=["rws12p"], writes=["rws12p"])
                tt("dve", v3(kk[:n, :]), v3(kk[:n, :]), s12p[:n, 0, :].unsqueeze(2).to_broadcast([n, 12, 64]), ALU.mult, ["rwkkv", "rws12p"], ["rwkkv"])
                fw.op("dve", lambda e_: e_.scalar_tensor_tensor(out=tmp[:n, :], in0=av[:n, :], scalar=-1.0, in1=k_a[:n, :], op0=ALU.add, op1=ALU.mult), reads=["rwa", "rwka"], writes=["rwtmp"])
                fw.op("dve", lambda e_: e_.scalar_tensor_tensor(out=kmod[:n, :], in0=tmp[:n, :], scalar=1.0, in1=k_, op0=ALU.add, op1=ALU.mult), reads=["rwtmp", "rwhs"], writes=["rwkmod"])
                tt("pool", tmp2[:n, :], r_, kmod[:n, :], ALU.mult, ["rwhs", "rwkmod"], ["rwtmp2"])
                tt("pool", tmp2[:n, :], tmp2[:n, :], r_k[:n, :], ALU.mult, ["rwtmp2", "rwrk"], ["rwtmp2"])
                fw.op("dve", lambda e_: e_.tensor_reduce(out=s12p[:n, 1, :], in_=v3(tmp2[:n, :]), axis=AX.X, op=ALU.add), reads=["rwtmp2"], writes=["rws12p"])
                tt("pool", v3(bon[:n, :]), v3(v_), s12p[:n, 1, :].unsqueeze(2).to_broadcast([n, 12, 64]), ALU.mult, ["rwhs", "rws12p"], [k(bon)])
                yield
                for hf in range(2):
                    B = pb.next()
                    fw.op("pe", lambda e_, hf=hf, B=B: e_.matmul(B[:n, 0:384], lhsT=self.maskf[:n, 0, :n], rhs=sw[:n, hf * 384:(hf + 1) * 384], start=True, stop=True), reads=["maskf", "rwsw"], writes=[k(B)])
                    if hf:
                        acopy(cs[:n, hf * 384:(hf + 1) * 384], B[:n, 0:384], [k(B)], [("rwcs", hf)])
                    else:
                        dcopy(cs[:n, hf * 384:(hf + 1) * 384], B[:n, 0:384], [k(B)], [("rwcs", hf)])
                csk = [("rwcs", 0), ("rwcs", 1)]
                yield
                fw.op("act", lambda e_: e_.activation(out=E3[:n, 0, :], in_=cs[:n, :], func=AF.Exp, scale=NEG), reads=csk, writes=[("rwE3", 0)])
                fw.op("act", lambda e_: e_.activation(out=E3[:n, 1, :], in_=cs[:n, :], func=AF.Exp, scale=-NEG), reads=csk, writes=[("rwE3", 1)])
                tt("dve", tmp[:n, :], cs[:n, :], sw[:n, :], ALU.subtract, csk + ["rwsw"], ["rwtmp"])
                fw.op("act", lambda e_: e_.activation(out=E3[:n, 2, :], in_=tmp[:n, :], func=AF.Exp, scale=NEG), reads=["rwtmp"], writes=[("rwE3", 2)])
                B = pb.next()
                for h in range(12):
                    fw.op("pe", lambda e_, h=h, B=B: e_.matmul(B[:64, h:h + 1], lhsT=sw[:n, h * 64:(h + 1) * 64], rhs=ones[:n, 0:1], start=True, stop=True), reads=["rwsw", "rwones"], writes=[k(B)])
                fw.op("act", lambda e_, B=B: e_.activation(out=gC[:, :], in_=B[:64, 0:12], func=AF.Exp, scale=NEG), reads=[k(B)], writes=[k(gC)])
                yield
                tt("dve", X4[:n, 0, :], kk[:n, :], E3[:n, 2, :], ALU.mult, ["rwkkv", ("rwE3", 2)], [(kX4, 0)])
                tt("pool", X4[:n, 1, :], r_, E3[:n, 0, :], ALU.mult, ["rwhs", ("rwE3", 0)], [(kX4, 1)])
                tt("dve", tmp[:n, :], kk[:n, :], av[:n, :], ALU.mult, ["rwkkv", "rwa"], ["rwtmp"])
                tt("pool", X4[:n, 2, :], tmp[:n, :], E3[:n, 1, :], ALU.mult, ["rwtmp", ("rwE3", 1)], [(kX4, 2)])
                tt("dve", X4[:n, 3, :], kmod[:n, :], E3[:n, 1, :], ALU.mult, ["rwkmod", ("rwE3", 1)], [(kX4, 3)])
                yield

            def stageB(n, nlev, SA, S):
                X4, XT = SA["X4"], S["XT"]
                kX4, kXT = k(X4), k(XT)
                for wi_, w_ in enumerate((2, 0, 3, 1)):
                    for hg in range(3):
                        B, Bv = bank6(64, bf=True)
                        for hh in range(4):
                            h = hg * 4 + hh
                            fw.op("pe", lambda e_, w_=w_, h=h, hh=hh, Bv=Bv: e_.transpose(out=Bv[:, hh, :n], in_=X4[:n, w_, h * 64:(h + 1) * 64], identity=self.idb[:n, :n]), reads=[(kX4, w_), "idb"], writes=[k(B)])
                        if (w_ + hg) % 2:
                            acopy(XT[:, w_, hg * 4:(hg + 1) * 4, :n], Bv[:, :, :n], [k(B)], [(kXT, w_)])
                        else:
                            dcopy(XT[:, w_, hg * 4:(hg + 1) * 4, :n], Bv[:, :, :n], [k(B)], [(kXT, w_)])
                    if wi_ % 2:
                        yield

                def prod(wl, wr, midx, sign, dst, dkey):
                    for hg in range(3):
                        B, Bv = bank6(n)
                        for hh in range(4):
                            h = hg * 4 + hh
                            fw.op("pe", lambda e_, h=h, hh=hh, Bv=Bv: e_.matmul(Bv[:, hh, :n], lhsT=XT[:, wl, h, :n], rhs=XT[:, wr, h, :n], start=True, stop=True), reads=[(kXT, wl), (kXT, wr)], writes=[k(B)])
                        hr = slice(hg * 4, (hg + 1) * 4)
                        acopy(raw[:n, hr, :n], Bv[:, :, :n], [k(B)], [("rwraw", hg)], scale=(sign if sign != 1.0 else None))
                        tt("pool", dst[:n, hr, :n], raw[:n, hr, :n], self.maskb[:n, midx, :n].unsqueeze(1).to_broadcast([n, 4, n]), ALU.mult, [("rwraw", hg), "maskb"], [dkey, (dkey, hg)])
                prod(2, 0, 2, -1.0, Pm["PT"], "rwMPT")
                prod(0, 2, 3, -1.0, Pm["P"], "rwMP")
                yield
                prod(3, 0, 2, 1.0, S["AkT"], k(S["AkT"]))
                prod(2, 1, 0, 1.0, S["BbT"], k(S["BbT"]))
                yield
                prod(3, 1, 0, 1.0, S["BkT"], k(S["BkT"]))
                WT, kWT = S["WT"], k(S["WT"])
                tt("dve", WT[:n, :, :n], Pm["PT"][:n, :, :n], self.idf[:n, :n].unsqueeze(1).to_broadcast([n, 12, n]), ALU.add, ["rwMPT", "idf"], allk(kWT))
                yield
                P, PT, Pn, PTn = "P", "PT", "Pn", "PTn"
                for lev in range(1, nlev):
                    for hg in range(3):
                        hr = slice(hg * 4, (hg + 1) * 4)
                        B, Bv = bank6(n)
                        for hh in range(4):
                            h = hg * 4 + hh
                            fw.op("pe", lambda e_, h=h, hh=hh, Bv=Bv, P=P, PT=PT: e_.matmul(Bv[:, hh, :n], lhsT=Pm[PT][:n, h, :n], rhs=Pm[P][:n, h, :n], start=True, stop=True), reads=allk("rwM" + P) + allk("rwM" + PT), writes=[k(B)])
                        acopy(Pm[Pn][:n, hr, :n], Bv[:, :, :n], [k(B)], [("rwM" + Pn, hg)])
                    yield
                    if lev < nlev - 1:
                        for hg in range(3):
                            hr = slice(hg * 4, (hg + 1) * 4)
                            B, Bv = bank6(n)
                            for hh in range(4):
                                h = hg * 4 + hh
                                fw.op("pe", lambda e_, h=h, hh=hh, Bv=Bv, P=P, PT=PT: e_.matmul(Bv[:, hh, :n], lhsT=Pm[P][:n, h, :n], rhs=Pm[PT][:n, h, :n], start=True, stop=True), reads=allk("rwM" + P) + allk("rwM" + PT), writes=[k(B)])
                            acopy(Pm[PTn][:n, hr, :n], Bv[:, :, :n], [k(B)], [("rwM" + PTn, hg)])
                        yield
                    for hg in range(3):
                        hr = slice(hg * 4, (hg + 1) * 4)
                        B, Bv = bank6(n)
                        for hh in range(4):
                            h = hg * 4 + hh
                            fw.op("pe", lambda e_, h=h, hh=hh, Bv=Bv, Pn=Pn: e_.matmul(Bv[:, hh, :n], lhsT=Pm[Pn][:n, h, :n], rhs=WT[:n, h, :n], start=True, stop=True), reads=[("rwM" + Pn, hg), (kWT, hg)], writes=[k(B)])
                        tt("dve", WT[:n, hr, :n], WT[:n, hr, :n], Bv[:, :, :n], ALU.add, [(kWT, hg), k(B)], [(kWT, hg)])
                    yield
                    P, Pn = Pn, P
                    PT, PTn = PTn, PT

            def solve(r0, n, SA, S):
                X4, XT, gC, vb, bon, sg = SA["X4"], S["XT"], SA["gC"], SA["vb"], SA["bon"], SA["sg"]
                kX4, kXT = k(X4), k(XT)
                AkT, BbT, BkT, WT = S["AkT"], S["BbT"], S["BkT"], S["WT"]
                for hg in range(3):
                    hr = slice(hg * 4, (hg + 1) * 4)
                    B, Bv = bank6(n)
                    for hh in range(4):
                        h = hg * 4 + hh
                        fw.op("pe", lambda e_, h=h, hh=hh, Bv=Bv: e_.matmul(Bv[:, hh, 0:64], lhsT=XT[:, 0, h, :n], rhs=Zb[:, h, :], start=True, stop=False), reads=[(kXT, 0), "rwZb"], writes=[k(B)])
                        fw.op("pe", lambda e_, h=h, hh=hh, Bv=Bv: e_.matmul(Bv[:, hh, 0:64], lhsT=AkT[:n, h, :n], rhs=vb[:n, h * 64:(h + 1) * 64], start=False, stop=True), reads=[k(AkT), k(vb)], writes=[k(B)])
                    if hg % 2:
                        acopy(RH[:n, hr, :], Bv[:, :, 0:64], [k(B)], [("rwRH", hg)])
                    else:
                        dcopy(RH[:n, hr, :], Bv[:, :, 0:64], [k(B)], [("rwRH", hg)])
                yield
                for hg in range(3):
                    hr = slice(hg * 4, (hg + 1) * 4)
                    B, Bv = bank6(n)
                    for hh in range(4):
                        h = hg * 4 + hh
                        fw.op("pe", lambda e_, h=h, hh=hh, Bv=Bv: e_.matmul(Bv[:, hh, 0:64], lhsT=WT[:n, h, :n], rhs=RH[:n, h, :], start=True, stop=True), reads=[k(WT), (k(WT), hg), ("rwRH", hg)], writes=[k(B)])
                    acopy(Un[:n, hr, :], Bv[:, :, 0:64], [k(B)], [("rwUn", hg)], scale=-1.0)
                yield
                for hg in range(3):
                    hr = slice(hg * 4, (hg + 1) * 4)
                    B = pb.next()
                    Bv = B[:64, 0:512].rearrange("p (h c) -> p h c", h=4)[:, :, 0:64]
                    B2, Bv2 = bank6(n)
                    for hh in range(4):
                        h = hg * 4 + hh
                        fw.op("pe", lambda e_, h=h, hh=hh, Bv2=Bv2: e_.matmul(Bv2[:, hh, 0:64], lhsT=XT[:, 1, h, :n], rhs=Zb[:, h, :], start=True, stop=False), reads=[(kXT, 1), "rwZb"], writes=[k(B2)])
                        fw.op("pe", lambda e_, h=h, hh=hh, Bv2=Bv2: e_.matmul(Bv2[:, hh, 0:64], lhsT=BbT[:n, h, :n], rhs=Un[:n, h, :], start=False, stop=False), reads=[k(BbT), ("rwUn", hg)], writes=[k(B2)])
                        fw.op("pe", lambda e_, h=h, hh=hh, Bv2=Bv2: e_.matmul(Bv2[:, hh, 0:64], lhsT=BkT[:n, h, :n], rhs=vb[:n, h * 64:(h + 1) * 64], start=False, stop=True), reads=[k(BkT), k(vb)], writes=[k(B2)])
                    for hh in range(4):
                        h = hg * 4 + hh
                        fw.op("pe", lambda e_, h=h, hh=hh, Bv=Bv: e_.matmul(Bv[:, hh, 0:64], lhsT=X4[:n, 2, h * 64:(h + 1) * 64], rhs=Un[:n, h, :], start=True, stop=False), reads=[(kX4, 2), ("rwUn", hg)], writes=[k(B)])
                        fw.op("pe", lambda e_, h=h, hh=hh, Bv=Bv: e_.matmul(Bv[:, hh, 0:64], lhsT=X4[:n, 3, h * 64:(h + 1) * 64], rhs=vb[:n, h * 64:(h + 1) * 64], start=False, stop=True), reads=[(kX4, 3), k(vb)], writes=[k(B)])
                    tt("dve", Z[:, hr, :], Z[:, hr, :], Bv, ALU.add, ["rwZ", k(B)], ["rwZ"])
                    tt("pool", Z[:, hr, :], Z[:, hr, :], gC[:, hr].unsqueeze(2).to_broadcast([64, 4, 64]), ALU.mult, ["rwZ", k(gC)], ["rwZ"])
                    acopy(Zb[:, hr, :], Z[:, hr, :], ["rwZ"], ["rwZb"])
                    dcopy(Y[:n, hg * 256:(hg + 1) * 256].rearrange("p (h c) -> p h c", h=4), Bv2[:, :, 0:64], [k(B2)], [("rwY", hg)])
                yield
                yk = [("rwY", 0), ("rwY", 1), ("rwY", 2)]
                b12 = lambda col: s12[:n, col, :].unsqueeze(2).to_broadcast([n, 12, 64])
                fw.op("dve", lambda e_: e_.tensor_reduce(out=s12[:n, 0, :], in_=v3(Y[:n, :]), axis=AX.X, op=ALU.add), reads=yk, writes=["rws12"])
                tt("pool", Y2[:n, :], Y[:n, :], Y[:n, :], ALU.mult, yk, ["rwY2"])
                fw.op("dve", lambda e_: e_.tensor_reduce(out=s12[:n, 1, :], in_=v3(Y2[:n, :]), axis=AX.X, op=ALU.add), reads=["rwY2"], writes=["rws12"])
                fw.op("dve", lambda e_: e_.tensor_scalar_mul(out=s12[:n, 0, :], in0=s12[:n, 0, :], scalar1=1.0 / 64), reads=["rws12"], writes=["rws12"])
                tt("dve", s12[:n, 2, :], s12[:n, 0, :], s12[:n, 0, :], ALU.mult, ["rws12"], ["rws12"])
                fw.op("dve", lambda e_: e_.scalar_tensor_tensor(out=s12[:n, 1, :], in0=s12[:n, 1, :], scalar=1.0 / 64, in1=s12[:n, 2, :], op0=ALU.mult, op1=ALU.subtract), reads=["rws12"], writes=["rws12"])
                fw.op("dve", lambda e_: e_.tensor_scalar_add(out=s12[:n, 1, :], in0=s12[:n, 1, :], scalar1=64e-5), reads=["rws12"], writes=["rws12"])
                fw.op("act", lambda e_: e_.sqrt(out=s12[:n, 1, :], in_=s12[:n, 1, :]), reads=["rws12"], writes=["rws12"])
                fw.op("dve", lambda e_: e_.reciprocal(out=s12[:n, 1, :], in_=s12[:n, 1, :]), reads=["rws12"], writes=["rws12"])
                yield
                tt("dve", v3(Y[:n, :]), v3(Y[:n, :]), b12(0), ALU.subtract, yk + ["rws12"], yk)
                tt("dve", v3(Y[:n, :]), v3(Y[:n, :]), b12(1), ALU.mult, yk + ["rws12"], yk)
                tt("pool", Y[:n, :], Y[:n, :], lg[:n, :], ALU.mult, yk + ["rwlg"], yk)
                tt("pool", Y[:n, :], Y[:n, :], lb[:n, :], ALU.add, yk + ["rwlb"], yk)
                tt("dve", Y[:n, :], Y[:n, :], bon[:n, :], ALU.add, yk + [k(bon)], yk)
                u_ = U.next()
                tt("pool", u_[:n, :], Y[:n, :], sg[:n, :], ALU.mult, yk + [k(sg)], [k(u_)])
                fw.dma("pool", self.u_d[r0:r0 + n, 0:A_W], u_[:n, :], reads=[k(u_)], writes=["D:u"])
                yield

            def drive(gens, ratios):
                gens = list(gens)
                alive = [g is not None for g in gens]
                while any(alive):
                    for gi, g in enumerate(gens):
                        for _ in range(ratios[gi]):
                            if alive[gi]:
                                try:
                                    next(g)
                                except StopIteration:
                                    alive[gi] = False

            for seg in range(2):
                if seg == 0:
                    C = min(128, SEQ)
                    chunks = [(r, C) for r in range(0, SEQ, C)]
                    fw.op("pool", lambda e_: e_.memset(Z[:], 0.0), writes=["rwZ"])
                    Sout = self.o_rwkv_p[e]
                else:
                    C = TS
                    chunks = [(SEQ, TS)]
                    Sout = self.o_rwkv_s[e]
                    fw.dma("sp", Ssb[:], self.st_rwkv[e].rearrange("h i j -> i h j"), writes=["rwS"])
                    for hg in range(3):
                        B = pb.next()
                        Bv = B[:64, 0:512].rearrange("p (h c) -> p h c", h=4)[:, :, 0:64]
                        for hh in range(4):
                            fw.op("pe", lambda e_, hh=hh, hg=hg, Bv=Bv: e_.transpose(out=Bv[:, hh, :], in_=Ssb[:, hg * 4 + hh, :], identity=self.idf[:64, :64]), reads=["rwS", "idf"], writes=[k(B)])
                        dcopy(Z[:, hg * 4:(hg + 1) * 4, :], Bv, [k(B)], ["rwZ"])
                fw.op("dve", lambda e_: e_.tensor_copy(out=Zb[:], in_=Z[:]), reads=["rwZ"], writes=["rwZb"])
                nlev = int(np.log2(C))
                nch = len(chunks)
                gA = lambda ci: stageA(ci, chunks[ci][0], chunks[ci][1], seg, setsA[ci % 3]) if ci < nch else None
                gB = lambda ci: stageB(chunks[ci][1], nlev, setsA[ci % 3], setsB[ci % 2]) if ci < nch else None
                gS = lambda ci: solve(chunks[ci][0], chunks[ci][1], setsA[ci % 3], setsB[ci % 2]) if 0 <= ci < nch else None
                drive([gA(0)], [1])
                drive([gA(1), gB(0)], [2, 4])
                for ci in range(nch):
                    drive([gA(ci + 2), gB(ci + 1), gS(ci)], [2, 4, 1])
                for hg in range(3):
                    B = pb.next()
                    Bv = B[:64, 0:512].rearrange("p (h c) -> p h c", h=4)[:, :, 0:64]
                    for hh in range(4):
                        fw.op("pe", lambda e_, hh=hh, hg=hg, Bv=Bv: e_.transpose(out=Bv[:, hh, :], in_=Z[:, hg * 4 + hh, :], identity=self.idf[:64, :64]), reads=["rwZ", "idf"], writes=[k(B)])
                    dcopy(Ssb[:, hg * 4:(hg + 1) * 4, :], Bv, [k(B)], ["rwS"])
                fw.dma("pool", Sout.rearrange("h i j -> i h j"), Ssb[:], reads=["rwS"], writes=["D:orw"])

    def odd_mixers(self, o):
        fw, k = self.fw, self.key
        SEQ = self.SEQ
        lg = [float(np.log(1.0 - 2.0 ** (-5.0 - h))) for h in range(C_H)]
        with self.phase() as es:
            sc = self.sb(es, "rtsc", [128, 2, C_H], F32)
            fw.dma("sp", sc[:], self.c_retsc, writes=["rtsc"])
            R = self.sb(es, "rtR", [128, C_H, 2, 256], F32)
            Rb = self.sb(es, "rtRb", [128, C_H, 2, 256], BF16)
            qkvg = self.sbn(es, "rtin", [128, 4, C_W], F32, 2)
            rot = self.sbn(es, "rtrot", [128, 2, 256], F32, 2)
            t1 = self.sb(es, "rtt1", [128, 12, 256], F32)
            t2 = self.sb(es, "rtt2", [128, 12, 256], F32)
            qkb = self.sbn(es, "rtqkb", [128, 12, 256], BF16, 2)
            vb = self.sbn(es, "rtvb", [128, C_W], BF16, 2)
            qT = self.sbn(es, "rtqT", [128, 12, 128], BF16, 2)
            kT = self.sbn(es, "rtkT", [128, 12, 128], BF16, 2)
            Sb = self.sbn(es, "rtSb", [128, C_H, 128], BF16, 2)
            sq = self.sb(es, "rtsq", [128, C_W], F32)
            ss = self.sb(es, "rtss", [128, C_H], F32)
            sgr = self.sbn(es, "rtsg", [128, C_W], F32, 2)
            yt = self.sb(es, "rtyt", [128, C_W], F32)
            ub = self.sbn(es, "rtub", [128, C_W], BF16, 2)
            pt = self.psn(es, "rtpt", [128, 4, 128], BF16, 1)
            pS = self.psn(es, "rtpS", [128, C_H, 128], F32, 1)
            pO = self.psn(es, "rtpO", [128, C_H, 256], F32, 1)
            pR = self.psn(es, "rtpR", [128, 2, 256], F32, 2)
            for seg in range(2):
                if seg == 0:
                    fw.op("pool", lambda e: e.memset(R[:], 0.0), writes=["rtR"])
                    chunks = self.tiles[:-1]
                    Rout = self.o_ret_p[o]
                else:
                    fw.dma("sp", R[:], self.st_ret[o].rearrange("h (c p) e -> p h c e", p=128), writes=["rtR"])
                    chunks = self.tiles[-1:]
                    Rout = self.o_ret_s[o]
                fw.op("dve", lambda e: e.tensor_copy(out=Rb[:], in_=R[:]), reads=["rtR"], writes=["rtRb"])
                def prefix(r0, n, T):
                    X, rt, QK, V, QT, KT, S_, U, sg = T
                    fw.dma("sp", X[:n, 0:2, :], self.h_d[r0:r0 + n, 0:2 * C_W].rearrange("p (a c) -> p a c", a=2), reads=["D:h"], writes=[(k(X), 0)])
                    fw.dma("act", X[:n, 2:4, :], self.h_d[r0:r0 + n, 2 * C_W:4 * C_W].rearrange("p (a c) -> p a c", a=2), reads=["D:h"], writes=[(k(X), 1)])
                    fw.dma("sp", rt[:n], self.c_rot[r0:r0 + n], writes=[k(rt)])
                    x12 = X[:n, 0:2, :].rearrange("p a (h c) -> p (a h) c", h=C_H)
                    x12p = X[:n, 0:2, :].rearrange("p a (h c two) -> p (a h) c two", h=C_H, two=2)
                    t2p = t2[:n].rearrange("p a (c two) -> p a c two", two=2)
                    rtp = rt[:n].rearrange("p a (c two) -> p a c two", two=2)
                    fw.op("pool", lambda e, x12=x12, rt=rt: e.tensor_tensor(out=t1[:n], in0=x12, in1=rt[:n, 0, :].unsqueeze(1).to_broadcast([n, 12, 256]), op=ALU.mult),
                          reads=[(k(X), 0), k(rt)], writes=["rtt1"])
                    fw.op("dve", lambda e, x12p=x12p, t2p=t2p, rtp=rtp: e.tensor_tensor(out=t2p[:, :, :, 0], in0=x12p[:, :, :, 1], in1=rtp[:, 1, :, 0].unsqueeze(1).to_broadcast([n, 12, 128]), op=ALU.mult),
                          reads=[(k(X), 0), k(rt)], writes=[("rtt2", 0)])
                    fw.op("dve", lambda e, x12p=x12p, t2p=t2p, rtp=rtp: e.tensor_tensor(out=t2p[:, :, :, 1], in0=x12p[:, :, :, 0], in1=rtp[:, 1, :, 1].unsqueeze(1).to_broadcast([n, 12, 128]), op=ALU.mult),
                          reads=[(k(X), 0), k(rt)], writes=[("rtt2", 1)])
                    yield
                    fw.op("dve", lambda e: e.tensor_tensor(out=t1[:n], in0=t1[:n], in1=t2[:n], op=ALU.add), reads=["rtt1", ("rtt2", 0), ("rtt2", 1)], writes=["rtt1"])
                    fw.op("dve", lambda e, QK=QK: e.tensor_tensor(out=QK[:n], in0=t1[:n], in1=sc[:n].rearrange("p a h -> p (a h)").unsqueeze(2).to_broadcast([n, 12, 256]), op=ALU.mult),
                          reads=["rtt1", "rtsc"], writes=[k(QK)])
                    fw.op("act", lambda e, V=V, X=X: e.copy(out=V[:n, :], in_=X[:n, 2, :]), reads=[(k(X), 1)], writes=[k(V)])
                    qflat = QK[:, 0:6, :].rearrange("p h c -> p (h c)")
                    kflat = QK[:, 6:12, :].rearrange("p h c -> p (h c)")
                    yield
                    self.transposes(QT, qflat, 12, n, pt, k(QT), k(QK))
                    yield
                    self.transposes(KT, kflat, 12, n, pt, k(KT), k(QK))
                    yield
                    S = pS.next()
                    for h in range(C_H):
                        for c in range(2):
                            fw.op("pe", lambda e, h=h, c=c, S=S, KT=KT, QT=QT: e.matmul(S[:n, h, :n], lhsT=KT[:, h * 2 + c, :n], rhs=QT[:, h * 2 + c, :n], start=(c == 0), stop=(c == 1)),
                                  reads=[k(KT), k(QT)], writes=[k(S)])
                    fw.op("dve", lambda e, S=S, S_=S_: e.tensor_tensor(out=S_[:n, :, :n], in0=S[:n, :, :n], in1=self.maskf[:n, 0, :n].unsqueeze(1).to_broadcast([n, C_H, n]), op=ALU.mult),
                          reads=[k(S), "maskf"], writes=[k(S_)])
                    fw.op("act", lambda e, X=X, sg=sg: e.activation(out=sg[:n, :], in_=X[:n, 3, :], func=AF.Silu), reads=[(k(X), 1)], writes=[k(sg)])
                    yield

                def solve(r0, n, T):
                    X, rt, QK, V, QT, KT, S_, U, sg = T
                    O = pO.next()
                    for h in range(C_H):
                        fw.op("pe", lambda e, h=h, O=O, S_=S_, V=V: e.matmul(O[:n, h, :], lhsT=S_[:n, h, :n], rhs=V[:n, h * 256:(h + 1) * 256], start=True, stop=False),
                              reads=[k(S_), k(V)], writes=[k(O)])
                        for c in range(2):
                            fw.op("pe", lambda e, h=h, c=c, O=O, QT=QT: e.matmul(O[:n, h, :], lhsT=QT[:, h * 2 + c, :n], rhs=Rb[:, h, c, :], start=False, stop=(c == 1)),
                                  reads=[k(QT), "rtRb"], writes=[k(O)])
                    yield
                    for h in range(C_H):
                        dR = pR.next()
                        for c in range(2):
                            fw.op("pe", lambda e, h=h, c=c, dR=dR, QK=QK, V=V: e.matmul(dR[:, c, :], lhsT=QK[:n, 6 + h, c * 128:(c + 1) * 128], rhs=V[:n, h * 256:(h + 1) * 256], start=True, stop=True),
                                  reads=[k(QK), k(V)], writes=[k(dR)])
                        fw.op("dve", lambda e, h=h, dR=dR: e.tensor_tensor(out=R[:, h], in0=R[:, h], in1=dR[:], op=ALU.add), reads=[k(dR), "rtR"], writes=["rtR"])
                        gch = float(np.exp(lg[h] * n))
                        fw.op("act", lambda e, h=h, gch=gch: e.mul(out=Rb[:, h], in_=R[:, h], mul=gch), reads=["rtR"], writes=["rtRb"])
                        fw.op("act", lambda e, h=h, gch=gch: e.mul(out=R[:, h], in_=R[:, h], mul=gch), reads=["rtR"], writes=["rtR"])
                        if h % 2:
                            yield
                    yield
                    Of = O[:n].rearrange("p h c -> p (h c)")
                    fw.op("act", lambda e, Of=Of: e.activation(out=sq[:n, :], in_=Of, func=AF.Square), reads=[k(O)], writes=["rtsq"])
                    fw.op("dve", lambda e: e.tensor_reduce(out=ss[:n, :], in_=sq[:n, :].rearrange("p (h c) -> p h c", h=C_H), axis=AX.X, op=ALU.add), reads=["rtsq"], writes=["rtss"])
                    fw.op("dve", lambda e: e.tensor_scalar(out=ss[:n, :], in0=ss[:n, :], scalar1=1.0 / 256.0, scalar2=1e-6, op0=ALU.mult, op1=ALU.add), reads=["rtss"], writes=["rtss"])
                    fw.op("act", lambda e: e.sqrt(out=ss[:n, :], in_=ss[:n, :]), reads=["rtss"], writes=["rtss"])
                    fw.op("dve", lambda e: e.reciprocal(out=ss[:n, :], in_=ss[:n, :]), reads=["rtss"], writes=["rtss"])
                    fw.op("dve", lambda e, O=O: e.tensor_tensor(out=yt[:n, :].rearrange("p (h c) -> p h c", h=C_H), in0=O[:n], in1=ss[:n, :].unsqueeze(2).to_broadcast([n, C_H, 256]), op=ALU.mult),
                          reads=[k(O), "rtss"], writes=["rtyt"])
                    fw.op("pool", lambda e, U=U, sg=sg: e.tensor_tensor(out=U[:n, :], in0=yt[:n, :], in1=sg[:n, :], op=ALU.mult), reads=["rtyt", k(sg)], writes=[k(U)])
                    fw.dma("pool", self.u_d[r0:r0 + n, 0:C_W], U[:n, :], reads=[k(U)], writes=["D:u"])

                    yield

                def drive2(g1, g2):
                    al = [g1 is not None, g2 is not None]
                    gs = [g1, g2]
                    while any(al):
                        for gi in range(2):
                            if al[gi]:
                                try:
                                    next(gs[gi])
                                except StopIteration:
                                    al[gi] = False
                Ts = [(qkvg.next(), rot.next(), qkb.next(), vb.next(), qT.next(), kT.next(), Sb.next(), ub.next(), sgr.next()) for _ in chunks]
                drive2(prefix(chunks[0][0], chunks[0][1], Ts[0]), None)
                for ci in range(len(chunks)):
                    gp = prefix(chunks[ci + 1][0], chunks[ci + 1][1], Ts[ci + 1]) if ci + 1 < len(chunks) else None
                    drive2(gp, solve(chunks[ci][0], chunks[ci][1], Ts[ci]))
                fw.dma("pool", Rout.rearrange("h (c p) e -> p h c e", p=128), R[:], reads=["rtR"], writes=["D:oret"])


PAST_LEN = 16384


def _consts(SEQ):
    NT = SEQ + TS
    c = {}
    c["c_ident"] = np.eye(128, dtype=np.float32)
    pos = np.concatenate([np.arange(SEQ, dtype=np.float32), np.arange(TS, dtype=np.float32) + np.float32(PAST_LEN)])
    angle = (1.0 / (np.float32(10000.0) ** np.linspace(0.0, 1.0, 128, dtype=np.float32))).astype(np.float32)
    ph = (pos[:, None] * np.repeat(angle, 2)[None]).astype(np.float32)
    sgn = np.tile(np.array([-1.0, 1.0], np.float32), 128)
    c["c_rot"] = np.stack([np.cos(ph), np.sin(ph) * sgn[None]], axis=1).astype(np.float32)
    lg = np.log(1.0 - 2.0 ** (-5.0 - np.arange(C_H, dtype=np.float32))).astype(np.float32)
    idx = np.arange(128, dtype=np.float32)
    xi = np.exp(lg[None, :] * (idx[:, None] + 1.0))
    kz = np.exp(-lg[None, :] * (idx[:, None] + 1.0)) * (256.0 ** -0.5)
    c["c_retsc"] = np.stack([xi, kz], axis=1).astype(np.float32)
    j = np.arange(128)[:, None]
    i = np.arange(128)[None, :]
    c["c_masks"] = np.stack([(j <= i), (j >= i), (j < i), (j > i)], axis=1).astype(np.float32)
    return c


def make_in_maps(inp, SEQ, DEPTH, n_cores):
    LE, LO = (DEPTH + 1) // 2, DEPTH // 2
    cst = _consts(SEQ)
    f = lambda a: np.ascontiguousarray(np.asarray(a, dtype=np.float32))
    maps = []
    for c in range(n_cores):
        b = c % inp["x_prompt"].shape[0]
        m = {}
        m["x_in"] = f(np.concatenate([inp["x_prompt"][b, :SEQ], inp["x_sample"][c]], axis=0))
        m["memp"] = f(inp["mem_prompt"][b])
        m["st_rwkv"] = f(inp["state_rwkv"][:LE, c])
        m["st_shift"] = f(inp["state_rwkv_shift"][:LE, c])
        for g in range(3):
            a = inp[f"cache_dwa_g{g}"][:LE, c]
            m[f"c_g{g}"] = f(a.reshape(a.shape[0], a.shape[1], 512))
        m["st_ret"] = f(inp["state_ret"][:max(LO, 1), c])
        m["c_mem"] = f(inp["cache_mem_kv"][:DEPTH, c].reshape(DEPTH, 256, 1024))
        m["w_in_even"] = f(inp["w_in_even"][:LE])
        m["w_out_even"] = f(inp["w_out_even"][:LE])
        m["w_in_odd"] = f(inp["w_in_odd"][:max(LO, 1)])
        m["w_out_odd"] = f(inp["w_out_odd"][:max(LO, 1)])
        m["w_mem_kv"] = f(inp["w_mem_kv"][:DEPTH])
        m["ln_g"] = f(inp["ln_g"][:DEPTH])
        m["ln_b"] = f(inp["ln_b"][:DEPTH])
        for nm in ("mu", "w0", "w_up", "a0", "a_up", "k_k", "k_a", "lnx_g", "lnx_b"):
            m["rwkv_" + nm] = f(inp["rwkv_" + nm][:LE])
        m["rwkv_r_k"] = f(np.asarray(inp["rwkv_r_k"])[:LE].reshape(LE, A_W))
        m.update(cst)
        maps.append(m)
    return maps


_NC_CACHE = {}


def run(inp, SEQ, DEPTH, n_cores, debug=False, kinds=None):
    key = (SEQ, DEPTH, debug, str(kinds))
    if key not in _NC_CACHE:
        _NC_CACHE[key] = Builder(SEQ, DEPTH, debug=debug, kinds=kinds).build()
    nc = _NC_CACHE[key]
    maps = make_in_maps(inp, SEQ, DEPTH, n_cores)
    res = run_bass_kernel_spmd(nc, maps, core_ids=list(range(n_cores)))
    return res.results


def kernel(**inp):
    inp = {k: np.asarray(v) for k, v in inp.items()}
    SEQ, DEPTH = inp["x_prompt"].shape[1], 4
    BP = inp["x_prompt"].shape[0]
    NS = inp["x_sample"].shape[0]
    r = run(inp, SEQ, DEPTH, NS)
    LE, LO = 2, 2
    P = range(BP)
    S = range(NS)
    st = lambda name, cores: np.stack([r[c][name] for c in cores], axis=1)
    y_p = np.stack([r[b]["y"][:SEQ] for b in P], 0)
    y_s = np.stack([r[c]["y"][SEQ:] for c in S], 0)
    outs = [y_p, y_s, st("o_rwkv_p", P), st("o_rwkv_s", S), st("o_shift_p", P), st("o_shift_s", S)]
    for g in range(3):
        a = st(f"o_g{g}_p", P)
        outs.append(a.reshape(LE, BP, a.shape[2], 2, 4, 64))
        a = st(f"o_g{g}_s", S)
        outs.append(a.reshape(LE, NS, TS, 2, 4, 64))
    outs.append(st("o_ret_p", P))
    outs.append(st("o_ret_s", S))
    outs.append(st("o_mem_p", P).reshape(DEPTH, BP, 256, 2, 4, 128))
    return tuple(np.ascontiguousarray(o.astype(np.float32)) for o in outs)
```
